# Optimizing a Trainium2 kernel written in Bass

```python
import math, functools
import jax, jax.numpy as jnp
from jax import lax
import numpy as np

D_MODEL = 4096
BATCH = 2
SEQ = 4096
DEPTH = 1
DEC_BATCH = 128
DEC_SEQ = 4
PAST_LEN = 8192
PAGE_SIZE = 128

N_META = 16
WINDOW = 128
ATTN_BLOCK = WINDOW
A_HEADS = D_MODEL // 128
A_KV_HEADS = A_HEADS // 4
A_GROUP = A_HEADS // A_KV_HEADS
A_HEAD_DIM = 64
ROT_DIM = A_HEAD_DIM // 4
ROPE_THETA = 500000.0
B_HEADS = D_MODEL // 256
B_HEAD_DIM = 128
CONV_W = 4
DELTA_CHUNK = 64
D_FF = 4 * D_MODEL
EPS = 1e-6

QA_DIM = A_HEADS * A_HEAD_DIM
KVA_DIM = A_KV_HEADS * A_HEAD_DIM
QB_DIM = B_HEADS * B_HEAD_DIM
CONV_DIM = 3 * QB_DIM
IN_SIZES = (QA_DIM, KVA_DIM, KVA_DIM, CONV_DIM, QB_DIM, B_HEADS, B_HEADS, D_MODEL, D_MODEL)
IN_DIM = QA_DIM + 2 * KVA_DIM + CONV_DIM + QB_DIM + 2 * B_HEADS + 2 * D_MODEL

kernel_name = 'hybrid_swa_sink_gated_deltanet_decoder_step'


def rmsnorm(x, w):
    xf = x.astype(jnp.float32)
    y = xf * lax.rsqrt(jnp.mean(xf * xf, axis=-1, keepdims=True) + EPS)
    return (y * w.astype(jnp.float32)).astype(x.dtype)


def l2norm(x):
    xf = x.astype(jnp.float32)
    return xf * lax.rsqrt(jnp.sum(xf * xf, axis=-1, keepdims=True) + EPS)


def partial_rope(x, pos):
    half = ROT_DIM // 2
    inv_freq = ROPE_THETA ** (-jnp.arange(half, dtype=jnp.float32) * (2.0 / ROT_DIM))
    ang = pos.astype(jnp.float32)[:, None] * inv_freq[None, :]
    cos = jnp.cos(ang)[:, None, :]
    sin = jnp.sin(ang)[:, None, :]
    xr = x[..., :ROT_DIM].astype(jnp.float32)
    x1, x2 = xr[..., :half], xr[..., half:]
    rot = jnp.concatenate([x1 * cos - x2 * sin, x2 * cos + x1 * sin], axis=-1)
    return jnp.concatenate([rot.astype(x.dtype), x[..., ROT_DIM:]], axis=-1)


def sink_probs(s, mask, sinks):
    sk = sinks.astype(jnp.float32).reshape(A_KV_HEADS, A_GROUP, 1, 1)
    s = jnp.where(mask, s, -jnp.inf)
    m = jnp.maximum(jnp.max(s, axis=-1, keepdims=True), sk)
    p = jnp.exp(s - m)
    return p / (jnp.sum(p, axis=-1, keepdims=True) + jnp.exp(sk - m))


def attend_prompt(q, k, v, sinks):
    Bn, L = q.shape[:2]
    nb = -(-L // ATTN_BLOCK)
    pad = nb * ATTN_BLOCK - L
    qb = jnp.pad(q, ((0, 0), (0, pad), (0, 0), (0, 0))).reshape(Bn, nb, ATTN_BLOCK, A_KV_HEADS, A_GROUP, A_HEAD_DIM)

    def band(x):
        xp = jnp.pad(x, ((0, 0), (ATTN_BLOCK, pad), (0, 0), (0, 0))).reshape(Bn, nb + 1, ATTN_BLOCK, A_KV_HEADS, A_HEAD_DIM)
        return jnp.concatenate([xp[:, :-1], xp[:, 1:]], axis=2)

    kb, vb = band(k), band(v)
    qpos = jnp.arange(nb)[:, None] * ATTN_BLOCK + jnp.arange(ATTN_BLOCK)[None, :]
    kpos = (jnp.arange(nb)[:, None] - 1) * ATTN_BLOCK + jnp.arange(2 * ATTN_BLOCK)[None, :]
    diff = qpos[:, :, None] - kpos[:, None, :]
    mask = (diff >= 0) & (diff <= WINDOW) & (kpos[:, None, :] >= 0)
    s = jnp.einsum('bnqhgd,bnkhd->bnhgqk', qb, kb, preferred_element_type=jnp.float32) * (A_HEAD_DIM ** -0.5)
    p = sink_probs(s, mask[None, :, None, None], sinks)
    o = jnp.einsum('bnhgqk,bnkhd->bnqhgd', p.astype(v.dtype), vb)
    o = o.reshape(Bn, nb * ATTN_BLOCK, QA_DIM)[:, :L]
    return o, k[:, -WINDOW:], v[:, -WINDOW:]


def attend_sample(q, k, v, sinks, cache_k, cache_v):
    Bd, T = q.shape[:2]
    n_buf = cache_k.shape[1]
    k_all = jnp.concatenate([cache_k.astype(k.dtype), k], axis=1)
    v_all = jnp.concatenate([cache_v.astype(v.dtype), v], axis=1)
    qpos = PAST_LEN + jnp.arange(T)
    kpos = PAST_LEN - n_buf + jnp.arange(n_buf + T)
    diff = qpos[:, None] - kpos[None, :]
    mask = (diff >= 0) & (diff <= WINDOW)
    qg = q.reshape(Bd, T, A_KV_HEADS, A_GROUP, A_HEAD_DIM)
    s = jnp.einsum('bqhgd,bkhd->bhgqk', qg, k_all, preferred_element_type=jnp.float32) * (A_HEAD_DIM ** -0.5)
    p = sink_probs(s, mask, sinks)
    o = jnp.einsum('bhgqk,bkhd->bqhgd', p.astype(v.dtype), v_all).reshape(Bd, T, QA_DIM)
    return o, k_all[:, -n_buf:], v_all[:, -n_buf:]


def short_conv(x, hist, w):
    T = x.shape[1]
    xx = jnp.concatenate([hist.astype(x.dtype), x], axis=1)
    y = xx[:, 0:T].astype(jnp.float32) * w[0].astype(jnp.float32)
    for j in range(1, CONV_W):
        y = y + xx[:, j:j + T].astype(jnp.float32) * w[j].astype(jnp.float32)
    return jax.nn.silu(y), xx[:, -(CONV_W - 1):]


def gated_delta_chunked(q, k, v, g, beta, s0, chunk):
    Bn, T, H, Dv = v.shape
    n = T // chunk

    def blk(x):
        return jnp.moveaxis(x.reshape((Bn, n, chunk) + x.shape[2:]), 2, 3)

    qc, kc, vc, gc, bc = blk(q), blk(k), blk(v), blk(g), blk(beta)
    G = jnp.cumsum(gc, axis=-1)
    tri = jnp.tril(jnp.ones((chunk, chunk), bool))
    tri_strict = jnp.tril(jnp.ones((chunk, chunk), bool), -1)
    decay = jnp.exp(jnp.where(tri, G[..., :, None] - G[..., None, :], -jnp.inf))
    kbeta = kc * bc[..., None]
    M = jnp.where(tri_strict, jnp.einsum('bnhid,bnhjd->bnhij', kbeta, kc) * decay, 0.0)
    eye = jnp.eye(chunk, dtype=jnp.float32)
    rhs = jnp.concatenate([vc * bc[..., None], kbeta * jnp.exp(G)[..., None]], axis=-1)
    sol = lax.linalg.triangular_solve(eye + M, rhs, left_side=True, lower=True, unit_diagonal=True)
    U, W = sol[..., :Dv], sol[..., Dv:]
    qk = jnp.einsum('bnhid,bnhjd->bnhij', qc, kc) * decay
    q_dec = qc * jnp.exp(G)[..., None]
    k_dec = kc * jnp.exp(G[..., -1:] - G)[..., None]
    g_last = jnp.exp(G[..., -1])

    def step(S, xs):
        U_i, W_i, qk_i, qd_i, kd_i, gl_i = xs
        v_new = U_i - jnp.einsum('bhcd,bhde->bhce', W_i, S)
        o = jnp.einsum('bhcd,bhde->bhce', qd_i, S) + jnp.einsum('bhij,bhje->bhie', qk_i, v_new)
        S = S * gl_i[..., None, None] + jnp.einsum('bhcd,bhce->bhde', kd_i, v_new)
        return S, o

    xs = tuple(jnp.moveaxis(a, 1, 0) for a in (U, W, qk, q_dec, k_dec, g_last))
    S, o = lax.scan(step, s0, xs)
    o = jnp.swapaxes(jnp.moveaxis(o, 0, 1), 2, 3).reshape(Bn, T, H, Dv)
    return o, S


def _project(hn, w_in):
    z = jnp.einsum('btd,dc->btc', hn, w_in)
    cuts, acc = [], 0
    for size in IN_SIZES[:-1]:
        acc += size
        cuts.append(acc)
    return jnp.split(z, cuts, axis=-1)


def _layer(h, pos, attend, conv_hist, s0, front_pad, chunk,
           norm_mix_pre, norm_mix_post, norm_mlp_pre, norm_mlp_post, w_in, sinks, conv_w,
           a_log, dt_bias, delta_norm, w_branch_a, w_branch_b, w_out, w_up, w_down):
    Bn, T, _ = h.shape
    hn = rmsnorm(h, norm_mix_pre)
    qa, ka, va, qkv_b, zb, b_raw, a_raw, ga, gb = _project(hn, w_in)
    qa = partial_rope(qa.reshape(Bn, T, A_HEADS, A_HEAD_DIM), pos)
    ka = partial_rope(ka.reshape(Bn, T, A_KV_HEADS, A_HEAD_DIM), pos)
    va = va.reshape(Bn, T, A_KV_HEADS, A_HEAD_DIM)
    oa, new_k, new_v = attend(qa, ka, va, sinks)
    xc, new_conv = short_conv(qkv_b, conv_hist, conv_w)
    qb, kb, vb = jnp.split(xc, [QB_DIM, 2 * QB_DIM], axis=-1)
    qb = l2norm(qb.reshape(Bn, T, B_HEADS, B_HEAD_DIM)) * (B_HEAD_DIM ** -0.5)
    kb = l2norm(kb.reshape(Bn, T, B_HEADS, B_HEAD_DIM))
    vb = vb.reshape(Bn, T, B_HEADS, B_HEAD_DIM)
    g = -jnp.exp(a_log.astype(jnp.float32)) * jax.nn.softplus(a_raw.astype(jnp.float32) + dt_bias.astype(jnp.float32))
    beta = jax.nn.sigmoid(b_raw.astype(jnp.float32))
    p4 = ((0, 0), (front_pad, 0), (0, 0), (0, 0))
    p3 = ((0, 0), (front_pad, 0), (0, 0))
    ob, s_new = gated_delta_chunked(jnp.pad(qb, p4), jnp.pad(kb, p4), jnp.pad(vb, p4),
                                    jnp.pad(g, p3), jnp.pad(beta, p3), s0.astype(jnp.float32), chunk)
    ob = ob[:, front_pad:]
    ob = rmsnorm(ob, delta_norm) * jax.nn.silu(zb.reshape(Bn, T, B_HEADS, B_HEAD_DIM).astype(jnp.float32))
    ob = ob.reshape(Bn, T, QB_DIM).astype(h.dtype)
    merged = jax.nn.sigmoid(ga) * (oa @ w_branch_a) + jax.nn.sigmoid(gb) * (ob @ w_branch_b)
    h = h + rmsnorm(merged @ w_out, norm_mix_post)
    u = jnp.square(jax.nn.relu(rmsnorm(h, norm_mlp_pre) @ w_up))
    h = h + rmsnorm(u @ w_down, norm_mlp_post)
    return h, new_k, new_v, new_conv, s_new.astype(s0.dtype)


def setup_inputs(seed: int = 0) -> dict:
    key = jax.random.key(seed)
    ks = jax.random.split(key, 24)
    f32 = jnp.float32

    def nrm(k, shape, scale):
        return jax.random.normal(k, shape, f32) * scale

    n_buf = min(WINDOW, PAST_LEN)
    dt = jnp.exp(jax.random.uniform(ks[20], (DEPTH, B_HEADS), f32, math.log(1e-3), math.log(1e-1)))
    return {
        'x_prompt': nrm(ks[0], (BATCH, SEQ, D_MODEL), 1.0),
        'x_sample': nrm(ks[1], (DEC_BATCH, DEC_SEQ, D_MODEL), 1.0),
        'cache_win_k': nrm(ks[2], (DEPTH, DEC_BATCH, n_buf, A_KV_HEADS, A_HEAD_DIM), 1.0),
        'cache_win_v': nrm(ks[3], (DEPTH, DEC_BATCH, n_buf, A_KV_HEADS, A_HEAD_DIM), 1.0),
        'state_conv': nrm(ks[4], (DEPTH, DEC_BATCH, CONV_W - 1, CONV_DIM), 1.0),
        'state_delta': nrm(ks[5], (DEPTH, DEC_BATCH, B_HEADS, B_HEAD_DIM, B_HEAD_DIM), 0.1),
        'meta_tokens': nrm(ks[6], (N_META, D_MODEL), 1.0),
        'norm_mix_pre': 1.0 + nrm(ks[7], (DEPTH, D_MODEL), 0.02),
        'norm_mix_post': 1.0 + nrm(ks[8], (DEPTH, D_MODEL), 0.02),
        'norm_mlp_pre': 1.0 + nrm(ks[9], (DEPTH, D_MODEL), 0.02),
        'norm_mlp_post': 1.0 + nrm(ks[10], (DEPTH, D_MODEL), 0.02),
        'w_in': nrm(ks[11], (DEPTH, D_MODEL, IN_DIM), D_MODEL ** -0.5),
        'sinks': nrm(ks[12], (DEPTH, A_HEADS), 0.5),
        'conv_w': nrm(ks[13], (DEPTH, CONV_W, CONV_DIM), CONV_W ** -0.5),
        'a_log': jnp.log(jax.random.uniform(ks[14], (DEPTH, B_HEADS), f32, 1.0, 16.0)),
        'dt_bias': dt + jnp.log(-jnp.expm1(-dt)),
        'delta_norm': 1.0 + nrm(ks[15], (DEPTH, B_HEAD_DIM), 0.02),
        'w_branch_a': nrm(ks[16], (DEPTH, QA_DIM, D_MODEL), QA_DIM ** -0.5),
        'w_branch_b': nrm(ks[17], (DEPTH, QB_DIM, D_MODEL), QB_DIM ** -0.5),
        'w_out': nrm(ks[18], (DEPTH, D_MODEL, D_MODEL), D_MODEL ** -0.5),
        'w_up': nrm(ks[19], (DEPTH, D_MODEL, D_FF), D_MODEL ** -0.5),
        'w_down': nrm(ks[21], (DEPTH, D_FF, D_MODEL), D_FF ** -0.5),
    }


def reference(x_prompt, x_sample, cache_win_k, cache_win_v, state_conv, state_delta, meta_tokens,
              norm_mix_pre, norm_mix_post, norm_mlp_pre, norm_mlp_post, w_in, sinks, conv_w,
              a_log, dt_bias, delta_norm, w_branch_a, w_branch_b, w_out, w_up, w_down):
    layer_w = (norm_mix_pre, norm_mix_post, norm_mlp_pre, norm_mlp_post, w_in, sinks, conv_w,
               a_log, dt_bias, delta_norm, w_branch_a, w_branch_b, w_out, w_up, w_down)
    Bn = x_prompt.shape[0]
    hp = jnp.concatenate([jnp.broadcast_to(meta_tokens.astype(x_prompt.dtype)[None], (Bn, N_META, D_MODEL)), x_prompt], axis=1)
    L = hp.shape[1]
    pos_p = jnp.arange(L)
    hs = x_sample
    pos_s = PAST_LEN + jnp.arange(hs.shape[1])
    front_pad = (-N_META) % DELTA_CHUNK
    pk, pv, pc, pd, sk, sv, sc, sd = [], [], [], [], [], [], [], []
    for l in range(DEPTH):
        wl = [w[l] for w in layer_w]
        conv0 = jnp.zeros((Bn, CONV_W - 1, CONV_DIM), hp.dtype)
        s0 = jnp.zeros((Bn, B_HEADS, B_HEAD_DIM, B_HEAD_DIM), hp.dtype)
        hp, k_p, v_p, c_p, d_p = _layer(hp, pos_p, attend_prompt, conv0, s0, front_pad, DELTA_CHUNK, *wl)
        att_s = functools.partial(attend_sample, cache_k=cache_win_k[l], cache_v=cache_win_v[l])
        hs, k_s, v_s, c_s, d_s = _layer(hs, pos_s, att_s, state_conv[l], state_delta[l], 0, hs.shape[1], *wl)
        pk.append(k_p); pv.append(v_p); pc.append(c_p); pd.append(d_p)
        sk.append(k_s); sv.append(v_s); sc.append(c_s); sd.append(d_s)
    return (hp[:, N_META:], hs,
            jnp.stack(pk), jnp.stack(pv), jnp.stack(pc), jnp.stack(pd),
            jnp.stack(sk), jnp.stack(sv), jnp.stack(sc), jnp.stack(sd))
```

```python
import math
import os
KSUB = os.environ.get('KSUB', 'i,ii,iii,iv').split(',')
from contextlib import ExitStack
import numpy as np
import concourse.bass as bass
import concourse.mybir as mybir
from concourse.bass_utils import run_bass_kernel_spmd

F32 = mybir.dt.float32
BF16 = mybir.dt.bfloat16
I32 = mybir.dt.int32
U32 = mybir.dt.uint32
AF = mybir.ActivationFunctionType
ALU = mybir.AluOpType

D = 4096
IN_DIM = 19488
NQ = 1152
NTOK = 1280
NSEQ = 4224
EPS = 1e-6
DEBUG = set()

ENGS = ("pe", "act", "dve", "pool", "sp")
SEM_ROLL = 20000
N_DMA_SEMS = 40


class Op:
    __slots__ = ("eng", "fn", "deps", "kind", "sig", "sem", "val", "idx", "pre")

    def __init__(self, eng, fn, kind):
        self.eng = eng
        self.fn = fn
        self.kind = kind
        self.deps = set()
        self.sig = False
        self.sem = None
        self.val = 0
        self.pre = None


class Prog:
    def __init__(self, nc):
        self.nc = nc
        self.ops = []
        self.last_w = {}
        self.readers = {}
        self.n_dma = 0
        self.dma_last = {}
        self.eng_last = {}
        self.cc_last = None

    def op(self, eng, fn, reads=(), writes=(), kind="c"):
        o = Op(eng, fn, kind)
        i = len(self.ops)
        o.idx = i
        for r in reads:
            w = self.last_w.get(r)
            if w is not None:
                o.deps.add(w)
            if isinstance(r, tuple) and r and r[0] == "ps":
                for rr in self.readers.get(r, ()):
                    if self.ops[rr].eng != eng:
                        o.deps.add(rr)
        for w_ in writes:
            w = self.last_w.get(w_)
            if w is not None:
                o.deps.add(w)
            latest = {}
            for r in self.readers.get(w_, ()):
                ro = self.ops[r]
                if ro.kind == "c" and ro.eng in ("pe", "act", "dve"):
                    if latest.get(ro.eng, -1) < r:
                        latest[ro.eng] = r
                else:
                    o.deps.add(r)
            o.deps.update(latest.values())
        for r in reads:
            self.readers.setdefault(r, []).append(i)
        for w_ in writes:
            self.last_w[w_] = i
            self.readers[w_] = []
        o.deps.discard(i)
        if eng == "pe" and kind == "c":
            o.deps = {d for d in o.deps if not (self.ops[d].eng == "pe" and self.ops[d].kind == "c")}
        if kind == "d":
            self.dma_last[self.n_dma % N_DMA_SEMS] = i
            self.n_dma += 1
        elif kind == "cc":
            self.cc_last = i
        if kind != "b":
            self.eng_last[eng] = i
        self.ops.append(o)
        return o

    def dma(self, eng, out, in_, reads=(), writes=()):
        return self.op(eng, lambda e, out=out, in_=in_: e.dma_start(out=out, in_=in_),
                       reads, writes, kind="d")

    def barrier(self):
        deps = set(self.dma_last.values()) | set(self.eng_last.values())
        if self.cc_last is not None:
            deps.add(self.cc_last)
        for e in ENGS:
            o = self.op(e, None, kind="b")
            o.deps = set(deps)
        self.last_w = {}
        self.readers = {}

    def emit(self):
        nc = self.nc
        ops = self.ops
        for o in ops:
            for d in o.deps:
                ops[d].sig = True
        for o in ops:
            if o.kind in ("d", "cc"):
                o.sig = True
        eng_sems = {e: [] for e in ENGS}
        eng_cnt = {e: 0 for e in ENGS}
        dma_sems = [nc.alloc_semaphore("dsem%d" % i) for i in range(N_DMA_SEMS)]
        cc_sem = nc.alloc_semaphore("ccsem")
        cc_cnt = 0
        dma_cnt = [0] * N_DMA_SEMS
        dma_prev = [None] * N_DMA_SEMS
        k = 0
        for o in ops:
            if o.kind == "d":
                s = k % N_DMA_SEMS
                k += 1
                o.pre = dma_prev[s]
                dma_cnt[s] += 16
                o.sem = dma_sems[s]
                o.val = dma_cnt[s]
                dma_prev[s] = o
            elif o.kind == "cc":
                cc_cnt += 1
                o.sem = cc_sem
                o.val = cc_cnt
            elif o.sig:
                c = eng_cnt[o.eng]
                si = c // SEM_ROLL
                if si >= len(eng_sems[o.eng]):
                    eng_sems[o.eng].append(nc.alloc_semaphore("es_%s_%d" % (o.eng, si)))
                o.sem = eng_sems[o.eng][si]
                o.val = c - si * SEM_ROLL + 1
                eng_cnt[o.eng] = c + 1
        per_eng = {e: [] for e in ENGS}
        for o in ops:
            per_eng[o.eng].append(o)
        final_dma = [o for o in dma_prev if o is not None]
        self.stats = {e: len(per_eng[e]) for e in ENGS}

        def run(eng_name, eng):
            waited = {}
            for o in per_eng[eng_name]:
                ws = {}
                for d in o.deps:
                    od = ops[d]
                    key = id(od.sem)
                    if ws.get(key, (None, 0))[1] < od.val:
                        ws[key] = (od.sem, od.val)
                if o.kind == "d" and o.pre is not None:
                    od = o.pre
                    key = id(od.sem)
                    if ws.get(key, (None, 0))[1] < od.val:
                        ws[key] = (od.sem, od.val)
                for key, (sem, val) in ws.items():
                    if waited.get(key, 0) >= val:
                        continue
                    eng.wait_ge(sem, val)
                    waited[key] = val
                if o.fn is None:
                    continue
                ins = o.fn(eng)
                if o.sig:
                    ins.then_inc(o.sem, 16 if o.kind == "d" else 1)
            if eng_name == "sp":
                for od in final_dma:
                    key = id(od.sem)
                    if waited.get(key, 0) >= od.val:
                        continue
                    eng.wait_ge(od.sem, od.val)

        all_sems = list(dma_sems) + [cc_sem]
        self._eng_sems = eng_sems
        with nc.Block() as block:
            @block.tensor
            def _(e):
                run("pe", e)

            @block.scalar
            def _(e):
                run("act", e)

            @block.vector
            def _(e):
                run("dve", e)

            @block.gpsimd
            def _(e):
                run("pool", e)

            @block.sync
            def _(e):
                run("sp", e)
        for e_ in ENGS:
            all_sems += eng_sems[e_]
        nc.clear_and_free_semaphores(all_sems)
        nc.all_engine_barrier()


def bc(ap, axis, n):
    lst = [list(x) for x in ap.ap]
    lst.insert(axis, [0, n])
    return bass.AP(ap.tensor, ap.offset, lst)


class Rot:
    def __init__(self, n):
        self.n = n
        self.i = -1

    def nxt(self):
        self.i = (self.i + 1) % self.n
        return self.i


class K:
    def __init__(self):
        self.nc = bass.Bass("TRN2", target_bir_lowering=False)
        self.P = Prog(self.nc)
        self.din = {}
        self.dout = {}
        self.ps = [self.nc.alloc_psum_tensor("ps%d" % i, [128, 512], F32) for i in range(8)]
        self.psrot = Rot(8)
        self.uid = 0

    def inp(self, name, shape, dt=F32):
        t = self.nc.dram_tensor(name, list(shape), dt, kind="ExternalInput").ap()
        self.din[name] = t
        return t

    def out(self, name, shape, dt=F32):
        t = self.nc.dram_tensor(name, list(shape), dt, kind="ExternalOutput").ap()
        self.dout[name] = t
        return t

    def scratch(self, name, shape, dt):
        if name in DEBUG:
            return self.out(name, shape, dt)
        return self.nc.dram_tensor(name, list(shape), dt, kind="Internal").ap()

    def psum(self):
        i = self.psrot.nxt()
        return self.ps[i], ("ps", i)

    def sb(self, st, name, shape, dt):
        self.uid += 1
        return st.enter_context(self.nc.sbuf_tensor("%s_%d" % (name, self.uid), list(shape), dt))


def build_consts(k, st):
    P = k.P
    c = {}
    ident = k.sb(st, "ident", [128, 128], F32)
    P.op("pool", lambda e: e.memset(ident[:], 0.0), writes=["ident"])
    P.op("pool", lambda e: e.affine_select(out=ident[:], in_=ident[:], pattern=[[-1, 128]],
                                           compare_op=ALU.not_equal, fill=1.0, base=0,
                                           channel_multiplier=1),
         reads=["ident"], writes=["ident"])
    ones = k.sb(st, "ones", [128, 128], F32)
    P.op("pool", lambda e: e.memset(ones[:], 1.0), writes=["ones"])
    onesb = k.sb(st, "onesb", [128, 128], BF16)
    P.op("pool", lambda e: e.memset(onesb[:], 1.0), writes=["onesb"])
    epst = k.sb(st, "epst", [128, 1], F32)
    P.op("pool", lambda e: e.memset(epst[:], EPS), writes=["epst"])
    c.update(ident=ident, ones=ones, onesb=onesb, eps=epst)
    return c


def stage_norm_T(k, c, src, nblk, wvec, dst, tag):
    P = k.P
    with ExitStack() as st:
        wbc = k.sb(st, "wbc", [128, D], F32)
        P.dma("sp", wbc[:], wvec.partition_broadcast(128), writes=[tag + "wbc"])
        xts = [k.sb(st, "xt", [128, D], F32) for _ in range(2)]
        hn = k.sb(st, "hn", [128, D], F32)
        junk = k.sb(st, "junk", [128, D], BF16)
        hTs = [k.sb(st, "hT", [128, 32, 128], BF16) for _ in range(2)]
        ss = [k.sb(st, "ss", [128, 1], F32) for _ in range(2)]
        rs = [k.sb(st, "rs", [128, 1], F32) for _ in range(2)]
        for blk in range(nblk):
            i = blk % 2
            xt, hT = xts[i], hTs[i]
            P.dma("sp", xt[:], src[blk * 128:(blk + 1) * 128, :], writes=[(tag, "xt", i)])
            P.op("act", lambda e, xt=xt, i=i: e.activation(out=junk[:], in_=xt[:], func=AF.Square,
                                                         accum_out=ss[i][:]),
                 reads=[(tag, "xt", i)], writes=[(tag, "junk"), (tag, "ss", i)])
            P.op("act", lambda e, i=i: e.activation(out=rs[i][:], in_=ss[i][:], func=AF.Sqrt,
                                                   scale=1.0 / D, bias=c["eps"][:]),
                 reads=[(tag, "ss", i), "epst"], writes=[(tag, "rs", i)])
            P.op("dve", lambda e, i=i: e.reciprocal(out=rs[i][:], in_=rs[i][:]),
                 reads=[(tag, "rs", i)], writes=[(tag, "rs", i)])
            P.op("dve", lambda e, xt=xt, i=i: e.scalar_tensor_tensor(
                out=hn[:], in0=xt[:], scalar=rs[i][:], in1=wbc[:], op0=ALU.mult, op1=ALU.mult),
                reads=[(tag, "xt", i), (tag, "rs", i), tag + "wbc"], writes=[(tag, "hn")])
            for g in range(8):
                ps, pk = k.psum()
                for q in range(4):
                    kc = g * 4 + q
                    P.op("pe", lambda e, ps=ps, q=q, kc=kc: e.transpose(
                        out=ps[:, q * 128:(q + 1) * 128], in_=hn[:, kc * 128:(kc + 1) * 128],
                        identity=c["ident"][:]),
                        reads=[(tag, "hn"), "ident"], writes=[pk])
                eng = "act" if g % 2 == 0 else "dve"
                if eng == "act":
                    P.op("act", lambda e, ps=ps, hT=hT, g=g: e.activation(
                        out=hT[:, g * 4:(g + 1) * 4, :],
                        in_=ps[:].rearrange("p (a t) -> p a t", a=4), func=AF.Copy),
                        reads=[pk], writes=[(tag, "hT", i, g)])
                else:
                    P.op("dve", lambda e, ps=ps, hT=hT, g=g: e.tensor_copy(
                        out=hT[:, g * 4:(g + 1) * 4, :],
                        in_=ps[:].rearrange("p (a t) -> p a t", a=4)),
                        reads=[pk], writes=[(tag, "hT", i, g)])
            P.dma("sp", dst[blk], hT[:].rearrange("p a t -> p (a t)"),
                  reads=[(tag, "hT", i, g) for g in range(8)], writes=[(tag, "dst", blk)])
        P.barrier()


class Gemm:
    def __init__(self, k, st, T):
        self.k = k
        self.T = T
        self.X = k.sb(st, "X", [128, T // 128, 32, 128], BF16)
        self.W = [k.sb(st, "W", [128, 32, 512], BF16) for _ in range(2)]
        self.wrot = Rot(2)
        self.tag = "g%d" % k.uid

    def load_x(self, xt_dram, blocks):
        P = self.k.P
        for j, blk in enumerate(blocks):
            P.dma("sp", self.X[:, j].rearrange("p a t -> p (a t)"), xt_dram[blk],
                  writes=[(self.tag, "X", j)])

    def load_w(self, wap, ncols):
        P = self.k.P
        i = self.wrot.nxt()
        W = self.W[i]
        wv = wap.rearrange("(a p) n -> p a n", p=128)
        for q in range(4):
            P.dma("pool", W[:, q * 8:(q + 1) * 8, 0:ncols], wv[:, q * 8:(q + 1) * 8, :],
                  writes=[(self.tag, "W", i, q)])
        return W, (self.tag, "W", i)

    def tm(self, wap, ncols, tblocks, epi):
        P = self.k.P
        W, wk = self.load_w(wap, ncols)
        for tb in tblocks:
            ps, pk = self.k.psum()
            for kc in range(32):
                P.op("pe", lambda e, ps=ps, kc=kc, tb=tb, W=W: e.matmul(
                    ps[:, 0:ncols], lhsT=self.X[:, tb, kc, :],
                    rhs=W[:, kc, 0:ncols], start=(kc == 0), stop=(kc == 31)),
                    reads=[(self.tag, "X", tb), wk + (kc // 8,)], writes=[pk])
            epi(tb, ps, pk)

    def fm(self, wap, ncols, subs, epi):
        P = self.k.P
        W, wk = self.load_w(wap, ncols)
        for ci in range(ncols // 128):
            for si, (t0, nt) in enumerate(subs):
                ps, pk = self.k.psum()
                xr = [(self.tag, "X", j) for j in range(t0 // 128, (t0 + nt + 127) // 128)]
                for kc in range(32):
                    P.op("pe", lambda e, ps=ps, kc=kc, ci=ci, t0=t0, nt=nt, W=W: e.matmul(
                        ps[:, 0:nt], lhsT=W[:, kc, ci * 128:(ci + 1) * 128],
                        rhs=self.X[:, t0 // 128:(t0 + nt) // 128, kc, :], start=(kc == 0),
                        stop=(kc == 31)),
                        reads=xr + [wk + (kc // 8,)], writes=[pk])
                epi(ci, si, t0, nt, ps, pk)


def stage_inproj_tok(k, c, io, sc):
    P = k.P
    w_in = io["w_in"]
    with ExitStack() as st:
        g = Gemm(k, st, NTOK)
        g.load_x(sc["XT_tok"], list(range(10)))
        cs = k.sb(st, "cs", [128, 10, 256], F32)
        P.dma("sp", cs[:], io["cs_tab"].rearrange("(b p) n -> p b n", p=128), writes=["cs"])
        stg = [k.sb(st, "stg", [128, 512], F32) for _ in range(3)]
        srot = Rot(3)
        t1 = k.sb(st, "t1", [128, 8, 16], F32)
        t2 = k.sb(st, "t2", [128, 8, 16], F32)

        for nt in range(6 if 'i' in KSUB else 0):
            def epi(tb, ps, pk, nt=nt):
                si = srot.nxt()
                s = stg[si]
                sk = ("stg", si)
                P.op("act", lambda e, s=s, ps=ps: e.activation(out=s[:], in_=ps[:], func=AF.Copy),
                     reads=[pk], writes=[sk])
                if nt < 5 and not os.environ.get('KNOROPE'):
                    pv = ps[:].rearrange("p (h d) -> p h d", h=8)
                    sv = s[:].rearrange("p (h d) -> p h d", h=8)
                    C16 = cs[:, tb, 0:128].rearrange("p (h d) -> p h d", h=8)
                    nS = cs[:, tb, 128:192].rearrange("p (h d) -> p h d", h=8)
                    Sn = cs[:, tb, 192:256].rearrange("p (h d) -> p h d", h=8)
                    P.op("dve", lambda e, sv=sv, C16=C16: e.tensor_tensor(
                        out=t1[:], in0=sv[:, :, 0:16], in1=C16, op=ALU.mult),
                        reads=[sk, "cs"], writes=["t1"])
                    P.op("dve", lambda e, sv=sv, nS=nS: e.tensor_tensor(
                        out=t2[:, :, 0:8], in0=sv[:, :, 8:16], in1=nS, op=ALU.mult),
                        reads=[sk, "cs"], writes=["t2a"])
                    P.op("dve", lambda e, sv=sv, Sn=Sn: e.tensor_tensor(
                        out=t2[:, :, 8:16], in0=sv[:, :, 0:8], in1=Sn, op=ALU.mult),
                        reads=[sk, "cs"], writes=["t2b"])
                    P.op("dve", lambda e, sv=sv: e.tensor_tensor(
                        out=sv[:, :, 0:16], in0=t1[:], in1=t2[:], op=ALU.add),
                        reads=["t1", "t2a", "t2b"], writes=[sk])
                P.dma("sp", sc["QKV"][tb * 128:(tb + 1) * 128, nt * 512:(nt + 1) * 512], s[:],
                      reads=[sk], writes=[("QKV", tb, nt)])
            g.tm(w_in[:, nt * 512:(nt + 1) * 512], 512, list(range(10)), epi)

        stb = [k.sb(st, "stb", [128, 512], BF16) for _ in range(3)]
        brot = Rot(3)
        subs = [(128, 512), (640, 512), (1152, 128)]
        for nt in range(16 if 'ii' in KSUB else 0):
            def epi(ci, si_, t0, n, ps, pk, nt=nt):
                bi = brot.nxt()
                s = stb[bi]
                P.op("act", lambda e, s=s, ps=ps, n=n: e.activation(out=s[:, 0:n], in_=ps[:, 0:n],
                                                                  func=AF.Sigmoid),
                     reads=[pk], writes=[("stb", bi)])
                ch = nt * 4 + ci
                P.dma("sp", sc["SG_T"][ch, :, t0 - 128:t0 - 128 + n], s[:, 0:n],
                      reads=[("stb", bi)], writes=[("SG", ch, t0)])
            c0 = 11296 + nt * 512
            g.fm(w_in[:, c0:c0 + 512], 512, subs, epi)

        for nt in range(16 if 'iii' in KSUB else 0):
            def epi(ci, si_, t0, n, ps, pk, nt=nt):
                si = srot.nxt()
                s = stg[si]
                P.op("act", lambda e, s=s, ps=ps: e.activation(out=s[:, 0:128], in_=ps[:, 0:128],
                                                             func=AF.Copy),
                     reads=[pk], writes=[("stg", si)])
                ch = nt * 4 + ci
                P.dma("sp", sc["DNS_T"][ch], s[:, 0:128], reads=[("stg", si)],
                      writes=[("DNS", ch)])
            c0 = 3072 + nt * 512
            g.fm(w_in[:, c0:c0 + 512], 512, [(1152, 128)], epi)

        for nt in range(12 if 'iv' in KSUB else 0):
            def epi(tb, ps, pk, nt=nt):
                si = srot.nxt()
                s = stg[si]
                P.op("act", lambda e, s=s, ps=ps: e.activation(out=s[:], in_=ps[:], func=AF.Copy),
                     reads=[pk], writes=[("stg", si)])
                src = s[0:64, :]
                P.dma("sp", sc["CVS"][:, nt * 512:(nt + 1) * 512], src,
                      reads=[("stg", si)], writes=[("CVS", nt)])
            c0 = 3072 + nt * 512
            g.tm(w_in[:, c0:c0 + 512], 512, [9], epi)

        def epi_ba(tb, ps, pk):
            si = srot.nxt()
            s = stg[si]
            P.op("act", lambda e, s=s, ps=ps: e.activation(out=s[:, 0:32], in_=ps[:, 0:32],
                                                         func=AF.Copy),
                 reads=[pk], writes=[("stg", si)])
            P.dma("sp", sc["BAS"][:, :], s[:, 0:32], reads=[("stg", si)], writes=["BAS"])
        g.tm(w_in[:, 11264:11296], 32, [9], epi_ba)
        P.barrier()


def stage_inproj_seq(k, c, io, sc):
    P = k.P
    with ExitStack() as st:
        g = Gemm(k, st, NTOK)
        stg = [k.sb(st, "stg", [128, 512], F32) for _ in range(3)]
        srot = Rot(3)
        groups = [list(range(0, 10)), list(range(10, 20)), list(range(20, 30)), [30, 31, 32]]
        for gi, blks in enumerate(groups):
            g.tag = "gs%d" % gi
            g.load_x(sc["XT_seq"], blks)
            T = len(blks) * 128
            u0 = blks[0] * 128
            subs = [(t, min(512, T - t)) for t in range(0, T, 512)]
            for nt in range(4):
                def epi(ci, si_, t0, n, ps, pk, nt=nt):
                    si = srot.nxt()
                    s = stg[si]
                    P.op("act", lambda e, s=s, ps=ps, n=n: e.activation(
                        out=s[:, 0:n], in_=ps[:, 0:n], func=AF.Copy),
                        reads=[pk], writes=[("stg", si)])
                    ch = nt * 4 + ci
                    P.dma("sp", sc["DN_T"][ch, :, u0 + t0:u0 + t0 + n], s[:, 0:n],
                          reads=[("stg", si)], writes=[("DN", ch, u0 + t0)])
                g.fm(io["w_dn"][:, nt * 512:(nt + 1) * 512], 512, subs, epi)

            def epi_ba(tb, ps, pk):
                si = srot.nxt()
                s = stg[si]
                P.op("act", lambda e, s=s, ps=ps: e.activation(out=s[:, 0:8], in_=ps[:, 0:8],
                                                             func=AF.Copy),
                     reads=[pk], writes=[("stg", si)])
                r0 = u0 + tb * 128
                P.dma("sp", sc["BA"][r0:r0 + 128, :], s[:, 0:8], reads=[("stg", si)],
                      writes=[("BA", r0)])
            g.tm(io["w_ba"], 8, list(range(len(blks))), epi_ba)
            P.barrier()


def stage_attention(k, c, io, sc):
    P = k.P
    QKV = sc["QKV"]
    with ExitStack() as st:
        mk = k.sb(st, "mk", [128, 6, 512], BF16)
        for i in range(6):
            P.dma("pool", mk[:, i, :], io["masks"][i], writes=[("mk", i)])
        sinkE = k.sb(st, "sinkE", [64, 32], F32)
        P.dma("sp", sinkE[:], io["sinks"].partition_broadcast(64), writes=["sinkE"])
        P.op("act", lambda e: e.activation(out=sinkE[:], in_=sinkE[:], func=AF.Exp),
             reads=["sinkE"], writes=["sinkE"])
        sinkb = k.sb(st, "sinkb", [64, 32, 128], F32)
        P.op("dve", lambda e: e.tensor_copy(out=sinkb[:], in_=bc(sinkE[:], 2, 128)),
             reads=["sinkE"], writes=["sinkb"])
        OC = k.sb(st, "OC", [64, 32, 64], F32)
        DC = k.sb(st, "DC", [64, 32, 64], F32)
        kin = [k.sb(st, "kin", [128, 512], F32) for _ in range(2)]
        qin = k.sb(st, "qin", [128, 2048], F32)
        kT = [k.sb(st, "kT", [64, 8, 128], BF16) for _ in range(2)]
        vb = [k.sb(st, "vb", [128, 512], BF16) for _ in range(2)]
        qT = k.sb(st, "qT", [64, 32, 128], BF16)
        pT = [k.sb(st, "pT", [128, 512], BF16) for _ in range(4)]
        prot = Rot(4)
        den = [k.sb(st, "den", [64, 512], F32) for _ in range(2)]
        oT = k.sb(st, "oT", [64, 32, 128], BF16)
        numt = k.sb(st, "numt", [64, 512], F32)
        ident = c["ident"]

        def transposes(src, nheads, dst, scale, rk, wk):
            for g in range(nheads // 4):
                ps, pk = k.psum()
                for q in range(4):
                    h = g * 4 + q
                    P.op("pe", lambda e, ps=ps, q=q, h=h: e.transpose(
                        out=ps[0:64, q * 128:(q + 1) * 128], in_=src[:, h * 64:(h + 1) * 64],
                        identity=ident[:]), reads=[rk, "ident"], writes=[pk])
                P.op("act", lambda e, ps=ps, g=g: e.activation(
                    out=dst[:, g * 4:(g + 1) * 4, :].rearrange("p a t -> p (a t)"),
                    in_=ps[0:64, :], func=AF.Copy, scale=scale), reads=[pk], writes=[wk + (g,)])

        kc_in = [k.sb(st, "kcin", [128, 512], F32) for _ in range(2)]
        kcT = [k.sb(st, "kcT", [64, 8, 128], BF16) for _ in range(2)]
        vc = [k.sb(st, "vc", [128, 512], BF16) for _ in range(2)]
        pTc = [k.sb(st, "pTc", [128, 128], BF16) for _ in range(2)]
        P.dma("sp", qin[:], QKV[1152:1280, 0:2048], writes=["qin"])
        transposes(qin, 32, qT, 0.125, "qin", ("qT",))
        qTr = [("qT", g) for g in range(8)]
        for s in range(16):
            i = s % 2
            P.dma("sp", kc_in[i][:], io["cache_k"][s], writes=[("kcin", i)])
            P.dma("pool", vc[i][:], io["cache_v"][s], writes=[("vc", i)])
            transposes(kc_in[i], 8, kcT[i], 1.0, ("kcin", i), ("kcT", i))
            ps, pk = k.psum()
            for hk in range(8):
                P.op("pe", lambda e, ps=ps, hk=hk, i=i, s=s: e.matmul(
                    ps[:, hk * 16:(hk + 1) * 16], lhsT=kcT[i][:, hk, :],
                    rhs=qT[:, 4 * hk:4 * hk + 4, 4 * s:4 * s + 4], start=True, stop=True),
                    reads=[("kcT", i, hk // 4)] + qTr, writes=[pk])
            P.op("act", lambda e, ps=ps, i=i: e.activation(out=pTc[i][:], in_=ps[:, 0:128],
                                                          func=AF.Exp),
                 reads=[pk], writes=[("pTc", i)])
            P.op("pool", lambda e, i=i: e.tensor_tensor(out=pTc[i][:], in0=pTc[i][:],
                                                        in1=mk[:, 5, 0:128], op=ALU.mult),
                 reads=[("pTc", i), ("mk", 5)], writes=[("pTc", i)])
            pso, pko = k.psum()
            for hk in range(8):
                P.op("pe", lambda e, pso=pso, hk=hk, i=i: e.matmul(
                    pso[0:64, hk * 16:(hk + 1) * 16], lhsT=vc[i][:, hk * 64:(hk + 1) * 64],
                    rhs=pTc[i][:, hk * 16:(hk + 1) * 16], start=True, stop=True),
                    reads=[("vc", i), ("pTc", i)], writes=[pko])
            P.op("pe", lambda e, pso=pso, i=i: e.matmul(
                pso[0:64, 128:256], lhsT=c["onesb"][:, 0:64], rhs=pTc[i][:], start=True, stop=True),
                reads=["onesb", ("pTc", i)], writes=[pko])
            P.op("act", lambda e, pso=pso, s=s: e.activation(
                out=OC[:, :, 4 * s:4 * s + 4],
                in_=pso[0:64, 0:128].rearrange("p (h t) -> p h t", t=4), func=AF.Copy),
                reads=[pko], writes=[("OC", s)])
            P.op("dve", lambda e, pso=pso, s=s: e.tensor_copy(
                out=DC[:, :, 4 * s:4 * s + 4],
                in_=pso[0:64, 128:256].rearrange("p (h t) -> p h t", t=4)),
                reads=[pko], writes=[("DC", s)])
            P.dma("sp", io["wink_s"][s, 0:124, :], io["cache_k"][s, 4:128, :], writes=[("wks", s)])
            P.dma("sp", io["winv_s"][s, 0:124, :], io["cache_v"][s, 4:128, :], writes=[("wvs", s)])
            P.dma("sp", io["wink_s"][s, 124:128, :], QKV[1152 + 4 * s:1156 + 4 * s, 2048:2560],
                  writes=[("wks2", s)])
            P.dma("sp", io["winv_s"][s, 124:128, :], QKV[1152 + 4 * s:1156 + 4 * s, 2560:3072],
                  writes=[("wvs2", s)])
        ocr = [("OC", s) for s in range(16)] + [("DC", s) for s in range(16)]

        for tb in range(10):
            i = tb % 2
            r0 = tb * 128
            P.dma("sp", kin[i][:], QKV[r0:r0 + 128, 2048:2560], writes=[("kin", i)])
            P.dma("pool", vb[i][:], QKV[r0:r0 + 128, 2560:3072], writes=[("vb", i)])
            transposes(kin[i], 8, kT[i], 1.0, ("kin", i), ("kT", i))
            if tb == 0:
                continue
            P.dma("sp", qin[:], QKV[r0:r0 + 128, 0:2048], writes=["qin"])
            transposes(qin, 32, qT, 0.125, "qin", ("qT",))
            mixed = (tb == 9)
            mc = 3 if mixed else 0
            mp = 4 if mixed else (2 if tb == 1 else 1)
            for hk in range(8):
                pts = []
                for (ki, mi) in ((1 - i, mp), (i, mc)):
                    ps, pk = k.psum()
                    P.op("pe", lambda e, ps=ps, ki=ki, hk=hk: e.matmul(
                        ps[:, :], lhsT=kT[ki][:, hk, :],
                        rhs=qT[:, 4 * hk:4 * hk + 4, :], start=True, stop=True),
                        reads=[("kT", ki, hk // 4), ("qT", hk // 2 * 0 + hk)], writes=[pk])
                    pi = prot.nxt()
                    P.op("act", lambda e, ps=ps, pi=pi: e.activation(out=pT[pi][:], in_=ps[:],
                                                                    func=AF.Exp),
                         reads=[pk], writes=[("pT", pi)])
                    P.op("pool", lambda e, pi=pi, mi=mi: e.tensor_tensor(
                        out=pT[pi][:], in0=pT[pi][:], in1=mk[:, mi, :], op=ALU.mult),
                        reads=[("pT", pi), ("mk", mi)], writes=[("pT", pi)])
                    pts.append((pi, ki))
                pso, pko = k.psum()
                psd, pkd = k.psum()
                for n_, (pi, ki) in enumerate(pts):
                    P.op("pe", lambda e, pso=pso, pi=pi, ki=ki, hk=hk, n_=n_: e.matmul(
                        pso[0:64, :], lhsT=vb[ki][:, hk * 64:(hk + 1) * 64], rhs=pT[pi][:],
                        start=(n_ == 0), stop=(n_ == 1)),
                        reads=[("vb", ki), ("pT", pi)], writes=[pko])
                for n_, (pi, ki) in enumerate(pts):
                    P.op("pe", lambda e, psd=psd, pi=pi, n_=n_: e.matmul(
                        psd[0:64, :], lhsT=c["onesb"][:, 0:64], rhs=pT[pi][:],
                        start=(n_ == 0), stop=(n_ == 1)),
                        reads=["onesb", ("pT", pi)], writes=[pkd])
                di = hk % 2
                dn = den[di]
                P.op("dve", lambda e, psd=psd, dn=dn, hk=hk: e.tensor_tensor(
                    out=dn[:], in0=psd[0:64, :],
                    in1=sinkb[:, 4 * hk:4 * hk + 4, :].rearrange("p a t -> p (a t)"), op=ALU.add),
                    reads=[pkd, "sinkb"], writes=[("den", di)])
                num_in = pso
                if mixed:
                    dv = dn[:].rearrange("p (a t) -> p a t", a=4)
                    P.op("dve", lambda e, dv=dv, hk=hk: e.tensor_tensor(
                        out=dv[:, :, 0:64], in0=dv[:, :, 0:64], in1=DC[:, 4 * hk:4 * hk + 4, :],
                        op=ALU.add), reads=[("den", di)] + ocr, writes=[("den", di)])
                P.op("act", lambda e, dn=dn: e.activation(out=dn[:], in_=dn[:], func=AF.Ln),
                     reads=[("den", di)], writes=[("den", di)])
                P.op("act", lambda e, dn=dn: e.activation(out=dn[:], in_=dn[:], func=AF.Exp,
                                                         scale=-1.0),
                     reads=[("den", di)], writes=[("den", di)])
                ov = oT[:, 4 * hk:4 * hk + 4, :]
                if mixed:
                    P.op("dve", lambda e, pso=pso: e.tensor_copy(out=numt[:], in_=pso[0:64, :]),
                         reads=[pko], writes=["numt"])
                    nv = numt[:].rearrange("p (a t) -> p a t", a=4)
                    P.op("dve", lambda e, nv=nv, hk=hk: e.tensor_tensor(
                        out=nv[:, :, 0:64], in0=nv[:, :, 0:64], in1=OC[:, 4 * hk:4 * hk + 4, :],
                        op=ALU.add), reads=["numt"] + ocr, writes=["numt"])
                    P.op("dve", lambda e, dn=dn, ov=ov: e.tensor_tensor(
                        out=ov.rearrange("p a t -> p (a t)"), in0=numt[:], in1=dn[:], op=ALU.mult),
                        reads=["numt", ("den", di)], writes=[("oT", hk)])
                else:
                    P.op("dve", lambda e, pso=pso, dn=dn, ov=ov: e.tensor_tensor(
                        out=ov.rearrange("p a t -> p (a t)"), in0=pso[0:64, :], in1=dn[:],
                        op=ALU.mult), reads=[pko, ("den", di)], writes=[("oT", hk)])
            P.dma("sp", sc["OA_T"][tb - 1].rearrange("p (h t) -> p h t", h=32), oT[:],
                  reads=[("oT", hk) for hk in range(8)], writes=[("OA", tb)])
        P.dma("sp", io["wink_p"][0:112, :], QKV[1024 + 16:1152, 2048:2560], writes=["wkp1"])
        P.dma("sp", io["wink_p"][112:128, :], QKV[1216:1232, 2048:2560], writes=["wkp2"])
        P.dma("sp", io["winv_p"][0:112, :], QKV[1024 + 16:1152, 2560:3072], writes=["wvp1"])
        P.dma("sp", io["winv_p"][112:128, :], QKV[1216:1232, 2560:3072], writes=["wvp2"])
        P.barrier()


class DnChunk:
    def __init__(self, k, st, c, dc, nset=2, lowp=False):
        self.k, self.c, self.dc = k, c, dc
        self.lowp = lowp
        LP = ("vb", "kbg", "kd", "qd", "QK", "WT") if lowp else ()
        self.sets = []
        for s in range(nset):
            t = {}
            if lowp:
                t["Rb"] = k.sb(st, "dn_Rb", [64, 256], BF16)
            for nm, shp in (("bx", [128, 16]), ("vb", [64, 512]), ("kbg", [64, 512]), ("kd", [64, 512]),
                            ("X4", [64, 256]), ("E4", [64, 256]), ("Dm", [64, 256]), ("Ds", [64, 256]),
                            ("DmT", [64, 256]), ("gB", [64, 512]), ("EG", [128, 256]),
                            ("qd", [128, 256]), ("N", [64, 256]), ("NT", [64, 256]),
                            ("Pa", [64, 256]), ("PTa", [64, 256]), ("Pb", [64, 256]),
                            ("PTb", [64, 256]), ("R", [64, 256]), ("QK", [64, 256]),
                            ("U", [64, 512]), ("WT", [128, 256]), ("tmp", [64, 256]),
                            ("nb", [64, 4]), ("bg2", [64, 4])):
                t[nm] = k.sb(st, "dn_" + nm, shp, BF16 if nm in LP else F32)
            self.sets.append(t)
        self.rot = Rot(nset)

    def run(self, *a, **kw):
        out = {}
        for _ in self.run_gen(out, *a, **kw):
            pass
        return out["T"], out["K_"]

    def run_gen(self, out, qs, ks, vs, rk, beta, g, bgk, levels=5, qsb=None, ksb=None, rkb=None):
        k, P, c, dc = self.k, self.k.P, self.c, self.dc
        si = self.rot.nxt()
        T = self.sets[si]
        K_ = lambda nm: ("dn", si, nm)
        out["T"], out["K_"] = T, K_
        ones, ident = c["ones"], c["ident"]
        U64, Ys = dc["U64"], dc["Ys"]
        psB, pkB = k.psum()
        P.op("pe", lambda e: e.matmul(psB[0:64, 0:4], lhsT=U64, rhs=g, start=True, stop=True),
             reads=[bgk, "dconst"], writes=[pkB])
        P.op("pe", lambda e: e.matmul(psB[0:64, 4:8], lhsT=Ys, rhs=g, start=True, stop=True),
             reads=[bgk, "dconst"], writes=[pkB])
        P.op("pe", lambda e: e.matmul(psB[:, 8:12], lhsT=dc["GL"], rhs=g, start=True, stop=True),
             reads=[bgk, "dconst"], writes=[pkB])
        bx = T["bx"]
        P.op("act", lambda e: e.activation(out=bx[:, 0:12], in_=psB[:, 0:12], func=AF.Exp),
             reads=[pkB], writes=[K_("bx")])
        P.op("pool", lambda e: e.tensor_scalar(out=T["nb"][:], in0=beta, scalar1=-1.0, scalar2=None,
                                               op0=ALU.mult), reads=[bgk], writes=[K_("nb")])
        P.op("pool", lambda e: e.tensor_tensor(out=T["bg2"][:], in0=beta, in1=bx[0:64, 0:4],
                                               op=ALU.mult), reads=[bgk, K_("bx")], writes=[K_("bg2")])
        yield
        psK, pkK = k.psum()
        psV, pkV = k.psum()
        for h in range(4):
            P.op("pe", lambda e, h=h: e.transpose(out=psK[0:64, h * 128:(h + 1) * 128], in_=ks(h),
                                                  identity=ident[:]), reads=rk + ["ident"], writes=[pkK])
        for h in range(4):
            P.op("pe", lambda e, h=h: e.transpose(out=psV[0:64, h * 128:(h + 1) * 128], in_=vs(h),
                                                  identity=ident[:]), reads=rk + ["ident"], writes=[pkV])
        v3 = lambda t: t[:].rearrange("p (h e) -> p h e", h=4)
        P.op("dve", lambda e: e.tensor_tensor(out=v3(T["vb"]), in0=psV[0:64, :].rearrange("p (h e) -> p h e", h=4),
                                              in1=bc(beta, 2, 128), op=ALU.mult),
             reads=[pkV, bgk], writes=[K_("vb")])
        P.op("dve", lambda e: e.tensor_tensor(out=v3(T["kbg"]), in0=psK[0:64, :].rearrange("p (h e) -> p h e", h=4),
                                              in1=bc(T["bg2"][:], 2, 128), op=ALU.mult),
             reads=[pkK, K_("bg2")], writes=[K_("kbg")])
        P.op("dve", lambda e: e.tensor_tensor(out=v3(T["kd"]), in0=psK[0:64, :].rearrange("p (h e) -> p h e", h=4),
                                              in1=bc(bx[0:64, 4:8], 2, 128), op=ALU.mult),
             reads=[pkK, K_("bx")], writes=[K_("kd")])
        yield
        x3 = T["X4"][:].rearrange("p (h i) -> p h i", h=4)
        P.op("pool", lambda e: e.tensor_tensor(out=x3, in0=bc(U64, 1, 4), in1=bc(g, 2, 64), op=ALU.mult),
             reads=[bgk, "dconst"], writes=[K_("X4")])
        psA, pkA = k.psum()
        for h in range(4):
            P.op("pe", lambda e, h=h: e.matmul(psA[0:64, h * 64:(h + 1) * 64],
                                               lhsT=T["X4"][:, h * 64:(h + 1) * 64], rhs=Ys,
                                               start=True, stop=True),
                 reads=[K_("X4"), "dconst"], writes=[pkA])
        P.op("pe", lambda e: e.matmul(psA[0:64, 256:512], lhsT=Ys, rhs=T["X4"][:], start=True, stop=True),
             reads=[K_("X4"), "dconst"], writes=[pkA])
        yield
        P.op("act", lambda e: e.activation(out=T["E4"][:], in_=psA[0:64, 0:256], func=AF.Exp),
             reads=[pkA], writes=[K_("E4")])
        P.op("act", lambda e: e.activation(out=T["DmT"][:], in_=psA[0:64, 256:512], func=AF.Exp),
             reads=[pkA], writes=[K_("DmT")])
        P.op("pool", lambda e: e.tensor_tensor(out=T["Ds"][:], in0=T["E4"][:], in1=dc["strict4"], op=ALU.mult),
             reads=[K_("E4"), "dconst"], writes=[K_("Ds")])
        P.op("pool", lambda e: e.tensor_tensor(out=T["DmT"][:], in0=T["DmT"][:], in1=dc["triu4"], op=ALU.mult),
             reads=[K_("DmT"), "dconst"], writes=[K_("DmT")])
        yield
        gb3 = T["gB"][:].rearrange("p (h d) -> p h d", h=4)
        P.op("pool", lambda e: e.tensor_copy(out=gb3, in_=bc(g, 2, 128)), reads=[bgk], writes=[K_("gB")])
        psG, pkG = k.psum()
        for h in range(4):
            P.op("pe", lambda e, h=h: e.matmul(psG[:, h * 64:(h + 1) * 64],
                                               lhsT=T["gB"][:, h * 128:(h + 1) * 128], rhs=U64,
                                               start=True, stop=True),
                 reads=[K_("gB"), "dconst"], writes=[pkG])
        yield
        P.op("act", lambda e: e.activation(out=T["EG"][:], in_=psG[:, 0:256], func=AF.Exp),
             reads=[pkG], writes=[K_("EG")])
        for h in range(4):
            P.op("pool", lambda e, h=h: e.tensor_tensor(out=T["qd"][:, h * 64:(h + 1) * 64], in0=qs(h),
                                                        in1=T["EG"][:, h * 64:(h + 1) * 64], op=ALU.mult),
                 reads=rk + [K_("EG")], writes=[K_("qd")])
        yield
        psKK, pkKK = k.psum()
        gq = qsb if qsb is not None else qs
        gk = ksb if ksb is not None else ks
        grk = rkb if rkb is not None else rk
        for h in range(4):
            P.op("pe", lambda e, h=h: e.matmul(psKK[0:64, h * 64:(h + 1) * 64], lhsT=gk(h), rhs=gk(h),
                                               start=True, stop=True), reads=grk, writes=[pkKK])
        for h in range(4):
            P.op("pe", lambda e, h=h: e.matmul(psKK[0:64, 256 + h * 64:256 + (h + 1) * 64], lhsT=gk(h),
                                               rhs=gq(h), start=True, stop=True), reads=grk, writes=[pkKK])
        yield
        t3 = T["tmp"][:].rearrange("p (h j) -> p h j", h=4)
        P.op("dve", lambda e: e.tensor_tensor(out=t3, in0=psKK[0:64, 0:256].rearrange("p (h j) -> p h j", h=4),
                                              in1=bc(T["nb"][:], 2, 64), op=ALU.mult),
             reads=[pkKK, K_("nb")], writes=[K_("tmp")])
        P.op("dve", lambda e: e.tensor_tensor(out=T["N"][:], in0=T["tmp"][:], in1=T["Ds"][:], op=ALU.mult),
             reads=[K_("tmp"), K_("Ds")], writes=[K_("N")])
        P.op("dve", lambda e: e.tensor_tensor(out=T["QK"][:], in0=psKK[0:64, 256:512], in1=T["DmT"][:],
                                              op=ALU.mult), reads=[pkKK, K_("DmT")], writes=[K_("QK")])
        yield
        psN, pkN = k.psum()
        for h in range(4):
            P.op("pe", lambda e, h=h: e.transpose(out=psN[0:64, h * 64:(h + 1) * 64],
                                                  in_=T["N"][:, h * 64:(h + 1) * 64],
                                                  identity=ident[0:64, 0:64]),
                 reads=[K_("N"), "ident"], writes=[pkN])
        yield
        P.op("act", lambda e: e.activation(out=T["NT"][:], in_=psN[0:64, 0:256], func=AF.Copy),
             reads=[pkN], writes=[K_("NT")])
        P.op("pool", lambda e: e.tensor_tensor(out=T["R"][:], in0=T["NT"][:], in1=dc["I4"], op=ALU.add),
             reads=[K_("NT"), "dconst"], writes=[K_("R")])
        Pc, PTc = ("N", "NT")
        for lv in range(levels):
            yield
            Pn, PTn = ("Pa", "PTa") if lv % 2 == 0 else ("Pb", "PTb")
            psP, pkP = k.psum()
            for h in range(4):
                sl = slice(h * 64, (h + 1) * 64)
                P.op("pe", lambda e, sl=sl, Pc=Pc, PTc=PTc, psP=psP: e.matmul(
                    psP[0:64, sl], lhsT=T[PTc][:, sl], rhs=T[Pc][:, sl], start=True, stop=True),
                    reads=[K_(Pc), K_(PTc)], writes=[pkP])
            if lv < levels - 1:
                for h in range(4):
                    sl = slice(h * 64, (h + 1) * 64)
                    so = slice(256 + h * 64, 256 + (h + 1) * 64)
                    P.op("pe", lambda e, sl=sl, so=so, Pc=Pc, PTc=PTc, psP=psP: e.matmul(
                        psP[0:64, so], lhsT=T[Pc][:, sl], rhs=T[PTc][:, sl], start=True, stop=True),
                        reads=[K_(Pc), K_(PTc)], writes=[pkP])
            yield
            P.op("act", lambda e, Pn=Pn, psP=psP: e.activation(out=T[Pn][:], in_=psP[0:64, 0:256], func=AF.Copy),
                 reads=[pkP], writes=[K_(Pn)])
            if lv < levels - 1:
                P.op("act", lambda e, PTn=PTn, psP=psP: e.activation(out=T[PTn][:], in_=psP[0:64, 256:512],
                                                                   func=AF.Copy), reads=[pkP], writes=[K_(PTn)])
            yield
            psR, pkR = k.psum()
            for h in range(4):
                sl = slice(h * 64, (h + 1) * 64)
                P.op("pe", lambda e, sl=sl, Pn=Pn, psR=psR: e.matmul(psR[0:64, sl], lhsT=T[Pn][:, sl], rhs=T["R"][:, sl],
                                                                    start=True, stop=True),
                     reads=[K_(Pn), K_("R")], writes=[pkR])
            yield
            P.op("dve", lambda e, psR=psR: e.tensor_tensor(out=T["R"][:], in0=psR[0:64, 0:256], in1=T["R"][:], op=ALU.add),
                 reads=[pkR, K_("R")], writes=[K_("R")])
            Pc, PTc = Pn, PTn
        yield
        Rn = "R"
        if self.lowp:
            Rn = "Rb"
            P.op("pool", lambda e: e.tensor_copy(out=T["Rb"][:], in_=T["R"][:]), reads=[K_("R")], writes=[K_("Rb")])
        psU, pkU = k.psum()
        for h in range(4):
            P.op("pe", lambda e, h=h: e.matmul(psU[0:64, h * 128:(h + 1) * 128], lhsT=T[Rn][:, h * 64:(h + 1) * 64],
                                               rhs=T["vb"][:, h * 128:(h + 1) * 128], start=True, stop=True),
                 reads=[K_(Rn), K_("vb")], writes=[pkU])
        P.op("act", lambda e: e.activation(out=T["U"][:], in_=psU[0:64, :], func=AF.Copy),
             reads=[pkU], writes=[K_("U")])
        psW, pkW = k.psum()
        for h in range(4):
            P.op("pe", lambda e, h=h: e.matmul(psW[:, h * 64:(h + 1) * 64], lhsT=T["kbg"][:, h * 128:(h + 1) * 128],
                                               rhs=T[Rn][:, h * 64:(h + 1) * 64], start=True, stop=True),
                 reads=[K_(Rn), K_("kbg")], writes=[pkW])
        P.op("act", lambda e: e.activation(out=T["WT"][:], in_=psW[:, 0:256], func=AF.Copy),
             reads=[pkW], writes=[K_("WT")])


def dn_consts(k, st, io, variant):
    P = k.P
    t = k.sb(st, "dconst", [128, 7, 256], F32)
    P.dma("sp", t[:], io["dconst"][variant].rearrange("a p n -> p a n"), writes=["dconst"])
    dc = {
        "U64": t[0:64, 0, 0:64], "Ys": t[0:64, 1, 0:64], "strict4": t[0:64, 2, :],
        "triu4": t[0:64, 3, :], "I4": t[0:64, 4, :], "GL": t[0:64, 5, 0:128], "tril4": t[0:64, 6, :],
    }
    return dc


def dn_gates(k, P, st_tiles, bgraw, nch, negA, dtb, c, keyp):
    braw = bgraw[:, 0:nch, 0:4]
    araw = bgraw[:, 0:nch, 4:8]
    P.op("act", lambda e: e.activation(out=braw, in_=braw, func=AF.Sigmoid), reads=[keyp], writes=[keyp])
    P.op("dve", lambda e: e.tensor_tensor(out=araw, in0=araw, in1=bc(dtb, 1, nch), op=ALU.add),
         reads=[keyp, "dtb"], writes=[keyp])
    P.op("act", lambda e: e.activation(out=araw, in_=araw, func=AF.Exp), reads=[keyp], writes=[keyp])
    P.op("act", lambda e: e.activation(out=araw, in_=araw, func=AF.Ln, bias=c["ones"][0:64, 0:1]),
         reads=[keyp, "ones"], writes=[keyp])
    P.op("dve", lambda e: e.tensor_tensor(out=araw, in0=araw, in1=bc(negA, 1, nch), op=ALU.mult),
         reads=[keyp, "negA"], writes=[keyp])


def stage_deltanet_prompt(k, c, io, sc):
    P = k.P
    DN = sc["DN_T"]
    with ExitStack() as st:
        dc = dn_consts(k, st, io, 0)
        cw = k.sb(st, "cw", [128, 12, 4], F32)
        P.dma("sp", cw[:], io["cwt"], writes=["cw"])
        negA = k.sb(st, "negA", [64, 4], F32)
        dtb = k.sb(st, "dtb", [64, 4], F32)
        P.dma("sp", negA[:], io["a_log"].partition_broadcast(64), writes=["negA"])
        P.dma("sp", dtb[:], io["dt_bias"].partition_broadcast(64), writes=["dtb"])
        P.op("act", lambda e: e.activation(out=negA[:], in_=negA[:], func=AF.Exp), reads=["negA"], writes=["negA"])
        P.op("dve", lambda e: e.tensor_scalar(out=negA[:], in0=negA[:], scalar1=-1.0, scalar2=None, op0=ALU.mult),
             reads=["negA"], writes=["negA"])
        dnw = k.sb(st, "dnw", [128, 1], F32)
        P.dma("sp", dnw[:], io["delta_norm"], writes=["dnw"])
        lnq = k.sb(st, "lnq", [128, 1], F32)
        P.op("pool", lambda e: e.memset(lnq[:], math.log(128.0 ** -0.5)), writes=["lnq"])
        raw = k.sb(st, "raw", [128, 12, 520], F32)
        xc = k.sb(st, "xc", [128, 12, 512], F32)
        zt = k.sb(st, "zt", [128, 4, 512], F32)
        sq = k.sb(st, "sq", [128, 512], F32)
        rin = k.sb(st, "rin", [128, 512], F32)
        oTg = k.sb(st, "oTg", [128, 4, 512], F32)
        obt = k.sb(st, "obt", [128, 4, 512], BF16)
        bgraw = k.sb(st, "bgraw", [64, 8, 8], F32)
        S4 = k.sb(st, "S4", [128, 512], F32)
        tmpS = k.sb(st, "tmpS", [128, 512], F32)
        vnew = k.sb(st, "vnew", [64, 512], F32)
        P.op("pool", lambda e: e.memset(S4[:], 0.0), writes=["S4"])
        P.op("pool", lambda e: e.memset(raw[:, :, 0:8], 0.0), writes=["rawpad"])
        xcb = k.sb(st, "xcb", [128, 8, 512], BF16)
        Sb = k.sb(st, "Sb", [128, 512], BF16)
        vnb = k.sb(st, "vnb", [64, 512], BF16)
        P.op("pool", lambda e: e.memset(Sb[:], 0.0), writes=["Sb"])
        ch = DnChunk(k, st, c, dc, nset=3, lowp=True)
        CBv = sc["CB"].rearrange("h e q t -> h e (q t)")
        ngroups = 9

        def do_group(gi):
            u0 = gi * 512
            n = 512 if gi < 8 else 64
            nch = n // 64
            for ph in range(12):
                if gi == 0:
                    P.dma("sp", raw[:, ph, 8:8 + n], DN[ph, :, u0:u0 + n], reads=["rawpad"], writes=[("raw", ph)])
                else:
                    P.dma("sp", raw[:, ph, 5:8 + n], DN[ph, :, u0 - 3:u0 + n], reads=["rawpad"], writes=[("raw", ph)])
            for h in range(4):
                P.dma("sp", zt[:, h, 0:n], DN[12 + h, :, u0:u0 + n], writes=[("zt", h)])
            P.dma("sp", bgraw[:, 0:nch, :], sc["BA"][u0:u0 + n, :].rearrange("(a c) k -> c a k", c=64),
                  writes=["bg"])
            dn_gates(k, P, None, bgraw, nch, negA[:], dtb[:], c, "bg")
            for ph in range(12):
                o_ = xc[:, ph, 0:n]
                P.op("dve", lambda e, ph=ph, o_=o_: e.tensor_scalar(out=o_, in0=raw[:, ph, 5:5 + n], scalar1=cw[:, ph, 0:1],
                                                                 scalar2=None, op0=ALU.mult),
                     reads=[("raw", ph), "cw"], writes=[("xc", ph)])
                for j_ in range(1, 4):
                    P.op("dve", lambda e, ph=ph, o_=o_, j_=j_: e.scalar_tensor_tensor(
                        out=o_, in0=raw[:, ph, 5 + j_:5 + j_ + n], scalar=cw[:, ph, j_:j_ + 1], in1=o_,
                        op0=ALU.mult, op1=ALU.add), reads=[("raw", ph), "cw", ("xc", ph)], writes=[("xc", ph)])
                P.op("act", lambda e, o_=o_: e.activation(out=o_, in_=o_, func=AF.Silu),
                     reads=[("xc", ph)], writes=[("xc", ph)])
            for ph in range(8):
                o_ = xc[:, ph, 0:n]
                P.op("pool", lambda e, o_=o_: e.tensor_tensor(out=sq[:, 0:n], in0=o_, in1=o_, op=ALU.mult),
                     reads=[("xc", ph)], writes=["sq"])
                ps, pk = k.psum()
                P.op("pe", lambda e, ps=ps: e.matmul(ps[:, 0:n], lhsT=c["ones"][:], rhs=sq[:, 0:n], start=True, stop=True),
                     reads=["sq", "ones"], writes=[pk])
                P.op("act", lambda e, ps=ps: e.activation(out=rin[:, 0:n], in_=ps[:, 0:n], func=AF.Ln, bias=c["eps"][:]),
                     reads=[pk, "epst"], writes=["rin"])
                if ph < 4:
                    P.op("act", lambda e: e.activation(out=rin[:, 0:n], in_=rin[:, 0:n], func=AF.Exp, scale=-0.5,
                                                       bias=lnq[:]), reads=["rin", "lnq"], writes=["rin"])
                else:
                    P.op("act", lambda e: e.activation(out=rin[:, 0:n], in_=rin[:, 0:n], func=AF.Exp, scale=-0.5),
                         reads=["rin"], writes=["rin"])
                P.op("dve", lambda e, o_=o_: e.tensor_tensor(out=o_, in0=o_, in1=rin[:, 0:n], op=ALU.mult),
                     reads=[("xc", ph), "rin"], writes=[("xc", ph)])
                P.op("pool", lambda e, o_=o_, ph=ph: e.tensor_copy(out=xcb[:, ph, 0:n], in_=o_),
                     reads=[("xc", ph)], writes=[("xcb", ph)])
            def do_chunk(ci):
                c0 = ci * 64
                qs = lambda h, c0=c0: xc[:, h, c0:c0 + 64]
                ks = lambda h, c0=c0: xc[:, 4 + h, c0:c0 + 64]
                vs = lambda h, c0=c0: xc[:, 8 + h, c0:c0 + 64]
                rk = [("xc", ph) for ph in range(12)]
                qsb = lambda h, c0=c0: xcb[:, h, c0:c0 + 64]
                ksb = lambda h, c0=c0: xcb[:, 4 + h, c0:c0 + 64]
                rkb = [("xcb", ph) for ph in range(8)]
                out = {}
                gen = ch.run_gen(out, qs, ks, vs, rk, bgraw[:, ci, 0:4], bgraw[:, ci, 4:8], "bg",
                                 qsb=qsb, ksb=ksb, rkb=rkb)
                return gen, out, c0

            def do_serial(ci, T, K_, c0):
                psWS, pkWS = k.psum()
                for h in range(4):
                    P.op("pe", lambda e, h=h, T=T: e.matmul(psWS[0:64, h * 128:(h + 1) * 128],
                                                            lhsT=T["WT"][:, h * 64:(h + 1) * 64],
                                                            rhs=Sb[:, h * 128:(h + 1) * 128], start=True, stop=True),
                         reads=[K_("WT"), "Sb"], writes=[pkWS])
                P.op("dve", lambda e, T=T: e.tensor_tensor(out=vnew[:], in0=T["U"][:], in1=psWS[0:64, :], op=ALU.subtract),
                     reads=[K_("U"), pkWS], writes=["vnew"])
                P.op("pool", lambda e: e.tensor_copy(out=vnb[:], in_=vnew[:]), reads=["vnew"], writes=["vnb"])
                psO, pkO = k.psum()
                for h in range(4):
                    P.op("pe", lambda e, h=h, T=T: e.matmul(psO[:, h * 64:(h + 1) * 64], lhsT=Sb[:, h * 128:(h + 1) * 128],
                                                            rhs=T["qd"][:, h * 64:(h + 1) * 64], start=True, stop=False),
                         reads=[K_("qd"), "Sb"], writes=[pkO])
                    P.op("pe", lambda e, h=h, T=T: e.matmul(psO[:, h * 64:(h + 1) * 64], lhsT=vnb[:, h * 128:(h + 1) * 128],
                                                            rhs=T["QK"][:, h * 64:(h + 1) * 64], start=False, stop=True),
                         reads=[K_("QK"), "vnb"], writes=[pkO])
                P.op("act", lambda e, c0=c0: e.activation(out=oTg[:, :, c0:c0 + 64],
                                                          in_=psO[:, 0:256].rearrange("p (h c) -> p h c", h=4), func=AF.Copy),
                     reads=[pkO], writes=[("oTg", ci)])
                psS, pkS = k.psum()
                for h in range(4):
                    P.op("pe", lambda e, h=h, T=T: e.matmul(psS[:, h * 128:(h + 1) * 128], lhsT=T["kd"][:, h * 128:(h + 1) * 128],
                                                            rhs=vnb[:, h * 128:(h + 1) * 128], start=True, stop=True),
                         reads=[K_("kd"), "vnb"], writes=[pkS])
                P.op("dve", lambda e, T=T: e.tensor_tensor(out=tmpS[:].rearrange("p (h e) -> p h e", h=4),
                                                          in0=S4[:].rearrange("p (h e) -> p h e", h=4),
                                                          in1=bc(T["bx"][:, 8:12], 2, 128), op=ALU.mult),
                     reads=["S4", K_("bx")], writes=["tmpS"])
                P.op("dve", lambda e: e.tensor_tensor(out=S4[:], in0=tmpS[:], in1=psS[:, :], op=ALU.add),
                     reads=["tmpS", pkS], writes=["S4"])
                P.op("pool", lambda e: e.tensor_copy(out=Sb[:], in_=S4[:]), reads=["S4"], writes=["Sb"])
            for ci0 in range(0, nch, 3):
                cis = list(range(ci0, min(ci0 + 3, nch)))
                items = [do_chunk(ci) for ci in cis]
                active = [it[0] for it in items]
                while active:
                    for g_ in list(active):
                        try:
                            next(g_)
                        except StopIteration:
                            active.remove(g_)
                for ci, (g_, o_, c0_) in zip(cis, items):
                    do_serial(ci, o_["T"], o_["K_"], c0_)
            okeys = [("oTg", ci) for ci in range(nch)]
            for h in range(4):
                P.op("pool", lambda e, h=h: e.tensor_tensor(out=sq[:, 0:n], in0=oTg[:, h, 0:n], in1=oTg[:, h, 0:n], op=ALU.mult),
                     reads=okeys, writes=["sq"])
                ps, pk = k.psum()
                P.op("pe", lambda e, ps=ps: e.matmul(ps[:, 0:n], lhsT=c["ones"][:], rhs=sq[:, 0:n], start=True, stop=True),
                     reads=["sq", "ones"], writes=[pk])
                P.op("act", lambda e, ps=ps: e.activation(out=rin[:, 0:n], in_=ps[:, 0:n], func=AF.Ln, scale=1.0 / 128,
                                                         bias=c["eps"][:]), reads=[pk, "epst"], writes=["rin"])
                P.op("act", lambda e: e.activation(out=rin[:, 0:n], in_=rin[:, 0:n], func=AF.Exp, scale=-0.5),
                     reads=["rin"], writes=["rin"])
                P.op("act", lambda e, h=h: e.activation(out=zt[:, h, 0:n], in_=zt[:, h, 0:n], func=AF.Silu),
                     reads=[("zt", h)], writes=[("zt", h)])
                P.op("dve", lambda e, h=h: e.scalar_tensor_tensor(out=rin[:, 0:n], in0=rin[:, 0:n], scalar=dnw[:, 0:1],
                                                                 in1=zt[:, h, 0:n], op0=ALU.mult, op1=ALU.mult),
                     reads=["rin", "dnw", ("zt", h)], writes=["rin"])
                P.op("dve", lambda e, h=h: e.tensor_tensor(out=obt[:, h, 0:n], in0=oTg[:, h, 0:n], in1=rin[:, 0:n], op=ALU.mult),
                     reads=okeys + ["rin"], writes=[("obt", h)])
                t0 = u0 - 48
                lo = 48 if gi == 0 else 0
                P.dma("sp", CBv[h, :, t0 + lo:t0 + n], obt[:, h, lo:n], reads=[("obt", h)], writes=[("CB", h, gi)])
            if gi == ngroups - 1:
                cvo = k.sb(st, "cvo", [3, 1536], F32)
                for g3 in range(3):
                    ps, pk = k.psum()
                    for q in range(4):
                        ph = g3 * 4 + q
                        P.op("pe", lambda e, ps=ps, q=q, ph=ph: e.transpose(out=ps[0:3, q * 128:(q + 1) * 128],
                                                                           in_=raw[:, ph, 69:72], identity=c["ident"][:]),
                             reads=[("raw", ph), "ident"], writes=[pk])
                    P.op("act", lambda e, ps=ps, g3=g3: e.activation(out=cvo[:, g3 * 512:(g3 + 1) * 512], in_=ps[0:3, :],
                                                                    func=AF.Copy), reads=[pk], writes=[("cvo", g3)])
                P.dma("sp", io["conv_p"], cvo[:], reads=[("cvo", g3) for g3 in range(3)], writes=["conv_p"])
        for gi in range(ngroups):
            do_group(gi)
        P.dma("sp", io["delta_p"].rearrange("h d e -> d h e"), S4[:].rearrange("p (h e) -> p h e", h=4),
              reads=["S4"], writes=["delta_p"])
        P.barrier()


def stage_deltanet_sample(k, c, io, sc):
    P = k.P
    DNS = sc["DNS_T"]
    with ExitStack() as st:
        dc = dn_consts(k, st, io, 1)
        seqm = dc["tril4"][:, 0:16]
        cm = k.sb(st, "cm", [128, 16, 64], F32)
        P.dma("sp", cm[:], io["cmask"], writes=["cm"])
        cw = k.sb(st, "cws", [128, 48, 4], F32)
        P.dma("sp", cw[:], io["cwt_s"], writes=["cws"])
        negA = k.sb(st, "negA", [64, 16], F32)
        dtb = k.sb(st, "dtb", [64, 16], F32)
        P.dma("sp", negA[:], io["a_log_all"].partition_broadcast(64), writes=["negA"])
        P.dma("sp", dtb[:], io["dt_bias_all"].partition_broadcast(64), writes=["dtb"])
        P.op("act", lambda e: e.activation(out=negA[:], in_=negA[:], func=AF.Exp), reads=["negA"], writes=["negA"])
        P.op("dve", lambda e: e.tensor_scalar(out=negA[:], in0=negA[:], scalar1=-1.0, scalar2=None, op0=ALU.mult),
             reads=["negA"], writes=["negA"])
        dnw = k.sb(st, "dnw", [128, 1], F32)
        P.dma("sp", dnw[:], io["delta_norm"], writes=["dnw"])
        lnq = k.sb(st, "lnq", [128, 1], F32)
        P.op("pool", lambda e: e.memset(lnq[:], math.log(128.0 ** -0.5)), writes=["lnq"])
        P.dma("sp", io["conv_s"], sc["CVS"].rearrange("(s t) n -> s t n", t=4)[:, 1:4, :], writes=["conv_s"])
        hin = k.sb(st, "hin", [48, 6144], F32)
        P.dma("sp", hin[:], io["state_conv"].rearrange("s r n -> (s r) n"), writes=["hin"])
        xx = k.sb(st, "xx", [128, 48, 16, 8], F32)
        for g6 in range(6):
            ps, pk = k.psum()
            for q in range(8):
                ph = g6 * 8 + q
                P.op("pe", lambda e, ps=ps, q=q, ph=ph: e.transpose(out=ps[:, q * 48:(q + 1) * 48],
                                                                   in_=hin[:, ph * 128:(ph + 1) * 128],
                                                                   identity=c["ident"][0:48, 0:48]),
                     reads=["hin", "ident"], writes=[pk])
            P.op("act", lambda e, ps=ps, g6=g6: e.activation(
                out=xx[:, g6 * 8:(g6 + 1) * 8, :, 0:3],
                in_=ps[:, 0:384].rearrange("p (a s r) -> p a s r", a=8, s=16), func=AF.Copy),
                reads=[pk], writes=[("xxh", g6)])
        xin = k.sb(st, "xin", [128, 48, 64], F32)
        for q in range(6):
            P.dma("sp", xin[:, q * 8:(q + 1) * 8, :], DNS[q * 8:(q + 1) * 8, :, 0:64].rearrange("a p t -> p a t"),
                  writes=[("xin", q)])
        for g6 in range(6):
            P.op("pool", lambda e, g6=g6: e.tensor_copy(
                out=xx[:, g6 * 8:(g6 + 1) * 8, :, 3:7],
                in_=xin[:, g6 * 8:(g6 + 1) * 8, 0:64].rearrange("p a (s t) -> p a s t", t=4)),
                reads=[("xin", g6)], writes=[("xxn", g6)])
        xcs = k.sb(st, "xcs", [128, 48, 64], F32)
        for ph in range(48):
            o_ = xcs[:, ph, :].rearrange("p (s t) -> p s t", t=4)
            rd = [("xxh", ph // 8), ("xxn", ph // 8), "cws"]
            P.op("dve", lambda e, ph=ph, o_=o_: e.tensor_scalar(out=o_, in0=xx[:, ph, :, 0:4], scalar1=cw[:, ph, 0:1],
                                                             scalar2=None, op0=ALU.mult), reads=rd, writes=[("xcs", ph)])
            for j_ in range(1, 4):
                P.op("dve", lambda e, ph=ph, o_=o_, j_=j_: e.scalar_tensor_tensor(
                    out=o_, in0=xx[:, ph, :, j_:j_ + 4], scalar=cw[:, ph, j_:j_ + 1], in1=o_,
                    op0=ALU.mult, op1=ALU.add), reads=rd + [("xcs", ph)], writes=[("xcs", ph)])
        P.op("act", lambda e: e.activation(out=xcs[:], in_=xcs[:], func=AF.Silu),
             reads=[("xcs", ph) for ph in range(48)], writes=["xcsA"])
        sq = k.sb(st, "sq", [128, 512], F32)
        rin = k.sb(st, "rin", [128, 512], F32)
        for g8 in range(4):
            v_ = xcs[:, g8 * 8:(g8 + 1) * 8, :].rearrange("p a t -> p (a t)")

            def l2(g8=g8, v_=v_):
                P.op("pool", lambda e: e.tensor_tensor(out=sq[:], in0=v_, in1=v_, op=ALU.mult),
                     reads=["xcsA", ("xn", g8)], writes=["sq"])
                ps, pk = k.psum()
                P.op("pe", lambda e: e.matmul(ps[:, :], lhsT=c["ones"][:], rhs=sq[:], start=True, stop=True),
                     reads=["sq", "ones"], writes=[pk])
                P.op("act", lambda e: e.activation(out=rin[:], in_=ps[:, :], func=AF.Ln, bias=c["eps"][:]),
                     reads=[pk, "epst"], writes=["rin"])
                if g8 < 2:
                    P.op("act", lambda e: e.activation(out=rin[:], in_=rin[:], func=AF.Exp, scale=-0.5, bias=lnq[:]),
                         reads=["rin", "lnq"], writes=["rin"])
                else:
                    P.op("act", lambda e: e.activation(out=rin[:], in_=rin[:], func=AF.Exp, scale=-0.5),
                         reads=["rin"], writes=["rin"])
                P.op("dve", lambda e: e.tensor_tensor(out=v_, in0=v_, in1=rin[:], op=ALU.mult),
                     reads=["xcsA", "rin"], writes=[("xn", g8)])
            l2()
        xk = ["xcsA"] + [("xn", g8) for g8 in range(4)]
        bt = k.sb(st, "bt", [64, 1, 32], F32)
        P.dma("sp", bt[:, 0, :], sc["BAS"][0:64, :], writes=["bgs"])
        braw = bt[:, :, 0:16]
        araw = bt[:, :, 16:32]
        P.op("act", lambda e: e.activation(out=braw, in_=braw, func=AF.Sigmoid), reads=["bgs"], writes=["bgs"])
        P.op("dve", lambda e: e.tensor_tensor(out=araw, in0=araw, in1=bc(dtb[:], 1, 1), op=ALU.add),
             reads=["bgs", "dtb"], writes=["bgs"])
        P.op("act", lambda e: e.activation(out=araw, in_=araw, func=AF.Exp), reads=["bgs"], writes=["bgs"])
        P.op("act", lambda e: e.activation(out=araw, in_=araw, func=AF.Ln, bias=c["ones"][0:64, 0:1]),
             reads=["bgs", "ones"], writes=["bgs"])
        P.op("dve", lambda e: e.tensor_tensor(out=araw, in0=araw, in1=bc(negA[:], 1, 1), op=ALU.mult),
             reads=["bgs", "negA"], writes=["bgs"])
        ch = DnChunk(k, st, c, dc, nset=1)
        Sall = [k.sb(st, "Sall", [128, 4, 16, 128], F32) for _ in range(1)]
        WTm = k.sb(st, "WTm", [128, 16, 64], F32)
        qdm = k.sb(st, "qdm", [128, 16, 64], F32)
        kdm = k.sb(st, "kdm", [64, 16, 128], F32)
        vnew = k.sb(st, "vnew", [64, 512], F32)
        gm = k.sb(st, "gm", [64, 4, 16], F32)
        egl = k.sb(st, "egl", [128, 64], F32)
        oTs = k.sb(st, "oTs", [128, 4, 64], F32)
        zs = k.sb(st, "zs", [128, 4, 64], F32)
        obs = k.sb(st, "obs", [128, 4, 64], BF16)

        def do_hg(hg):
            Si = 0
            S = Sall[Si]
            for h in range(4):
                P.dma("sp", S[:, h], io["state_delta"][:, 4 * hg + h].rearrange("s d e -> d s e"),
                      writes=[("S", Si, h)])
            qs = lambda h: xcs[:, 4 * hg + h, :]
            ks = lambda h: xcs[:, 16 + 4 * hg + h, :]
            vs = lambda h: xcs[:, 32 + 4 * hg + h, :]
            beta = bt[:, 0, 4 * hg:4 * hg + 4]
            g = bt[:, 0, 16 + 4 * hg:16 + 4 * hg + 4]
            T, K_ = ch.run(qs, ks, vs, xk, beta, g, "bgs", levels=1)
            P.op("pool", lambda e: e.tensor_tensor(out=gm[:], in0=bc(g, 2, 16), in1=bc(seqm, 1, 4), op=ALU.mult),
                 reads=["bgs", "dconst"], writes=["gm"])
            psE, pkE = k.psum()
            P.op("pe", lambda e: e.matmul(psE[:, 0:64], lhsT=c["ones"][0:64, :], rhs=gm[:].rearrange("p h s -> p (h s)"),
                                          start=True, stop=True), reads=["gm", "ones"], writes=[pkE])
            P.op("act", lambda e: e.activation(out=egl[:], in_=psE[:, 0:64], func=AF.Exp), reads=[pkE], writes=["egl"])
            psWS, pkWS = k.psum()
            for h in range(4):
                P.op("pool", lambda e, h=h: e.tensor_tensor(out=WTm[:], in0=bc(T["WT"][:, h * 64:(h + 1) * 64], 1, 16),
                                                            in1=cm[:], op=ALU.mult),
                     reads=[K_("WT"), "cm"], writes=["WTm"])
                for s in range(16):
                    P.op("pe", lambda e, h=h, s=s: e.matmul(psWS[0:64, h * 128:(h + 1) * 128], lhsT=WTm[:, s, :],
                                                            rhs=S[:, h, s, :], start=(s == 0), stop=(s == 15)),
                         reads=["WTm", ("S", Si, h)], writes=[pkWS])
            P.op("dve", lambda e: e.tensor_tensor(out=vnew[:], in0=T["U"][:], in1=psWS[0:64, :], op=ALU.subtract),
                 reads=[K_("U"), pkWS], writes=["vnew"])
            psO, pkO = k.psum()
            for h in range(4):
                P.op("pool", lambda e, h=h: e.tensor_tensor(out=qdm[:], in0=bc(T["qd"][:, h * 64:(h + 1) * 64], 1, 16),
                                                            in1=cm[:], op=ALU.mult),
                     reads=[K_("qd"), "cm"], writes=["qdm"])
                for s in range(16):
                    P.op("pe", lambda e, h=h, s=s: e.matmul(psO[:, h * 64:(h + 1) * 64], lhsT=S[:, h, s, :],
                                                            rhs=qdm[:, s, :], start=(s == 0), stop=False),
                         reads=["qdm", ("S", Si, h)], writes=[pkO])
                P.op("pe", lambda e, h=h: e.matmul(psO[:, h * 64:(h + 1) * 64], lhsT=vnew[:, h * 128:(h + 1) * 128],
                                                   rhs=T["QK"][:, h * 64:(h + 1) * 64], start=False, stop=True),
                     reads=[K_("QK"), "vnew"], writes=[pkO])
            P.op("act", lambda e: e.activation(out=oTs[:], in_=psO[:, 0:256].rearrange("p (h c) -> p h c", h=4),
                                               func=AF.Copy), reads=[pkO], writes=["oTs"])
            for h in range(4):
                P.op("pool", lambda e, h=h: e.tensor_tensor(out=kdm[:], in0=bc(T["kd"][:, h * 128:(h + 1) * 128], 1, 16),
                                                            in1=bc(seqm, 2, 128), op=ALU.mult),
                     reads=[K_("kd"), "dconst"], writes=["kdm"])
                for s4 in range(4):
                    psS, pkS = k.psum()
                    for q in range(4):
                        s = s4 * 4 + q
                        P.op("pe", lambda e, h=h, s=s, q=q, psS=psS: e.matmul(
                            psS[:, q * 128:(q + 1) * 128], lhsT=kdm[:, s, :], rhs=vnew[:, h * 128:(h + 1) * 128],
                            start=True, stop=True), reads=["kdm", "vnew"], writes=[pkS])
                    for q in range(4):
                        s = s4 * 4 + q
                        P.op("dve", lambda e, h=h, s=s, q=q, psS=psS: e.scalar_tensor_tensor(
                            out=S[:, h, s, :], in0=S[:, h, s, :], scalar=egl[:, h * 16 + s:h * 16 + s + 1],
                            in1=psS[:, q * 128:(q + 1) * 128], op0=ALU.mult, op1=ALU.add),
                            reads=[("S", Si, h), "egl", pkS], writes=[("S", Si, h)])
                P.dma("sp", io["delta_s"][:, 4 * hg + h].rearrange("s d e -> d s e"), S[:, h],
                      reads=[("S", Si, h)], writes=[("dso", hg, h)])
            P.dma("sp", zs[:], DNS[48 + 4 * hg:52 + 4 * hg, :, 0:64].rearrange("a p t -> p a t"), writes=["zs"])
            P.op("act", lambda e: e.activation(out=zs[:], in_=zs[:], func=AF.Silu), reads=["zs"], writes=["zs"])
            of = oTs[:].rearrange("p h c -> p (h c)")
            P.op("pool", lambda e: e.tensor_tensor(out=sq[:, 0:256], in0=of, in1=of, op=ALU.mult),
                 reads=["oTs"], writes=["sq"])
            ps, pk = k.psum()
            P.op("pe", lambda e: e.matmul(ps[:, 0:256], lhsT=c["ones"][:], rhs=sq[:, 0:256], start=True, stop=True),
                 reads=["sq", "ones"], writes=[pk])
            P.op("act", lambda e: e.activation(out=rin[:, 0:256], in_=ps[:, 0:256], func=AF.Ln, scale=1.0 / 128,
                                               bias=c["eps"][:]), reads=[pk, "epst"], writes=["rin"])
            P.op("act", lambda e: e.activation(out=rin[:, 0:256], in_=rin[:, 0:256], func=AF.Exp, scale=-0.5),
                 reads=["rin"], writes=["rin"])
            P.op("dve", lambda e: e.scalar_tensor_tensor(out=rin[:, 0:256], in0=rin[:, 0:256], scalar=dnw[:, 0:1],
                                                         in1=zs[:].rearrange("p h c -> p (h c)"), op0=ALU.mult,
                                                         op1=ALU.mult), reads=["rin", "dnw", "zs"], writes=["rin"])
            P.op("dve", lambda e: e.tensor_tensor(out=obs[:].rearrange("p h c -> p (h c)"), in0=of, in1=rin[:, 0:256],
                                                  op=ALU.mult), reads=["oTs", "rin"], writes=["obs"])
            P.dma("sp", sc["OBS_T"][4 * hg:4 * hg + 4].rearrange("a p t -> p a t"), obs[:], reads=["obs"],
                  writes=[("OBS", hg)])
        for hg in range(4):
            do_hg(hg)
        P.barrier()


def stage_exchange(k, c, io, sc):
    P = k.P
    cb2 = sc["CB"].rearrange("h e q t -> (h e q) t")
    for i in range(0 if os.environ.get("KNOCC") else 5):
        P.op("pool", lambda e, i=i: e.collective_compute(
            "AllGather", ALU.bypass, replica_groups=[[0, 1, 2, 3], [4, 5, 6, 7]],
            ins=[cb2[i * 512:(i + 1) * 512, :]], outs=[sc["GB"][i * 2048:(i + 1) * 2048, :]]), kind="cc")


def stage_branches(k, c, io, sc):
    P = k.P
    with ExitStack() as st:
        Xa = k.sb(st, "Xa", [64, 32, NQ], BF16)
        Xb = k.sb(st, "Xb", [128, 16, NQ], BF16)
        for blk in range(9):
            P.dma("sp", Xa[:, :, blk * 128:(blk + 1) * 128], sc["OA_T"][blk].rearrange("p (h t) -> p h t", h=32),
                  writes=[("Xa", blk)])
        idx = k.sb(st, "idx", [128, 32], I32)
        P.dma("sp", idx[:], io["gidx"], writes=["idx"])
        P.op("pool", lambda e: e.memset(Xb[:, :, 1088:NQ], 0.0), writes=["Xbpad"])
        P.dma("sp", Xb[:, :, 1024:1088], sc["OBS_T"].rearrange("h p t -> p h t"), writes=["Xbs"])
        gst = [k.sb(st, "gst", [128, 1024], BF16) for _ in range(2)]
        for H in range(16):
            P.op("pool", lambda e, H=H: e.indirect_dma_start(
                out=Xb[:, H, 0:1024], out_offset=None, in_=sc["GB"],
                in_offset=bass.IndirectOffsetOnAxis(ap=idx[:, H:H + 1], axis=0)),
                reads=["idx"], writes=[("Xb", H)], kind="d")
            gi = H % 2
            P.op("pool", lambda e, H=H, gi=gi: e.indirect_dma_start(
                out=gst[gi][:], out_offset=None, in_=sc["GB"],
                in_offset=bass.IndirectOffsetOnAxis(ap=idx[:, 16 + H:17 + H], axis=0)),
                reads=["idx"], writes=[("gst", gi)], kind="d")
            P.op("dve", lambda e, H=H, gi=gi: e.tensor_copy(out=Xb[:, H, 1088:1104], in_=gst[gi][:, 0:16]),
                 reads=[("gst", gi), "Xbpad"], writes=[("Xbt", H)])
        xbk = ["Xbpad", "Xbs"] + [("Xb", H) for H in range(16)] + [("Xbt", H) for H in range(16)]
        xak = [("Xa", blk) for blk in range(9)]
        Wa = [k.sb(st, "Wa", [64, 32, 256], BF16) for _ in range(2)]
        Wb = [k.sb(st, "Wb", [128, 16, 256], BF16) for _ in range(2)]
        sg = [k.sb(st, "sg", [128, 2, 512], BF16) for _ in range(2)]
        tt = [k.sb(st, "tt", [128, 512], F32) for _ in range(2)]
        uu = [k.sb(st, "uu", [128, 512], F32) for _ in range(2)]
        ms = [k.sb(st, "ms", [128, 512], BF16) for _ in range(2)]
        rot = Rot(2)
        wav = io["w_branch_a"].rearrange("(h d) n -> d h n", d=64)
        wbv = io["w_branch_b"].rearrange("(h d) n -> d h n", d=128)
        subs = [(0, 512), (512, 512), (1024, 128)]
        for nt in range(16):
            wi = nt % 2
            for q in range(4):
                P.dma("pool", Wa[wi][:, q * 8:(q + 1) * 8, :], wav[:, q * 8:(q + 1) * 8, nt * 256:(nt + 1) * 256],
                      writes=[("Wa", wi, q)])
            for q in range(2):
                P.dma("pool", Wb[wi][:, q * 8:(q + 1) * 8, :], wbv[:, q * 8:(q + 1) * 8, nt * 256:(nt + 1) * 256],
                      writes=[("Wb", wi, q)])
            for ci in range(2):
                chn = nt * 2 + ci
                for (t0, n) in subs:
                    def unit(wi=wi, ci=ci, chn=chn, t0=t0, n=n):
                        psA, pkA = k.psum()
                        psB, pkB = k.psum()
                        for h in range(32):
                            P.op("pe", lambda e, h=h: e.matmul(psA[:, 0:n], lhsT=Wa[wi][:, h, ci * 128:(ci + 1) * 128],
                                                               rhs=Xa[:, h, t0:t0 + n], start=(h == 0), stop=(h == 31)),
                                 reads=xak + [("Wa", wi, h // 8)], writes=[pkA])
                        for h in range(16):
                            P.op("pe", lambda e, h=h: e.matmul(psB[:, 0:n], lhsT=Wb[wi][:, h, ci * 128:(ci + 1) * 128],
                                                               rhs=Xb[:, h, t0:t0 + n], start=(h == 0), stop=(h == 15)),
                                 reads=xbk + [("Wb", wi, h // 8)], writes=[pkB])
                        ri = rot.nxt()
                        P.dma("sp", sg[ri][:, 0, 0:n], sc["SG_T"][chn, :, t0:t0 + n], writes=[("sg", ri, 0)])
                        P.dma("sp", sg[ri][:, 1, 0:n], sc["SG_T"][32 + chn, :, t0:t0 + n], writes=[("sg", ri, 1)])
                        P.op("dve", lambda e: e.tensor_tensor(out=tt[ri][:, 0:n], in0=psA[:, 0:n], in1=sg[ri][:, 0, 0:n],
                                                              op=ALU.mult), reads=[pkA, ("sg", ri, 0)], writes=[("tt", ri)])
                        P.op("dve", lambda e: e.tensor_tensor(out=uu[ri][:, 0:n], in0=psB[:, 0:n], in1=sg[ri][:, 1, 0:n],
                                                              op=ALU.mult), reads=[pkB, ("sg", ri, 1)], writes=[("uu", ri)])
                        P.op("pool", lambda e: e.tensor_tensor(out=ms[ri][:, 0:n], in0=tt[ri][:, 0:n], in1=uu[ri][:, 0:n],
                                                               op=ALU.add), reads=[("tt", ri), ("uu", ri)], writes=[("ms", ri)])
                        P.dma("sp", sc["MG_T"][chn, :, t0:t0 + n], ms[ri][:, 0:n], reads=[("ms", ri)],
                              writes=[("MG", chn, t0)])
                    unit()
        P.barrier()


def stage_gemm_tm(k, c, xsrc, KC, wsrc, dst, tag):
    P = k.P
    ng = KC // 16
    with ExitStack() as st:
        X = [k.sb(st, "X", [128, 16, NQ], BF16) for _ in range(2)]
        W = [k.sb(st, "W", [128, 16, 512], BF16) for _ in range(2)]
        acc = k.sb(st, "acc", [128, 9, 512], F32)
        xrot, wrot = Rot(2), Rot(2)
        for nt in range(8):
            for g in range(ng):
                def unit(nt=nt, g=g):
                    xi, wi = xrot.nxt(), wrot.nxt()
                    if ng > 1 or nt == 0:
                        for q in range(2):
                            P.dma("sp", X[xi][:, q * 8:(q + 1) * 8, :],
                                  xsrc[g * 16 + q * 8:g * 16 + (q + 1) * 8].rearrange("a p t -> p a t"),
                                  writes=[(tag, "X", xi, q)])
                    wv = wsrc[g * 2048:(g + 1) * 2048, nt * 512:(nt + 1) * 512].rearrange("(a p) n -> p a n", p=128)
                    for q in range(2):
                        P.dma("pool", W[wi][:, q * 8:(q + 1) * 8, :], wv[:, q * 8:(q + 1) * 8, :],
                              writes=[(tag, "W", wi, q)])
                    for tb in range(9):
                        ps, pk = k.psum()
                        for kc in range(16):
                            P.op("pe", lambda e, ps=ps, kc=kc, tb=tb: e.matmul(
                                ps[:, :], lhsT=X[xi][:, kc, tb * 128:(tb + 1) * 128], rhs=W[wi][:, kc, :],
                                start=(kc == 0), stop=(kc == 15)),
                                reads=[(tag, "X", xi, kc // 8), (tag, "W", wi, kc // 8)], writes=[pk])
                        if g == 0:
                            P.op("act", lambda e, ps=ps, tb=tb: e.activation(out=acc[:, tb, :], in_=ps[:, :], func=AF.Copy),
                                 reads=[pk], writes=[(tag, "acc", tb)])
                        else:
                            P.op("dve", lambda e, ps=ps, tb=tb: e.tensor_tensor(out=acc[:, tb, :], in0=ps[:, :],
                                                                             in1=acc[:, tb, :], op=ALU.add),
                                 reads=[pk, (tag, "acc", tb)], writes=[(tag, "acc", tb)])
                        if g == ng - 1:
                            P.dma("sp", dst[tb * 128:(tb + 1) * 128, nt * 512:(nt + 1) * 512], acc[:, tb, :],
                                  reads=[(tag, "acc", tb)], writes=[(tag, "dst", tb, nt)])
                unit()
        P.barrier()


def stage_gemm_tm_k32(k, c, xsrc, wsrc, dst, tag):
    P = k.P
    with ExitStack() as st:
        X = k.sb(st, "X", [128, 32, NQ], BF16)
        W = [k.sb(st, "W", [128, 32, 512], BF16) for _ in range(2)]
        stg = [k.sb(st, "stg", [128, 512], F32) for _ in range(3)]
        srot = Rot(3)
        for q in range(4):
            P.dma("sp", X[:, q * 8:(q + 1) * 8, :], xsrc[q * 8:(q + 1) * 8].rearrange("a p t -> p a t"),
                  writes=[(tag, "X", q)])
        for nt in range(8):
            wi = nt % 2
            wv = wsrc[:, nt * 512:(nt + 1) * 512].rearrange("(a p) n -> p a n", p=128)
            for q in range(4):
                P.dma("pool", W[wi][:, q * 8:(q + 1) * 8, :], wv[:, q * 8:(q + 1) * 8, :], writes=[(tag, "W", wi, q)])
            for tb in range(9):
                ps, pk = k.psum()
                for kc in range(32):
                    P.op("pe", lambda e, ps=ps, kc=kc, tb=tb, wi=wi: e.matmul(
                        ps[:, :], lhsT=X[:, kc, tb * 128:(tb + 1) * 128], rhs=W[wi][:, kc, :],
                        start=(kc == 0), stop=(kc == 31)),
                        reads=[(tag, "X", kc // 8), (tag, "W", wi, kc // 8)], writes=[pk])
                si = srot.nxt()
                P.op("act", lambda e, ps=ps, si=si: e.activation(out=stg[si][:], in_=ps[:, :], func=AF.Copy),
                     reads=[pk], writes=[(tag, "stg", si)])
                P.dma("sp", dst[tb * 128:(tb + 1) * 128, nt * 512:(nt + 1) * 512], stg[si][:],
                      reads=[(tag, "stg", si)], writes=[(tag, "dst", tb, nt)])
        P.barrier()


def stage_postnorm(k, c, ypre, wpost, resid, hout, wnext, xt_next, tag):
    P = k.P
    with ExitStack() as st:
        wp = k.sb(st, "wp", [128, D], F32)
        P.dma("sp", wp[:], wpost.partition_broadcast(128), writes=[tag + "wp"])
        if wnext is not None:
            wn = k.sb(st, "wn", [128, D], F32)
            P.dma("sp", wn[:], wnext.partition_broadcast(128), writes=[tag + "wn"])
            hTs = [k.sb(st, "hT", [128, 32, 128], BF16) for _ in range(2)]
            hn = k.sb(st, "hn", [128, D], F32)
        yts = [k.sb(st, "yt", [128, D], F32) for _ in range(2)]
        rts = [k.sb(st, "rt", [128, D], F32) for _ in range(2)]
        junk = k.sb(st, "junk", [128, D], BF16)
        ss = [k.sb(st, "ss", [128, 2], F32) for _ in range(2)]
        for blk in range(9):
            def unit(blk=blk):
                i = blk % 2
                yt, rt = yts[i], rts[i]
                P.dma("sp", yt[:], ypre[blk * 128:(blk + 1) * 128, :], writes=[(tag, "yt", i)])
                P.dma("sp", rt[:], resid[blk * 128:(blk + 1) * 128, :], writes=[(tag, "rt", i)])
                P.op("act", lambda e: e.activation(out=junk[:], in_=yt[:], func=AF.Square, accum_out=ss[i][:, 0:1]),
                     reads=[(tag, "yt", i)], writes=[(tag, "junk"), (tag, "ss", i)])
                P.op("act", lambda e: e.activation(out=ss[i][:, 0:1], in_=ss[i][:, 0:1], func=AF.Sqrt, scale=1.0 / D,
                                                   bias=c["eps"][:]), reads=[(tag, "ss", i), "epst"], writes=[(tag, "ss", i)])
                P.op("dve", lambda e: e.reciprocal(out=ss[i][:, 0:1], in_=ss[i][:, 0:1]),
                     reads=[(tag, "ss", i)], writes=[(tag, "ss", i)])
                P.op("dve", lambda e: e.scalar_tensor_tensor(out=yt[:], in0=yt[:], scalar=ss[i][:, 0:1], in1=wp[:],
                                                             op0=ALU.mult, op1=ALU.mult),
                     reads=[(tag, "yt", i), (tag, "ss", i), tag + "wp"], writes=[(tag, "yt", i)])
                P.op("pool", lambda e: e.tensor_tensor(out=rt[:], in0=rt[:], in1=yt[:], op=ALU.add),
                     reads=[(tag, "yt", i), (tag, "rt", i)], writes=[(tag, "rt", i)])
                P.dma("sp", hout[blk * 128:(blk + 1) * 128, :], rt[:], reads=[(tag, "rt", i)], writes=[(tag, "ho", blk)])
                if wnext is None:
                    return
                hT = hTs[i]
                P.op("act", lambda e: e.activation(out=junk[:], in_=rt[:], func=AF.Square, accum_out=ss[i][:, 1:2]),
                     reads=[(tag, "rt", i)], writes=[(tag, "junk"), (tag, "ss2", i)])
                P.op("act", lambda e: e.activation(out=ss[i][:, 1:2], in_=ss[i][:, 1:2], func=AF.Sqrt, scale=1.0 / D,
                                                   bias=c["eps"][:]), reads=[(tag, "ss2", i), "epst"], writes=[(tag, "ss2", i)])
                P.op("dve", lambda e: e.reciprocal(out=ss[i][:, 1:2], in_=ss[i][:, 1:2]),
                     reads=[(tag, "ss2", i)], writes=[(tag, "ss2", i)])
                P.op("dve", lambda e: e.scalar_tensor_tensor(out=hn[:], in0=rt[:], scalar=ss[i][:, 1:2], in1=wn[:],
                                                             op0=ALU.mult, op1=ALU.mult),
                     reads=[(tag, "rt", i), (tag, "ss2", i), tag + "wn"], writes=[(tag, "hn")])
                for g in range(8):
                    ps, pk = k.psum()
                    for q in range(4):
                        kc = g * 4 + q
                        P.op("pe", lambda e, ps=ps, q=q, kc=kc: e.transpose(
                            out=ps[:, q * 128:(q + 1) * 128], in_=hn[:, kc * 128:(kc + 1) * 128],
                            identity=c["ident"][:]), reads=[(tag, "hn"), "ident"], writes=[pk])
                    if g % 2 == 0:
                        P.op("act", lambda e, ps=ps, g=g: e.activation(
                            out=hT[:, g * 4:(g + 1) * 4, :], in_=ps[:].rearrange("p (a t) -> p a t", a=4), func=AF.Copy),
                            reads=[pk], writes=[(tag, "hT", i, g)])
                    else:
                        P.op("dve", lambda e, ps=ps, g=g: e.tensor_copy(
                            out=hT[:, g * 4:(g + 1) * 4, :], in_=ps[:].rearrange("p (a t) -> p a t", a=4)),
                            reads=[pk], writes=[(tag, "hT", i, g)])
                P.dma("sp", xt_next[blk], hT[:].rearrange("p a t -> p (a t)"),
                      reads=[(tag, "hT", i, g) for g in range(8)], writes=[(tag, "xt", blk)])
            unit()
        P.barrier()


def stage_mlp_up(k, c, io, sc):
    P = k.P
    with ExitStack() as st:
        g = Gemm(k, st, NQ)
        g.tag = "up"
        g.load_x(sc["XT2"], list(range(9)))
        stg = [k.sb(st, "stg", [128, 512], F32) for _ in range(3)]
        stb = [k.sb(st, "stb", [128, 512], BF16) for _ in range(3)]
        srot = Rot(3)
        subs = [(0, 512), (512, 512), (1024, 128)]
        for nt in range(32):
            def epi(ci, si_, t0, n, ps, pk, nt=nt):
                si = srot.nxt()
                P.op("act", lambda e: e.activation(out=stg[si][:, 0:n], in_=ps[:, 0:n], func=AF.Relu),
                     reads=[pk], writes=[("ustg", si)])
                P.op("pool", lambda e: e.tensor_tensor(out=stb[si][:, 0:n], in0=stg[si][:, 0:n], in1=stg[si][:, 0:n],
                                                       op=ALU.mult), reads=[("ustg", si)], writes=[("ustb", si)])
                ch = nt * 4 + ci
                P.dma("sp", sc["U_T"][ch, :, t0:t0 + n], stb[si][:, 0:n], reads=[("ustb", si)], writes=[("UT", ch, t0)])
            g.fm(io["w_up"][:, nt * 512:(nt + 1) * 512], 512, subs, epi)
        P.barrier()

def build(stop_after=None):
    k = K()
    io = {}
    io["x_tok"] = k.inp("x_tok", [NTOK, D])
    io["x_seq"] = k.inp("x_seq", [NSEQ, D])
    io["w_in"] = k.inp("w_in", [D, IN_DIM])
    io["w_dn"] = k.inp("w_dn", [D, 2048])
    io["w_ba"] = k.inp("w_ba", [D, 8])
    io["norm_mix_pre"] = k.inp("norm_mix_pre", [1, D])
    io["cs_tab"] = k.inp("cs_tab", [NTOK, 256])
    io["masks"] = k.inp("masks", [6, 128, 512])
    io["sinks"] = k.inp("sinks", [1, 32])
    io["cache_k"] = k.inp("cache_k", [16, 128, 512])
    io["cache_v"] = k.inp("cache_v", [16, 128, 512])
    io["dconst"] = k.inp("dconst", [2, 7, 128, 256])
    io["cwt"] = k.inp("cwt", [128, 12, 4])
    io["a_log"] = k.inp("a_log", [1, 4])
    io["dt_bias"] = k.inp("dt_bias", [1, 4])
    io["delta_norm"] = k.inp("delta_norm", [128, 1])
    io["gidx"] = k.inp("gidx", [128, 32], I32)
    io["w_branch_a"] = k.inp("w_branch_a", [2048, D])
    io["w_branch_b"] = k.inp("w_branch_b", [2048, D])
    io["w_out"] = k.inp("w_out", [D, D])
    io["w_up"] = k.inp("w_up", [D, 4 * D])
    io["w_down"] = k.inp("w_down", [4 * D, D])
    io["norm_mix_post"] = k.inp("norm_mix_post", [1, D])
    io["norm_mlp_pre"] = k.inp("norm_mlp_pre", [1, D])
    io["norm_mlp_post"] = k.inp("norm_mlp_post", [1, D])
    io["y_tok"] = k.out("y_tok", [NQ, D])
    io["cmask"] = k.inp("cmask", [128, 16, 64])
    io["cwt_s"] = k.inp("cwt_s", [128, 48, 4])
    io["a_log_all"] = k.inp("a_log_all", [1, 16])
    io["dt_bias_all"] = k.inp("dt_bias_all", [1, 16])
    io["state_conv"] = k.inp("state_conv", [16, 3, 6144])
    io["state_delta"] = k.inp("state_delta", [16, 16, 128, 128])
    io["conv_s"] = k.out("conv_s", [16, 3, 6144])
    io["delta_s"] = k.out("delta_s", [16, 16, 128, 128])
    io["conv_p"] = k.out("conv_p", [3, 1536])
    io["delta_p"] = k.out("delta_p", [4, 128, 128])
    io["wink_s"] = k.out("wink_s", [16, 128, 512])
    io["winv_s"] = k.out("winv_s", [16, 128, 512])
    io["wink_p"] = k.out("wink_p", [128, 512])
    io["winv_p"] = k.out("winv_p", [128, 512])
    sc = {}
    sc["XT_tok"] = k.scratch("XT_tok", [10, 128, D], BF16)
    sc["XT_seq"] = k.scratch("XT_seq", [33, 128, D], BF16)
    sc["QKV"] = k.scratch("QKV", [NTOK, 3072], F32)
    sc["SG_T"] = k.scratch("SG_T", [64, 128, NQ], BF16)
    sc["DNS_T"] = k.scratch("DNS_T", [64, 128, 128], F32)
    sc["CVS"] = k.scratch("CVS", [64, 6144], F32)
    sc["BAS"] = k.scratch("BAS", [128, 32], F32)
    sc["DN_T"] = k.scratch("DN_T", [16, 128, NSEQ], F32)
    sc["BA"] = k.scratch("BA", [NSEQ, 8], F32)
    sc["OA_T"] = k.scratch("OA_T", [9, 64, 4096], BF16)
    sc["CB"] = k.scratch("CB", [4, 128, 5, 1024], BF16)
    sc["OBS_T"] = k.scratch("OBS_T", [16, 128, 64], BF16)
    sc["GB"] = k.scratch("GB", [4 * 2560, 1024], BF16)
    sc["MG_T"] = k.scratch("MG_T", [32, 128, NQ], BF16)
    sc["YP1"] = k.scratch("YP1", [NQ, D], F32)
    sc["H1"] = k.scratch("H1", [NQ, D], F32)
    sc["XT2"] = k.scratch("XT2", [9, 128, D], BF16)
    sc["U_T"] = k.scratch("U_T", [128, 128, NQ], BF16)
    sc["YP2"] = k.scratch("YP2", [NQ, D], F32)
    with ExitStack() as cst:
        c = build_consts(k, cst)
        stages = [
            lambda: stage_norm_T(k, c, io["x_tok"], 10, io["norm_mix_pre"], sc["XT_tok"], "n1"),
            lambda: stage_norm_T(k, c, io["x_seq"], 33, io["norm_mix_pre"], sc["XT_seq"], "n2"),
            lambda: stage_inproj_tok(k, c, io, sc),
            lambda: stage_inproj_seq(k, c, io, sc),
            lambda: stage_deltanet_prompt(k, c, io, sc),
            lambda: stage_deltanet_sample(k, c, io, sc),
            lambda: stage_exchange(k, c, io, sc),
            lambda: stage_attention(k, c, io, sc),
            lambda: stage_branches(k, c, io, sc),
            lambda: stage_gemm_tm_k32(k, c, sc["MG_T"], io["w_out"], sc["YP1"], "wo"),
            lambda: stage_postnorm(k, c, sc["YP1"], io["norm_mix_post"], io["x_tok"][128:NTOK, :], sc["H1"],
                                   io["norm_mlp_pre"], sc["XT2"], "pn1"),
            lambda: stage_mlp_up(k, c, io, sc),
            lambda: stage_gemm_tm(k, c, sc["U_T"], 128, io["w_down"], sc["YP2"], "wd"),
            lambda: stage_postnorm(k, c, sc["YP2"], io["norm_mlp_post"], sc["H1"], io["y_tok"], None, None, "pn2"),
        ]
        sel = os.environ.get("KSTAGES")
        sel = [int(x) for x in sel.split(",")] if sel else None
        for si, s in enumerate(stages):
            if stop_after is not None and si >= stop_after:
                break
            if sel is not None and si not in sel:
                continue
            s()
        k.P.emit()
    return k


def rope_tab(pos):
    half = 8
    inv_freq = (np.float32(500000.0) ** (-np.arange(half, dtype=np.float32) * np.float32(2.0 / 16))).astype(np.float32)
    ang = pos.astype(np.float32)[:, None] * inv_freq[None, :]
    cos = np.cos(ang).astype(np.float32)
    sin = np.sin(ang).astype(np.float32)
    n = pos.shape[0]
    tab = np.zeros((n, 256), np.float32)
    c16 = np.concatenate([cos, cos], axis=1)
    tab[:, 0:128] = np.tile(c16, (1, 8))
    tab[:, 128:192] = np.tile(-sin, (1, 8))
    tab[:, 192:256] = np.tile(sin, (1, 8))
    return tab


def make_dconst():
    out = np.zeros((2, 7, 128, 256), np.float32)
    a = np.arange(64)
    for v in range(2):
        same = np.ones((64, 64), bool) if v == 0 else (a[:, None] // 4 == a[None, :] // 4)
        U = ((a[:, None] <= a[None, :]) & same).astype(np.float32)
        Ys = ((a[:, None] > a[None, :]) & same).astype(np.float32)
        out[v, 0, :64, :64] = U
        out[v, 1, :64, :64] = Ys
        out[v, 2, :64, :] = np.tile(Ys, (1, 4))
        out[v, 3, :64, :] = np.tile(U, (1, 4))
        out[v, 4, :64, :] = np.tile(np.eye(64, dtype=np.float32), (1, 4))
        out[v, 5, :64, :128] = 1.0
        out[v, 6, :64, :] = np.tile(((a[:, None] >= a[None, :]) & same).astype(np.float32), (1, 4))
    out[1, 6] = 0.0
    out[1, 6, :64, :16] = (a[:, None] // 4 == np.arange(16)[None, :]).astype(np.float32)
    return out


def make_cwt(conv_w, j):
    t = np.zeros((128, 12, 4), np.float32)
    for part in range(3):
        for hl in range(4):
            c0 = part * 2048 + (4 * j + hl) * 128
            t[:, part * 4 + hl, :] = conv_w[:, c0:c0 + 128].T
    return t


def make_gidx(j):
    e = np.arange(128)[:, None]
    H = np.arange(16)[None, :]
    rank, hl = H // 4, H % 4

    def row(q):
        r = (hl * 128 + e) * 5 + q
        return (r // 512) * 2048 + rank * 512 + (r % 512)
    return np.concatenate([row(j), row(4)], axis=1).astype(np.int32)


def make_masks(j):
    kk = np.arange(128)[:, None]
    qq = np.arange(128)[None, :]
    m = np.zeros((6, 128, 512), np.float32)
    mc = (kk <= qq).astype(np.float32)
    mp = (kk >= qq).astype(np.float32)
    mcm = (((kk < 64) & (qq < 64) & (kk // 4 == qq // 4) & (kk <= qq)) |
           ((kk >= 64) & (kk <= qq) & (qq < 80))).astype(np.float32)
    mpt = ((qq >= 64) & (qq < 80) & (kk >= qq - 64)).astype(np.float32)
    m[0] = np.tile(mc, (1, 4))
    m[1] = np.tile(mp, (1, 4))
    m[2] = np.tile(mp, (1, 4)) if j > 0 else 0.0
    m[3] = np.tile(mcm, (1, 4))
    m[4] = np.tile(mpt, (1, 4))
    t = np.arange(128)[None, :] % 4
    m[5, :, 0:128] = (kk >= t).astype(np.float32)
    return m


def prep_core(inp, c):
    b, j = c // 4, c % 4
    f32 = np.float32
    hp = np.concatenate([inp["meta_tokens"].astype(f32), inp["x_prompt"][b].astype(f32)], axis=0)
    x_tok = np.zeros((NTOK, D), f32)
    pos = np.zeros((NTOK,), np.int64)
    if j > 0:
        x_tok[0:128] = hp[1024 * j - 128:1024 * j]
        pos[0:128] = np.arange(1024 * j - 128, 1024 * j)
    x_tok[128:1152] = hp[1024 * j:1024 * j + 1024]
    pos[128:1152] = np.arange(1024 * j, 1024 * j + 1024)
    x_tok[1152:1216] = inp["x_sample"][16 * c:16 * c + 16].reshape(64, D)
    pos[1152:1216] = 8192 + (np.arange(64) % 4)
    x_tok[1216:1232] = hp[4096:4112]
    pos[1216:1232] = np.arange(4096, 4112)
    x_seq = np.zeros((NSEQ, D), f32)
    x_seq[48:48 + 4112] = hp
    w_in = inp["w_in"][0]
    sl = lambda base: w_in[:, base + 512 * j: base + 512 * j + 512]
    w_dn = np.ascontiguousarray(np.concatenate([sl(3072), sl(5120), sl(7168), sl(9216)], axis=1))
    w_ba = np.ascontiguousarray(np.concatenate([w_in[:, 11264 + 4 * j:11268 + 4 * j],
                                                w_in[:, 11280 + 4 * j:11284 + 4 * j]], axis=1))
    m = {
        "x_tok": x_tok, "x_seq": x_seq, "w_in": np.ascontiguousarray(w_in), "w_dn": w_dn, "w_ba": w_ba,
        "norm_mix_pre": np.ascontiguousarray(inp["norm_mix_pre"][0:1]),
        "cs_tab": rope_tab(pos),
        "masks": make_masks(j),
        "gidx": make_gidx(j),
        "w_branch_a": np.ascontiguousarray(inp["w_branch_a"][0]), "w_branch_b": np.ascontiguousarray(inp["w_branch_b"][0]),
        "w_out": np.ascontiguousarray(inp["w_out"][0]), "w_up": np.ascontiguousarray(inp["w_up"][0]),
        "w_down": np.ascontiguousarray(inp["w_down"][0]),
        "norm_mix_post": np.ascontiguousarray(inp["norm_mix_post"][0:1]),
        "norm_mlp_pre": np.ascontiguousarray(inp["norm_mlp_pre"][0:1]),
        "norm_mlp_post": np.ascontiguousarray(inp["norm_mlp_post"][0:1]),
        "dconst": make_dconst(),
        "cmask": np.ascontiguousarray(np.broadcast_to(
            (np.arange(64)[None, :] // 4 == np.arange(16)[:, None]).astype(f32)[None], (128, 16, 64))),
        "cwt_s": np.ascontiguousarray(inp["conv_w"][0].reshape(4, 48, 128).transpose(2, 1, 0)).astype(f32),
        "a_log_all": np.ascontiguousarray(inp["a_log"][0:1]).astype(f32),
        "dt_bias_all": np.ascontiguousarray(inp["dt_bias"][0:1]).astype(f32),
        "state_conv": np.ascontiguousarray(inp["state_conv"][0, 16 * c:16 * c + 16]),
        "state_delta": np.ascontiguousarray(inp["state_delta"][0, 16 * c:16 * c + 16]),
        "cwt": make_cwt(inp["conv_w"][0], j),
        "a_log": np.ascontiguousarray(inp["a_log"][0:1, 4 * j:4 * j + 4]).astype(f32),
        "dt_bias": np.ascontiguousarray(inp["dt_bias"][0:1, 4 * j:4 * j + 4]).astype(f32),
        "delta_norm": np.ascontiguousarray(inp["delta_norm"][0].reshape(128, 1)).astype(f32),
        "sinks": np.ascontiguousarray(inp["sinks"][0:1]).astype(f32),
        "cache_k": np.ascontiguousarray(inp["cache_win_k"][0, 16 * c:16 * c + 16]).reshape(16, 128, 512),
        "cache_v": np.ascontiguousarray(inp["cache_win_v"][0, 16 * c:16 * c + 16]).reshape(16, 128, 512),
    }
    return m

_CACHE = {}


def kernel(**inputs):
    inp = {kk: np.asarray(v) for kk, v in inputs.items()}
    if "k" not in _CACHE:
        _CACHE["k"] = build()
    k = _CACHE["k"]
    in_maps = [prep_core(inp, c) for c in range(8)]
    res = run_bass_kernel_spmd(k.nc, in_maps, core_ids=list(range(8)))
    R = res.results
    f32 = np.float32
    y_prompt = np.zeros((2, 4096, D), f32)
    y_sample = np.zeros((128, 4, D), f32)
    wk_p = np.zeros((1, 2, 128, 8, 64), f32)
    wv_p = np.zeros((1, 2, 128, 8, 64), f32)
    cv_p = np.zeros((1, 2, 3, 6144), f32)
    dl_p = np.zeros((1, 2, 16, 128, 128), f32)
    wk_s = np.zeros((1, 128, 128, 8, 64), f32)
    wv_s = np.zeros((1, 128, 128, 8, 64), f32)
    cv_s = np.zeros((1, 128, 3, 6144), f32)
    dl_s = np.zeros((1, 128, 16, 128, 128), f32)
    for c in range(8):
        b, j = c // 4, c % 4
        r = R[c]
        y = np.asarray(r["y_tok"])
        lo = 1024 * j
        if j == 0:
            y_prompt[b, 0:1008] = y[16:1024]
        else:
            y_prompt[b, lo - 16:lo + 1008] = y[0:1024]
        if j == 3:
            y_prompt[b, 4080:4096] = y[1088:1104]
            wk_p[0, b] = np.asarray(r["wink_p"]).reshape(128, 8, 64)
            wv_p[0, b] = np.asarray(r["winv_p"]).reshape(128, 8, 64)
        y_sample[16 * c:16 * c + 16] = y[1024:1088].reshape(16, 4, D)
        cp = np.asarray(r["conv_p"]).reshape(3, 3, 4, 128)
        cv_p[0, b].reshape(3, 3, 16, 128)[:, :, 4 * j:4 * j + 4, :] = cp
        dl_p[0, b, 4 * j:4 * j + 4] = np.asarray(r["delta_p"])
        wk_s[0, 16 * c:16 * c + 16] = np.asarray(r["wink_s"]).reshape(16, 128, 8, 64)
        wv_s[0, 16 * c:16 * c + 16] = np.asarray(r["winv_s"]).reshape(16, 128, 8, 64)
        cv_s[0, 16 * c:16 * c + 16] = np.asarray(r["conv_s"])
        dl_s[0, 16 * c:16 * c + 16] = np.asarray(r["delta_s"])
    return (y_prompt, y_sample, wk_p, wv_p, cv_p, dl_p, wk_s, wv_s, cv_s, dl_s)
```

```python
import math
import os
KSUB = os.environ.get('KSUB', 'i,ii,iii,iv').split(',')
from contextlib import ExitStack
import numpy as np
import concourse.bass as bass
import concourse.mybir as mybir
from concourse.bass_utils import run_bass_kernel_spmd

F32 = mybir.dt.float32
BF16 = mybir.dt.bfloat16
I32 = mybir.dt.int32
U32 = mybir.dt.uint32
AF = mybir.ActivationFunctionType
ALU = mybir.AluOpType

D = 4096
IN_DIM = 19488
NQ = 1152
NTOK = 1280
NSEQ = 4224
EPS = 1e-6
DEBUG = set()

ENGS = ("pe", "act", "dve", "pool", "sp")
SEM_ROLL = 20000
N_DMA_SEMS = 40


class Op:
    __slots__ = ("eng", "fn", "deps", "kind", "sig", "sem", "val", "idx", "pre")

    def __init__(self, eng, fn, kind):
        self.eng = eng
        self.fn = fn
        self.kind = kind
        self.deps = set()
        self.sig = False
        self.sem = None
        self.val = 0
        self.pre = None


class Prog:
    def __init__(self, nc):
        self.nc = nc
        self.ops = []
        self.last_w = {}
        self.readers = {}
        self.n_dma = 0
        self.dma_last = {}
        self.eng_last = {}
        self.cc_last = None

    def op(self, eng, fn, reads=(), writes=(), kind="c"):
        o = Op(eng, fn, kind)
        i = len(self.ops)
        o.idx = i
        for r in reads:
            w = self.last_w.get(r)
            if w is not None:
                o.deps.add(w)
            if isinstance(r, tuple) and r and r[0] == "ps":
                for rr in self.readers.get(r, ()):
                    if self.ops[rr].eng != eng:
                        o.deps.add(rr)
        for w_ in writes:
            w = self.last_w.get(w_)
            if w is not None:
                o.deps.add(w)
            latest = {}
            for r in self.readers.get(w_, ()):
                ro = self.ops[r]
                if ro.kind == "c" and ro.eng in ("pe", "act", "dve"):
                    if latest.get(ro.eng, -1) < r:
                        latest[ro.eng] = r
                else:
                    o.deps.add(r)
            o.deps.update(latest.values())
        for r in reads:
            self.readers.setdefault(r, []).append(i)
        for w_ in writes:
            self.last_w[w_] = i
            self.readers[w_] = []
        o.deps.discard(i)
        if eng == "pe" and kind == "c":
            o.deps = {d for d in o.deps if not (self.ops[d].eng == "pe" and self.ops[d].kind == "c")}
        if kind == "d":
            self.dma_last[self.n_dma % N_DMA_SEMS] = i
            self.n_dma += 1
        elif kind == "cc":
            self.cc_last = i
        if kind != "b":
            self.eng_last[eng] = i
        self.ops.append(o)
        return o

    def dma(self, eng, out, in_, reads=(), writes=()):
        return self.op(eng, lambda e, out=out, in_=in_: e.dma_start(out=out, in_=in_),
                       reads, writes, kind="d")

    def barrier(self):
        deps = set(self.dma_last.values()) | set(self.eng_last.values())
        if self.cc_last is not None:
            deps.add(self.cc_last)
        for e in ENGS:
            o = self.op(e, None, kind="b")
            o.deps = set(deps)
        self.last_w = {}
        self.readers = {}

    def emit(self):
        nc = self.nc
        ops = self.ops
        for o in ops:
            for d in o.deps:
                ops[d].sig = True
        for o in ops:
            if o.kind in ("d", "cc"):
                o.sig = True
        eng_sems = {e: [] for e in ENGS}
        eng_cnt = {e: 0 for e in ENGS}
        dma_sems = [nc.alloc_semaphore("dsem%d" % i) for i in range(N_DMA_SEMS)]
        cc_sem = nc.alloc_semaphore("ccsem")
        cc_cnt = 0
        dma_cnt = [0] * N_DMA_SEMS
        dma_prev = [None] * N_DMA_SEMS
        k = 0
        for o in ops:
            if o.kind == "d":
                s = k % N_DMA_SEMS
                k += 1
                o.pre = dma_prev[s]
                dma_cnt[s] += 16
                o.sem = dma_sems[s]
                o.val = dma_cnt[s]
                dma_prev[s] = o
            elif o.kind == "cc":
                cc_cnt += 1
                o.sem = cc_sem
                o.val = cc_cnt
            elif o.sig:
                c = eng_cnt[o.eng]
                si = c // SEM_ROLL
                if si >= len(eng_sems[o.eng]):
                    eng_sems[o.eng].append(nc.alloc_semaphore("es_%s_%d" % (o.eng, si)))
                o.sem = eng_sems[o.eng][si]
                o.val = c - si * SEM_ROLL + 1
                eng_cnt[o.eng] = c + 1
        per_eng = {e: [] for e in ENGS}
        for o in ops:
            per_eng[o.eng].append(o)
        final_dma = [o for o in dma_prev if o is not None]
        self.stats = {e: len(per_eng[e]) for e in ENGS}

        def run(eng_name, eng):
            waited = {}
            for o in per_eng[eng_name]:
                ws = {}
                for d in o.deps:
                    od = ops[d]
                    key = id(od.sem)
                    if ws.get(key, (None, 0))[1] < od.val:
                        ws[key] = (od.sem, od.val)
                if o.kind == "d" and o.pre is not None:
                    od = o.pre
                    key = id(od.sem)
                    if ws.get(key, (None, 0))[1] < od.val:
                        ws[key] = (od.sem, od.val)
                for key, (sem, val) in ws.items():
                    if waited.get(key, 0) >= val:
                        continue
                    eng.wait_ge(sem, val)
                    waited[key] = val
                if o.fn is None:
                    continue
                ins = o.fn(eng)
                if o.sig:
                    ins.then_inc(o.sem, 16 if o.kind == "d" else 1)
            if eng_name == "sp":
                for od in final_dma:
                    key = id(od.sem)
                    if waited.get(key, 0) >= od.val:
                        continue
                    eng.wait_ge(od.sem, od.val)

        all_sems = list(dma_sems) + [cc_sem]
        self._eng_sems = eng_sems
        with nc.Block() as block:
            @block.tensor
            def _(e):
                run("pe", e)

            @block.scalar
            def _(e):
                run("act", e)

            @block.vector
            def _(e):
                run("dve", e)

            @block.gpsimd
            def _(e):
                run("pool", e)

            @block.sync
            def _(e):
                run("sp", e)
        for e_ in ENGS:
            all_sems += eng_sems[e_]
        nc.clear_and_free_semaphores(all_sems)
        nc.all_engine_barrier()


def bc(ap, axis, n):
    lst = [list(x) for x in ap.ap]
    lst.insert(axis, [0, n])
    return bass.AP(ap.tensor, ap.offset, lst)


class Rot:
    def __init__(self, n):
        self.n = n
        self.i = -1

    def nxt(self):
        self.i = (self.i + 1) % self.n
        return self.i


class K:
    def __init__(self):
        self.nc = bass.Bass("TRN2", target_bir_lowering=False)
        self.P = Prog(self.nc)
        self.din = {}
        self.dout = {}
        self.ps = [self.nc.alloc_psum_tensor("ps%d" % i, [128, 512], F32) for i in range(8)]
        self.psrot = Rot(8)
        self.uid = 0

    def inp(self, name, shape, dt=F32):
        t = self.nc.dram_tensor(name, list(shape), dt, kind="ExternalInput").ap()
        self.din[name] = t
        return t

    def out(self, name, shape, dt=F32):
        t = self.nc.dram_tensor(name, list(shape), dt, kind="ExternalOutput").ap()
        self.dout[name] = t
        return t

    def scratch(self, name, shape, dt):
        if name in DEBUG:
            return self.out(name, shape, dt)
        return self.nc.dram_tensor(name, list(shape), dt, kind="Internal").ap()

    def psum(self):
        i = self.psrot.nxt()
        return self.ps[i], ("ps", i)

    def sb(self, st, name, shape, dt):
        self.uid += 1
        return st.enter_context(self.nc.sbuf_tensor("%s_%d" % (name, self.uid), list(shape), dt))


def build_consts(k, st):
    P = k.P
    c = {}
    ident = k.sb(st, "ident", [128, 128], F32)
    P.op("pool", lambda e: e.memset(ident[:], 0.0), writes=["ident"])
    P.op("pool", lambda e: e.affine_select(out=ident[:], in_=ident[:], pattern=[[-1, 128]],
                                           compare_op=ALU.not_equal, fill=1.0, base=0,
                                           channel_multiplier=1),
         reads=["ident"], writes=["ident"])
    ones = k.sb(st, "ones", [128, 128], F32)
    P.op("pool", lambda e: e.memset(ones[:], 1.0), writes=["ones"])
    onesb = k.sb(st, "onesb", [128, 128], BF16)
    P.op("pool", lambda e: e.memset(onesb[:], 1.0), writes=["onesb"])
    epst = k.sb(st, "epst", [128, 1], F32)
    P.op("pool", lambda e: e.memset(epst[:], EPS), writes=["epst"])
    c.update(ident=ident, ones=ones, onesb=onesb, eps=epst)
    return c


def stage_norm_T(k, c, src, nblk, wvec, dst, tag):
    P = k.P
    with ExitStack() as st:
        wbc = k.sb(st, "wbc", [128, D], F32)
        P.dma("sp", wbc[:], wvec.partition_broadcast(128), writes=[tag + "wbc"])
        xts = [k.sb(st, "xt", [128, D], F32) for _ in range(2)]
        hn = k.sb(st, "hn", [128, D], F32)
        junk = k.sb(st, "junk", [128, D], BF16)
        hTs = [k.sb(st, "hT", [128, 32, 128], BF16) for _ in range(2)]
        ss = [k.sb(st, "ss", [128, 1], F32) for _ in range(2)]
        rs = [k.sb(st, "rs", [128, 1], F32) for _ in range(2)]
        for blk in range(nblk):
            i = blk % 2
            xt, hT = xts[i], hTs[i]
            P.dma("sp", xt[:], src[blk * 128:(blk + 1) * 128, :], writes=[(tag, "xt", i)])
            P.op("act", lambda e, xt=xt, i=i: e.activation(out=junk[:], in_=xt[:], func=AF.Square,
                                                         accum_out=ss[i][:]),
                 reads=[(tag, "xt", i)], writes=[(tag, "junk"), (tag, "ss", i)])
            P.op("act", lambda e, i=i: e.activation(out=rs[i][:], in_=ss[i][:], func=AF.Sqrt,
                                                   scale=1.0 / D, bias=c["eps"][:]),
                 reads=[(tag, "ss", i), "epst"], writes=[(tag, "rs", i)])
            P.op("dve", lambda e, i=i: e.reciprocal(out=rs[i][:], in_=rs[i][:]),
                 reads=[(tag, "rs", i)], writes=[(tag, "rs", i)])
            P.op("dve", lambda e, xt=xt, i=i: e.scalar_tensor_tensor(
                out=hn[:], in0=xt[:], scalar=rs[i][:], in1=wbc[:], op0=ALU.mult, op1=ALU.mult),
                reads=[(tag, "xt", i), (tag, "rs", i), tag + "wbc"], writes=[(tag, "hn")])
            for g in range(8):
                ps, pk = k.psum()
                for q in range(4):
                    kc = g * 4 + q
                    P.op("pe", lambda e, ps=ps, q=q, kc=kc: e.transpose(
                        out=ps[:, q * 128:(q + 1) * 128], in_=hn[:, kc * 128:(kc + 1) * 128],
                        identity=c["ident"][:]),
                        reads=[(tag, "hn"), "ident"], writes=[pk])
                eng = "act" if g % 2 == 0 else "dve"
                if eng == "act":
                    P.op("act", lambda e, ps=ps, hT=hT, g=g: e.activation(
                        out=hT[:, g * 4:(g + 1) * 4, :],
                        in_=ps[:].rearrange("p (a t) -> p a t", a=4), func=AF.Copy),
                        reads=[pk], writes=[(tag, "hT", i, g)])
                else:
                    P.op("dve", lambda e, ps=ps, hT=hT, g=g: e.tensor_copy(
                        out=hT[:, g * 4:(g + 1) * 4, :],
                        in_=ps[:].rearrange("p (a t) -> p a t", a=4)),
                        reads=[pk], writes=[(tag, "hT", i, g)])
            P.dma("sp", dst[blk], hT[:].rearrange("p a t -> p (a t)"),
                  reads=[(tag, "hT", i, g) for g in range(8)], writes=[(tag, "dst", blk)])
        P.barrier()


class Gemm:
    def __init__(self, k, st, T):
        self.k = k
        self.T = T
        self.X = k.sb(st, "X", [128, T // 128, 32, 128], BF16)
        self.W = [k.sb(st, "W", [128, 32, 512], BF16) for _ in range(2)]
        self.wrot = Rot(2)
        self.tag = "g%d" % k.uid

    def load_x(self, xt_dram, blocks):
        P = self.k.P
        for j, blk in enumerate(blocks):
            P.dma("sp", self.X[:, j].rearrange("p a t -> p (a t)"), xt_dram[blk],
                  writes=[(self.tag, "X", j)])

    def load_w(self, wap, ncols):
        P = self.k.P
        i = self.wrot.nxt()
        W = self.W[i]
        wv = wap.rearrange("(a p) n -> p a n", p=128)
        for q in range(4):
            P.dma("pool", W[:, q * 8:(q + 1) * 8, 0:ncols], wv[:, q * 8:(q + 1) * 8, :],
                  writes=[(self.tag, "W", i, q)])
        return W, (self.tag, "W", i)

    def tm(self, wap, ncols, tblocks, epi):
        P = self.k.P
        W, wk = self.load_w(wap, ncols)
        for tb in tblocks:
            ps, pk = self.k.psum()
            for kc in range(32):
                P.op("pe", lambda e, ps=ps, kc=kc, tb=tb, W=W: e.matmul(
                    ps[:, 0:ncols], lhsT=self.X[:, tb, kc, :],
                    rhs=W[:, kc, 0:ncols], start=(kc == 0), stop=(kc == 31)),
                    reads=[(self.tag, "X", tb), wk + (kc // 8,)], writes=[pk])
            epi(tb, ps, pk)

    def fm(self, wap, ncols, subs, epi):
        P = self.k.P
        W, wk = self.load_w(wap, ncols)
        for ci in range(ncols // 128):
            for si, (t0, nt) in enumerate(subs):
                ps, pk = self.k.psum()
                xr = [(self.tag, "X", j) for j in range(t0 // 128, (t0 + nt + 127) // 128)]
                for kc in range(32):
                    P.op("pe", lambda e, ps=ps, kc=kc, ci=ci, t0=t0, nt=nt, W=W: e.matmul(
                        ps[:, 0:nt], lhsT=W[:, kc, ci * 128:(ci + 1) * 128],
                        rhs=self.X[:, t0 // 128:(t0 + nt) // 128, kc, :], start=(kc == 0),
                        stop=(kc == 31)),
                        reads=xr + [wk + (kc // 8,)], writes=[pk])
                epi(ci, si, t0, nt, ps, pk)


def stage_inproj_tok(k, c, io, sc):
    P = k.P
    w_in = io["w_in"]
    with ExitStack() as st:
        g = Gemm(k, st, NTOK)
        g.load_x(sc["XT_tok"], list(range(10)))
        cs = k.sb(st, "cs", [128, 10, 256], F32)
        P.dma("sp", cs[:], io["cs_tab"].rearrange("(b p) n -> p b n", p=128), writes=["cs"])
        stg = [k.sb(st, "stg", [128, 512], F32) for _ in range(3)]
        srot = Rot(3)
        t1 = k.sb(st, "t1", [128, 8, 16], F32)
        t2 = k.sb(st, "t2", [128, 8, 16], F32)

        for nt in range(6 if 'i' in KSUB else 0):
            def epi(tb, ps, pk, nt=nt):
                si = srot.nxt()
                s = stg[si]
                sk = ("stg", si)
                P.op("act", lambda e, s=s, ps=ps: e.activation(out=s[:], in_=ps[:], func=AF.Copy),
                     reads=[pk], writes=[sk])
                if nt < 5 and not os.environ.get('KNOROPE'):
                    pv = ps[:].rearrange("p (h d) -> p h d", h=8)
                    sv = s[:].rearrange("p (h d) -> p h d", h=8)
                    C16 = cs[:, tb, 0:128].rearrange("p (h d) -> p h d", h=8)
                    nS = cs[:, tb, 128:192].rearrange("p (h d) -> p h d", h=8)
                    Sn = cs[:, tb, 192:256].rearrange("p (h d) -> p h d", h=8)
                    P.op("dve", lambda e, sv=sv, C16=C16: e.tensor_tensor(
                        out=t1[:], in0=sv[:, :, 0:16], in1=C16, op=ALU.mult),
                        reads=[sk, "cs"], writes=["t1"])
                    P.op("dve", lambda e, sv=sv, nS=nS: e.tensor_tensor(
                        out=t2[:, :, 0:8], in0=sv[:, :, 8:16], in1=nS, op=ALU.mult),
                        reads=[sk, "cs"], writes=["t2a"])
                    P.op("dve", lambda e, sv=sv, Sn=Sn: e.tensor_tensor(
                        out=t2[:, :, 8:16], in0=sv[:, :, 0:8], in1=Sn, op=ALU.mult),
                        reads=[sk, "cs"], writes=["t2b"])
                    P.op("dve", lambda e, sv=sv: e.tensor_tensor(
                        out=sv[:, :, 0:16], in0=t1[:], in1=t2[:], op=ALU.add),
                        reads=["t1", "t2a", "t2b"], writes=[sk])
                P.dma("sp", sc["QKV"][tb * 128:(tb + 1) * 128, nt * 512:(nt + 1) * 512], s[:],
                      reads=[sk], writes=[("QKV", tb, nt)])
            g.tm(w_in[:, nt * 512:(nt + 1) * 512], 512, list(range(10)), epi)

        stb = [k.sb(st, "stb", [128, 512], BF16) for _ in range(3)]
        brot = Rot(3)
        subs = [(128, 512), (640, 512), (1152, 128)]
        for nt in range(16 if 'ii' in KSUB else 0):
            def epi(ci, si_, t0, n, ps, pk, nt=nt):
                bi = brot.nxt()
                s = stb[bi]
                P.op("act", lambda e, s=s, ps=ps, n=n: e.activation(out=s[:, 0:n], in_=ps[:, 0:n],
                                                                  func=AF.Sigmoid),
                     reads=[pk], writes=[("stb", bi)])
                ch = nt * 4 + ci
                P.dma("sp", sc["SG_T"][ch, :, t0 - 128:t0 - 128 + n], s[:, 0:n],
                      reads=[("stb", bi)], writes=[("SG", ch, t0)])
            c0 = 11296 + nt * 512
            g.fm(w_in[:, c0:c0 + 512], 512, subs, epi)

        for nt in range(16 if 'iii' in KSUB else 0):
            def epi(ci, si_, t0, n, ps, pk, nt=nt):
                si = srot.nxt()
                s = stg[si]
                P.op("act", lambda e, s=s, ps=ps: e.activation(out=s[:, 0:128], in_=ps[:, 0:128],
                                                             func=AF.Copy),
                     reads=[pk], writes=[("stg", si)])
                ch = nt * 4 + ci
                P.dma("sp", sc["DNS_T"][ch], s[:, 0:128], reads=[("stg", si)],
                      writes=[("DNS", ch)])
            c0 = 3072 + nt * 512
            g.fm(w_in[:, c0:c0 + 512], 512, [(1152, 128)], epi)

        for nt in range(12 if 'iv' in KSUB else 0):
            def epi(tb, ps, pk, nt=nt):
                si = srot.nxt()
                s = stg[si]
                P.op("act", lambda e, s=s, ps=ps: e.activation(out=s[:], in_=ps[:], func=AF.Copy),
                     reads=[pk], writes=[("stg", si)])
                src = s[0:64, :]
                P.dma("sp", sc["CVS"][:, nt * 512:(nt + 1) * 512], src,
                      reads=[("stg", si)], writes=[("CVS", nt)])
            c0 = 3072 + nt * 512
            g.tm(w_in[:, c0:c0 + 512], 512, [9], epi)

        def epi_ba(tb, ps, pk):
            si = srot.nxt()
            s = stg[si]
            P.op("act", lambda e, s=s, ps=ps: e.activation(out=s[:, 0:32], in_=ps[:, 0:32],
                                                         func=AF.Copy),
                 reads=[pk], writes=[("stg", si)])
            P.dma("sp", sc["BAS"][:, :], s[:, 0:32], reads=[("stg", si)], writes=["BAS"])
        g.tm(w_in[:, 11264:11296], 32, [9], epi_ba)
        P.barrier()


def stage_inproj_seq(k, c, io, sc):
    P = k.P
    with ExitStack() as st:
        g = Gemm(k, st, NTOK)
        stg = [k.sb(st, "stg", [128, 512], F32) for _ in range(3)]
        srot = Rot(3)
        groups = [list(range(0, 10)), list(range(10, 20)), list(range(20, 30)), [30, 31, 32]]
        for gi, blks in enumerate(groups):
            g.tag = "gs%d" % gi
            g.load_x(sc["XT_seq"], blks)
            T = len(blks) * 128
            u0 = blks[0] * 128
            subs = [(t, min(512, T - t)) for t in range(0, T, 512)]
            for nt in range(4):
                def epi(ci, si_, t0, n, ps, pk, nt=nt):
                    si = srot.nxt()
                    s = stg[si]
                    P.op("act", lambda e, s=s, ps=ps, n=n: e.activation(
                        out=s[:, 0:n], in_=ps[:, 0:n], func=AF.Copy),
                        reads=[pk], writes=[("stg", si)])
                    ch = nt * 4 + ci
                    P.dma("sp", sc["DN_T"][ch, :, u0 + t0:u0 + t0 + n], s[:, 0:n],
                          reads=[("stg", si)], writes=[("DN", ch, u0 + t0)])
                g.fm(io["w_dn"][:, nt * 512:(nt + 1) * 512], 512, subs, epi)

            def epi_ba(tb, ps, pk):
                si = srot.nxt()
                s = stg[si]
                P.op("act", lambda e, s=s, ps=ps: e.activation(out=s[:, 0:8], in_=ps[:, 0:8],
                                                             func=AF.Copy),
                     reads=[pk], writes=[("stg", si)])
                r0 = u0 + tb * 128
                P.dma("sp", sc["BA"][r0:r0 + 128, :], s[:, 0:8], reads=[("stg", si)],
                      writes=[("BA", r0)])
            g.tm(io["w_ba"], 8, list(range(len(blks))), epi_ba)
            P.barrier()


def stage_attention(k, c, io, sc):
    P = k.P
    QKV = sc["QKV"]
    with ExitStack() as st:
        mk = k.sb(st, "mk", [128, 6, 512], BF16)
        for i in range(6):
            P.dma("pool", mk[:, i, :], io["masks"][i], writes=[("mk", i)])
        sinkE = k.sb(st, "sinkE", [64, 32], F32)
        P.dma("sp", sinkE[:], io["sinks"].partition_broadcast(64), writes=["sinkE"])
        P.op("act", lambda e: e.activation(out=sinkE[:], in_=sinkE[:], func=AF.Exp),
             reads=["sinkE"], writes=["sinkE"])
        sinkb = k.sb(st, "sinkb", [64, 32, 128], F32)
        P.op("dve", lambda e: e.tensor_copy(out=sinkb[:], in_=bc(sinkE[:], 2, 128)),
             reads=["sinkE"], writes=["sinkb"])
        OC = k.sb(st, "OC", [64, 32, 64], F32)
        DC = k.sb(st, "DC", [64, 32, 64], F32)
        kin = [k.sb(st, "kin", [128, 512], F32) for _ in range(2)]
        qin = k.sb(st, "qin", [128, 2048], F32)
        kT = [k.sb(st, "kT", [64, 8, 128], BF16) for _ in range(2)]
        vb = [k.sb(st, "vb", [128, 512], BF16) for _ in range(2)]
        qT = k.sb(st, "qT", [64, 32, 128], BF16)
        pT = [k.sb(st, "pT", [128, 512], BF16) for _ in range(4)]
        prot = Rot(4)
        den = [k.sb(st, "den", [64, 512], F32) for _ in range(2)]
        oT = k.sb(st, "oT", [64, 32, 128], BF16)
        numt = k.sb(st, "numt", [64, 512], F32)
        ident = c["ident"]

        def transposes(src, nheads, dst, scale, rk, wk):
            for g in range(nheads // 4):
                ps, pk = k.psum()
                for q in range(4):
                    h = g * 4 + q
                    P.op("pe", lambda e, ps=ps, q=q, h=h: e.transpose(
                        out=ps[0:64, q * 128:(q + 1) * 128], in_=src[:, h * 64:(h + 1) * 64],
                        identity=ident[:]), reads=[rk, "ident"], writes=[pk])
                P.op("act", lambda e, ps=ps, g=g: e.activation(
                    out=dst[:, g * 4:(g + 1) * 4, :].rearrange("p a t -> p (a t)"),
                    in_=ps[0:64, :], func=AF.Copy, scale=scale), reads=[pk], writes=[wk + (g,)])

        kc_in = [k.sb(st, "kcin", [128, 512], F32) for _ in range(2)]
        kcT = [k.sb(st, "kcT", [64, 8, 128], BF16) for _ in range(2)]
        vc = [k.sb(st, "vc", [128, 512], BF16) for _ in range(2)]
        pTc = [k.sb(st, "pTc", [128, 128], BF16) for _ in range(2)]
        P.dma("sp", qin[:], QKV[1152:1280, 0:2048], writes=["qin"])
        transposes(qin, 32, qT, 0.125, "qin", ("qT",))
        qTr = [("qT", g) for g in range(8)]
        for s in range(16):
            i = s % 2
            P.dma("sp", kc_in[i][:], io["cache_k"][s], writes=[("kcin", i)])
            P.dma("pool", vc[i][:], io["cache_v"][s], writes=[("vc", i)])
            transposes(kc_in[i], 8, kcT[i], 1.0, ("kcin", i), ("kcT", i))
            ps, pk = k.psum()
            for hk in range(8):
                P.op("pe", lambda e, ps=ps, hk=hk, i=i, s=s: e.matmul(
                    ps[:, hk * 16:(hk + 1) * 16], lhsT=kcT[i][:, hk, :],
                    rhs=qT[:, 4 * hk:4 * hk + 4, 4 * s:4 * s + 4], start=True, stop=True),
                    reads=[("kcT", i, hk // 4)] + qTr, writes=[pk])
            P.op("act", lambda e, ps=ps, i=i: e.activation(out=pTc[i][:], in_=ps[:, 0:128],
                                                          func=AF.Exp),
                 reads=[pk], writes=[("pTc", i)])
            P.op("pool", lambda e, i=i: e.tensor_tensor(out=pTc[i][:], in0=pTc[i][:],
                                                        in1=mk[:, 5, 0:128], op=ALU.mult),
                 reads=[("pTc", i), ("mk", 5)], writes=[("pTc", i)])
            pso, pko = k.psum()
            for hk in range(8):
                P.op("pe", lambda e, pso=pso, hk=hk, i=i: e.matmul(
                    pso[0:64, hk * 16:(hk + 1) * 16], lhsT=vc[i][:, hk * 64:(hk + 1) * 64],
                    rhs=pTc[i][:, hk * 16:(hk + 1) * 16], start=True, stop=True),
                    reads=[("vc", i), ("pTc", i)], writes=[pko])
            P.op("pe", lambda e, pso=pso, i=i: e.matmul(
                pso[0:64, 128:256], lhsT=c["onesb"][:, 0:64], rhs=pTc[i][:], start=True, stop=True),
                reads=["onesb", ("pTc", i)], writes=[pko])
            P.op("act", lambda e, pso=pso, s=s: e.activation(
                out=OC[:, :, 4 * s:4 * s + 4],
                in_=pso[0:64, 0:128].rearrange("p (h t) -> p h t", t=4), func=AF.Copy),
                reads=[pko], writes=[("OC", s)])
            P.op("dve", lambda e, pso=pso, s=s: e.tensor_copy(
                out=DC[:, :, 4 * s:4 * s + 4],
                in_=pso[0:64, 128:256].rearrange("p (h t) -> p h t", t=4)),
                reads=[pko], writes=[("DC", s)])
            P.dma("sp", io["wink_s"][s, 0:124, :], io["cache_k"][s, 4:128, :], writes=[("wks", s)])
            P.dma("sp", io["winv_s"][s, 0:124, :], io["cache_v"][s, 4:128, :], writes=[("wvs", s)])
            P.dma("sp", io["wink_s"][s, 124:128, :], QKV[1152 + 4 * s:1156 + 4 * s, 2048:2560],
                  writes=[("wks2", s)])
            P.dma("sp", io["winv_s"][s, 124:128, :], QKV[1152 + 4 * s:1156 + 4 * s, 2560:3072],
                  writes=[("wvs2", s)])
        ocr = [("OC", s) for s in range(16)] + [("DC", s) for s in range(16)]

        for tb in range(10):
            i = tb % 2
            r0 = tb * 128
            P.dma("sp", kin[i][:], QKV[r0:r0 + 128, 2048:2560], writes=[("kin", i)])
            P.dma("pool", vb[i][:], QKV[r0:r0 + 128, 2560:3072], writes=[("vb", i)])
            transposes(kin[i], 8, kT[i], 1.0, ("kin", i), ("kT", i))
            if tb == 0:
                continue
            P.dma("sp", qin[:], QKV[r0:r0 + 128, 0:2048], writes=["qin"])
            transposes(qin, 32, qT, 0.125, "qin", ("qT",))
            mixed = (tb == 9)
            mc = 3 if mixed else 0
            mp = 4 if mixed else (2 if tb == 1 else 1)
            for hk in range(8):
                pts = []
                for (ki, mi) in ((1 - i, mp), (i, mc)):
                    ps, pk = k.psum()
                    P.op("pe", lambda e, ps=ps, ki=ki, hk=hk: e.matmul(
                        ps[:, :], lhsT=kT[ki][:, hk, :],
                        rhs=qT[:, 4 * hk:4 * hk + 4, :], start=True, stop=True),
                        reads=[("kT", ki, hk // 4), ("qT", hk // 2 * 0 + hk)], writes=[pk])
                    pi = prot.nxt()
                    P.op("act", lambda e, ps=ps, pi=pi: e.activation(out=pT[pi][:], in_=ps[:],
                                                                    func=AF.Exp),
                         reads=[pk], writes=[("pT", pi)])
                    P.op("pool", lambda e, pi=pi, mi=mi: e.tensor_tensor(
                        out=pT[pi][:], in0=pT[pi][:], in1=mk[:, mi, :], op=ALU.mult),
                        reads=[("pT", pi), ("mk", mi)], writes=[("pT", pi)])
                    pts.append((pi, ki))
                pso, pko = k.psum()
                psd, pkd = k.psum()
                for n_, (pi, ki) in enumerate(pts):
                    P.op("pe", lambda e, pso=pso, pi=pi, ki=ki, hk=hk, n_=n_: e.matmul(
                        pso[0:64, :], lhsT=vb[ki][:, hk * 64:(hk + 1) * 64], rhs=pT[pi][:],
                        start=(n_ == 0), stop=(n_ == 1)),
                        reads=[("vb", ki), ("pT", pi)], writes=[pko])
                for n_, (pi, ki) in enumerate(pts):
                    P.op("pe", lambda e, psd=psd, pi=pi, n_=n_: e.matmul(
                        psd[0:64, :], lhsT=c["onesb"][:, 0:64], rhs=pT[pi][:],
                        start=(n_ == 0), stop=(n_ == 1)),
                        reads=["onesb", ("pT", pi)], writes=[pkd])
                di = hk % 2
                dn = den[di]
                P.op("dve", lambda e, psd=psd, dn=dn, hk=hk: e.tensor_tensor(
                    out=dn[:], in0=psd[0:64, :],
                    in1=sinkb[:, 4 * hk:4 * hk + 4, :].rearrange("p a t -> p (a t)"), op=ALU.add),
                    reads=[pkd, "sinkb"], writes=[("den", di)])
                num_in = pso
                if mixed:
                    dv = dn[:].rearrange("p (a t) -> p a t", a=4)
                    P.op("dve", lambda e, dv=dv, hk=hk: e.tensor_tensor(
                        out=dv[:, :, 0:64], in0=dv[:, :, 0:64], in1=DC[:, 4 * hk:4 * hk + 4, :],
                        op=ALU.add), reads=[("den", di)] + ocr, writes=[("den", di)])
                P.op("act", lambda e, dn=dn: e.activation(out=dn[:], in_=dn[:], func=AF.Ln),
                     reads=[("den", di)], writes=[("den", di)])
                P.op("act", lambda e, dn=dn: e.activation(out=dn[:], in_=dn[:], func=AF.Exp,
                                                         scale=-1.0),
                     reads=[("den", di)], writes=[("den", di)])
                ov = oT[:, 4 * hk:4 * hk + 4, :]
                if mixed:
                    P.op("dve", lambda e, pso=pso: e.tensor_copy(out=numt[:], in_=pso[0:64, :]),
                         reads=[pko], writes=["numt"])
                    nv = numt[:].rearrange("p (a t) -> p a t", a=4)
                    P.op("dve", lambda e, nv=nv, hk=hk: e.tensor_tensor(
                        out=nv[:, :, 0:64], in0=nv[:, :, 0:64], in1=OC[:, 4 * hk:4 * hk + 4, :],
                        op=ALU.add), reads=["numt"] + ocr, writes=["numt"])
                    P.op("dve", lambda e, dn=dn, ov=ov: e.tensor_tensor(
                        out=ov.rearrange("p a t -> p (a t)"), in0=numt[:], in1=dn[:], op=ALU.mult),
                        reads=["numt", ("den", di)], writes=[("oT", hk)])
                else:
                    P.op("dve", lambda e, pso=pso, dn=dn, ov=ov: e.tensor_tensor(
                        out=ov.rearrange("p a t -> p (a t)"), in0=pso[0:64, :], in1=dn[:],
                        op=ALU.mult), reads=[pko, ("den", di)], writes=[("oT", hk)])
            P.dma("sp", sc["OA_T"][tb - 1].rearrange("p (h t) -> p h t", h=32), oT[:],
                  reads=[("oT", hk) for hk in range(8)], writes=[("OA", tb)])
        P.dma("sp", io["wink_p"][0:112, :], QKV[1024 + 16:1152, 2048:2560], writes=["wkp1"])
        P.dma("sp", io["wink_p"][112:128, :], QKV[1216:1232, 2048:2560], writes=["wkp2"])
        P.dma("sp", io["winv_p"][0:112, :], QKV[1024 + 16:1152, 2560:3072], writes=["wvp1"])
        P.dma("sp", io["winv_p"][112:128, :], QKV[1216:1232, 2560:3072], writes=["wvp2"])
        P.barrier()


class DnChunk:
    def __init__(self, k, st, c, dc, nset=2, lowp=False):
        self.k, self.c, self.dc = k, c, dc
        self.lowp = lowp
        LP = ("vb", "kbg", "kd", "qd", "QK", "WT") if lowp else ()
        self.sets = []
        for s in range(nset):
            t = {}
            if lowp:
                t["Rb"] = k.sb(st, "dn_Rb", [64, 256], BF16)
            for nm, shp in (("bx", [128, 16]), ("vb", [64, 512]), ("kbg", [64, 512]), ("kd", [64, 512]),
                            ("X4", [64, 256]), ("E4", [64, 256]), ("Dm", [64, 256]), ("Ds", [64, 256]),
                            ("DmT", [64, 256]), ("gB", [64, 512]), ("EG", [128, 256]),
                            ("qd", [128, 256]), ("N", [64, 256]), ("NT", [64, 256]),
                            ("Pa", [64, 256]), ("PTa", [64, 256]), ("Pb", [64, 256]),
                            ("PTb", [64, 256]), ("R", [64, 256]), ("QK", [64, 256]),
                            ("U", [64, 512]), ("WT", [128, 256]), ("tmp", [64, 256]),
                            ("nb", [64, 4]), ("bg2", [64, 4])):
                t[nm] = k.sb(st, "dn_" + nm, shp, BF16 if nm in LP else F32)
            self.sets.append(t)
        self.rot = Rot(nset)

    def run(self, *a, **kw):
        out = {}
        for _ in self.run_gen(out, *a, **kw):
            pass
        return out["T"], out["K_"]

    def run_gen(self, out, qs, ks, vs, rk, beta, g, bgk, levels=5, qsb=None, ksb=None, rkb=None):
        k, P, c, dc = self.k, self.k.P, self.c, self.dc
        si = self.rot.nxt()
        T = self.sets[si]
        K_ = lambda nm: ("dn", si, nm)
        out["T"], out["K_"] = T, K_
        ones, ident = c["ones"], c["ident"]
        U64, Ys = dc["U64"], dc["Ys"]
        psB, pkB = k.psum()
        P.op("pe", lambda e: e.matmul(psB[0:64, 0:4], lhsT=U64, rhs=g, start=True, stop=True),
             reads=[bgk, "dconst"], writes=[pkB])
        P.op("pe", lambda e: e.matmul(psB[0:64, 4:8], lhsT=Ys, rhs=g, start=True, stop=True),
             reads=[bgk, "dconst"], writes=[pkB])
        P.op("pe", lambda e: e.matmul(psB[:, 8:12], lhsT=dc["GL"], rhs=g, start=True, stop=True),
             reads=[bgk, "dconst"], writes=[pkB])
        bx = T["bx"]
        P.op("act", lambda e: e.activation(out=bx[:, 0:12], in_=psB[:, 0:12], func=AF.Exp),
             reads=[pkB], writes=[K_("bx")])
        P.op("pool", lambda e: e.tensor_scalar(out=T["nb"][:], in0=beta, scalar1=-1.0, scalar2=None,
                                               op0=ALU.mult), reads=[bgk], writes=[K_("nb")])
        P.op("pool", lambda e: e.tensor_tensor(out=T["bg2"][:], in0=beta, in1=bx[0:64, 0:4],
                                               op=ALU.mult), reads=[bgk, K_("bx")], writes=[K_("bg2")])
        yield
        psK, pkK = k.psum()
        psV, pkV = k.psum()
        for h in range(4):
            P.op("pe", lambda e, h=h: e.transpose(out=psK[0:64, h * 128:(h + 1) * 128], in_=ks(h),
                                                  identity=ident[:]), reads=rk + ["ident"], writes=[pkK])
        for h in range(4):
            P.op("pe", lambda e, h=h: e.transpose(out=psV[0:64, h * 128:(h + 1) * 128], in_=vs(h),
                                                  identity=ident[:]), reads=rk + ["ident"], writes=[pkV])
        v3 = lambda t: t[:].rearrange("p (h e) -> p h e", h=4)
        P.op("dve", lambda e: e.tensor_tensor(out=v3(T["vb"]), in0=psV[0:64, :].rearrange("p (h e) -> p h e", h=4),
                                              in1=bc(beta, 2, 128), op=ALU.mult),
             reads=[pkV, bgk], writes=[K_("vb")])
        P.op("dve", lambda e: e.tensor_tensor(out=v3(T["kbg"]), in0=psK[0:64, :].rearrange("p (h e) -> p h e", h=4),
                                              in1=bc(T["bg2"][:], 2, 128), op=ALU.mult),
             reads=[pkK, K_("bg2")], writes=[K_("kbg")])
        P.op("dve", lambda e: e.tensor_tensor(out=v3(T["kd"]), in0=psK[0:64, :].rearrange("p (h e) -> p h e", h=4),
                                              in1=bc(bx[0:64, 4:8], 2, 128), op=ALU.mult),
             reads=[pkK, K_("bx")], writes=[K_("kd")])
        yield
        x3 = T["X4"][:].rearrange("p (h i) -> p h i", h=4)
        P.op("pool", lambda e: e.tensor_tensor(out=x3, in0=bc(U64, 1, 4), in1=bc(g, 2, 64), op=ALU.mult),
             reads=[bgk, "dconst"], writes=[K_("X4")])
        psA, pkA = k.psum()
        for h in range(4):
            P.op("pe", lambda e, h=h: e.matmul(psA[0:64, h * 64:(h + 1) * 64],
                                               lhsT=T["X4"][:, h * 64:(h + 1) * 64], rhs=Ys,
                                               start=True, stop=True),
                 reads=[K_("X4"), "dconst"], writes=[pkA])
        P.op("pe", lambda e: e.matmul(psA[0:64, 256:512], lhsT=Ys, rhs=T["X4"][:], start=True, stop=True),
             reads=[K_("X4"), "dconst"], writes=[pkA])
        yield
        P.op("act", lambda e: e.activation(out=T["E4"][:], in_=psA[0:64, 0:256], func=AF.Exp),
             reads=[pkA], writes=[K_("E4")])
        P.op("act", lambda e: e.activation(out=T["DmT"][:], in_=psA[0:64, 256:512], func=AF.Exp),
             reads=[pkA], writes=[K_("DmT")])
        P.op("pool", lambda e: e.tensor_tensor(out=T["Ds"][:], in0=T["E4"][:], in1=dc["strict4"], op=ALU.mult),
             reads=[K_("E4"), "dconst"], writes=[K_("Ds")])
        P.op("pool", lambda e: e.tensor_tensor(out=T["DmT"][:], in0=T["DmT"][:], in1=dc["triu4"], op=ALU.mult),
             reads=[K_("DmT"), "dconst"], writes=[K_("DmT")])
        yield
        gb3 = T["gB"][:].rearrange("p (h d) -> p h d", h=4)
        P.op("pool", lambda e: e.tensor_copy(out=gb3, in_=bc(g, 2, 128)), reads=[bgk], writes=[K_("gB")])
        psG, pkG = k.psum()
        for h in range(4):
            P.op("pe", lambda e, h=h: e.matmul(psG[:, h * 64:(h + 1) * 64],
                                               lhsT=T["gB"][:, h * 128:(h + 1) * 128], rhs=U64,
                                               start=True, stop=True),
                 reads=[K_("gB"), "dconst"], writes=[pkG])
        yield
        P.op("act", lambda e: e.activation(out=T["EG"][:], in_=psG[:, 0:256], func=AF.Exp),
             reads=[pkG], writes=[K_("EG")])
        for h in range(4):
            P.op("pool", lambda e, h=h: e.tensor_tensor(out=T["qd"][:, h * 64:(h + 1) * 64], in0=qs(h),
                                                        in1=T["EG"][:, h * 64:(h + 1) * 64], op=ALU.mult),
                 reads=rk + [K_("EG")], writes=[K_("qd")])
        yield
        psKK, pkKK = k.psum()
        gq = qsb if qsb is not None else qs
        gk = ksb if ksb is not None else ks
        grk = rkb if rkb is not None else rk
        for h in range(4):
            P.op("pe", lambda e, h=h: e.matmul(psKK[0:64, h * 64:(h + 1) * 64], lhsT=gk(h), rhs=gk(h),
                                               start=True, stop=True), reads=grk, writes=[pkKK])
        for h in range(4):
            P.op("pe", lambda e, h=h: e.matmul(psKK[0:64, 256 + h * 64:256 + (h + 1) * 64], lhsT=gk(h),
                                               rhs=gq(h), start=True, stop=True), reads=grk, writes=[pkKK])
        yield
        t3 = T["tmp"][:].rearrange("p (h j) -> p h j", h=4)
        P.op("dve", lambda e: e.tensor_tensor(out=t3, in0=psKK[0:64, 0:256].rearrange("p (h j) -> p h j", h=4),
                                              in1=bc(T["nb"][:], 2, 64), op=ALU.mult),
             reads=[pkKK, K_("nb")], writes=[K_("tmp")])
        P.op("dve", lambda e: e.tensor_tensor(out=T["N"][:], in0=T["tmp"][:], in1=T["Ds"][:], op=ALU.mult),
             reads=[K_("tmp"), K_("Ds")], writes=[K_("N")])
        P.op("dve", lambda e: e.tensor_tensor(out=T["QK"][:], in0=psKK[0:64, 256:512], in1=T["DmT"][:],
                                              op=ALU.mult), reads=[pkKK, K_("DmT")], writes=[K_("QK")])
        yield
        psN, pkN = k.psum()
        for h in range(4):
            P.op("pe", lambda e, h=h: e.transpose(out=psN[0:64, h * 64:(h + 1) * 64],
                                                  in_=T["N"][:, h * 64:(h + 1) * 64],
                                                  identity=ident[0:64, 0:64]),
                 reads=[K_("N"), "ident"], writes=[pkN])
        yield
        P.op("act", lambda e: e.activation(out=T["NT"][:], in_=psN[0:64, 0:256], func=AF.Copy),
             reads=[pkN], writes=[K_("NT")])
        P.op("pool", lambda e: e.tensor_tensor(out=T["R"][:], in0=T["NT"][:], in1=dc["I4"], op=ALU.add),
             reads=[K_("NT"), "dconst"], writes=[K_("R")])
        Pc, PTc = ("N", "NT")
        for lv in range(levels):
            yield
            Pn, PTn = ("Pa", "PTa") if lv % 2 == 0 else ("Pb", "PTb")
            psP, pkP = k.psum()
            for h in range(4):
                sl = slice(h * 64, (h + 1) * 64)
                P.op("pe", lambda e, sl=sl, Pc=Pc, PTc=PTc, psP=psP: e.matmul(
                    psP[0:64, sl], lhsT=T[PTc][:, sl], rhs=T[Pc][:, sl], start=True, stop=True),
                    reads=[K_(Pc), K_(PTc)], writes=[pkP])
            if lv < levels - 1:
                for h in range(4):
                    sl = slice(h * 64, (h + 1) * 64)
                    so = slice(256 + h * 64, 256 + (h + 1) * 64)
                    P.op("pe", lambda e, sl=sl, so=so, Pc=Pc, PTc=PTc, psP=psP: e.matmul(
                        psP[0:64, so], lhsT=T[Pc][:, sl], rhs=T[PTc][:, sl], start=True, stop=True),
                        reads=[K_(Pc), K_(PTc)], writes=[pkP])
            yield
            P.op("act", lambda e, Pn=Pn, psP=psP: e.activation(out=T[Pn][:], in_=psP[0:64, 0:256], func=AF.Copy),
                 reads=[pkP], writes=[K_(Pn)])
            if lv < levels - 1:
                P.op("act", lambda e, PTn=PTn, psP=psP: e.activation(out=T[PTn][:], in_=psP[0:64, 256:512],
                                                                   func=AF.Copy), reads=[pkP], writes=[K_(PTn)])
            yield
            psR, pkR = k.psum()
            for h in range(4):
                sl = slice(h * 64, (h + 1) * 64)
                P.op("pe", lambda e, sl=sl, Pn=Pn, psR=psR: e.matmul(psR[0:64, sl], lhsT=T[Pn][:, sl], rhs=T["R"][:, sl],
                                                                    start=True, stop=True),
                     reads=[K_(Pn), K_("R")], writes=[pkR])
            yield
            P.op("dve", lambda e, psR=psR: e.tensor_tensor(out=T["R"][:], in0=psR[0:64, 0:256], in1=T["R"][:], op=ALU.add),
                 reads=[pkR, K_("R")], writes=[K_("R")])
            Pc, PTc = Pn, PTn
        yield
        Rn = "R"
        if self.lowp:
            Rn = "Rb"
            P.op("pool", lambda e: e.tensor_copy(out=T["Rb"][:], in_=T["R"][:]), reads=[K_("R")], writes=[K_("Rb")])
        psU, pkU = k.psum()
        for h in range(4):
            P.op("pe", lambda e, h=h: e.matmul(psU[0:64, h * 128:(h + 1) * 128], lhsT=T[Rn][:, h * 64:(h + 1) * 64],
                                               rhs=T["vb"][:, h * 128:(h + 1) * 128], start=True, stop=True),
                 reads=[K_(Rn), K_("vb")], writes=[pkU])
        P.op("act", lambda e: e.activation(out=T["U"][:], in_=psU[0:64, :], func=AF.Copy),
             reads=[pkU], writes=[K_("U")])
        psW, pkW = k.psum()
        for h in range(4):
            P.op("pe", lambda e, h=h: e.matmul(psW[:, h * 64:(h + 1) * 64], lhsT=T["kbg"][:, h * 128:(h + 1) * 128],
                                               rhs=T[Rn][:, h * 64:(h + 1) * 64], start=True, stop=True),
                 reads=[K_(Rn), K_("kbg")], writes=[pkW])
        P.op("act", lambda e: e.activation(out=T["WT"][:], in_=psW[:, 0:256], func=AF.Copy),
             reads=[pkW], writes=[K_("WT")])


def dn_consts(k, st, io, variant):
    P = k.P
    t = k.sb(st, "dconst", [128, 7, 256], F32)
    P.dma("sp", t[:], io["dconst"][variant].rearrange("a p n -> p a n"), writes=["dconst"])
    dc = {
        "U64": t[0:64, 0, 0:64], "Ys": t[0:64, 1, 0:64], "strict4": t[0:64, 2, :],
        "triu4": t[0:64, 3, :], "I4": t[0:64, 4, :], "GL": t[0:64, 5, 0:128], "tril4": t[0:64, 6, :],
    }
    return dc


def dn_gates(k, P, st_tiles, bgraw, nch, negA, dtb, c, keyp):
    braw = bgraw[:, 0:nch, 0:4]
    araw = bgraw[:, 0:nch, 4:8]
    P.op("act", lambda e: e.activation(out=braw, in_=braw, func=AF.Sigmoid), reads=[keyp], writes=[keyp])
    P.op("dve", lambda e: e.tensor_tensor(out=araw, in0=araw, in1=bc(dtb, 1, nch), op=ALU.add),
         reads=[keyp, "dtb"], writes=[keyp])
    P.op("act", lambda e: e.activation(out=araw, in_=araw, func=AF.Exp), reads=[keyp], writes=[keyp])
    P.op("act", lambda e: e.activation(out=araw, in_=araw, func=AF.Ln, bias=c["ones"][0:64, 0:1]),
         reads=[keyp, "ones"], writes=[keyp])
    P.op("dve", lambda e: e.tensor_tensor(out=araw, in0=araw, in1=bc(negA, 1, nch), op=ALU.mult),
         reads=[keyp, "negA"], writes=[keyp])


def stage_deltanet_prompt(k, c, io, sc):
    P = k.P
    DN = sc["DN_T"]
    with ExitStack() as st:
        dc = dn_consts(k, st, io, 0)
        cw = k.sb(st, "cw", [128, 12, 4], F32)
        P.dma("sp", cw[:], io["cwt"], writes=["cw"])
        negA = k.sb(st, "negA", [64, 4], F32)
        dtb = k.sb(st, "dtb", [64, 4], F32)
        P.dma("sp", negA[:], io["a_log"].partition_broadcast(64), writes=["negA"])
        P.dma("sp", dtb[:], io["dt_bias"].partition_broadcast(64), writes=["dtb"])
        P.op("act", lambda e: e.activation(out=negA[:], in_=negA[:], func=AF.Exp), reads=["negA"], writes=["negA"])
        P.op("dve", lambda e: e.tensor_scalar(out=negA[:], in0=negA[:], scalar1=-1.0, scalar2=None, op0=ALU.mult),
             reads=["negA"], writes=["negA"])
        dnw = k.sb(st, "dnw", [128, 1], F32)
        P.dma("sp", dnw[:], io["delta_norm"], writes=["dnw"])
        lnq = k.sb(st, "lnq", [128, 1], F32)
        P.op("pool", lambda e: e.memset(lnq[:], math.log(128.0 ** -0.5)), writes=["lnq"])
        raw = k.sb(st, "raw", [128, 12, 520], F32)
        xc = k.sb(st, "xc", [128, 12, 512], F32)
        zt = k.sb(st, "zt", [128, 4, 512], F32)
        sq = k.sb(st, "sq", [128, 512], F32)
        rin = k.sb(st, "rin", [128, 512], F32)
        oTg = k.sb(st, "oTg", [128, 4, 512], F32)
        obt = k.sb(st, "obt", [128, 4, 512], BF16)
        bgraw = k.sb(st, "bgraw", [64, 8, 8], F32)
        S4 = k.sb(st, "S4", [128, 512], F32)
        tmpS = k.sb(st, "tmpS", [128, 512], F32)
        vnew = k.sb(st, "vnew", [64, 512], F32)
        P.op("pool", lambda e: e.memset(S4[:], 0.0), writes=["S4"])
        P.op("pool", lambda e: e.memset(raw[:, :, 0:8], 0.0), writes=["rawpad"])
        xcb = k.sb(st, "xcb", [128, 8, 512], BF16)
        Sb = k.sb(st, "Sb", [128, 512], BF16)
        vnb = k.sb(st, "vnb", [64, 512], BF16)
        P.op("pool", lambda e: e.memset(Sb[:], 0.0), writes=["Sb"])
        ch = DnChunk(k, st, c, dc, nset=4, lowp=True)
        CBv = sc["CB"].rearrange("h e q t -> h e (q t)")
        ngroups = 9

        def do_group(gi):
            u0 = gi * 512
            n = 512 if gi < 8 else 64
            nch = n // 64
            for ph in range(12):
                if gi == 0:
                    P.dma("sp", raw[:, ph, 8:8 + n], DN[ph, :, u0:u0 + n], reads=["rawpad"], writes=[("raw", ph)])
                else:
                    P.dma("sp", raw[:, ph, 5:8 + n], DN[ph, :, u0 - 3:u0 + n], reads=["rawpad"], writes=[("raw", ph)])
            for h in range(4):
                P.dma("sp", zt[:, h, 0:n], DN[12 + h, :, u0:u0 + n], writes=[("zt", h)])
            P.dma("sp", bgraw[:, 0:nch, :], sc["BA"][u0:u0 + n, :].rearrange("(a c) k -> c a k", c=64),
                  writes=["bg"])
            dn_gates(k, P, None, bgraw, nch, negA[:], dtb[:], c, "bg")
            for ph in range(12):
                o_ = xc[:, ph, 0:n]
                P.op("dve", lambda e, ph=ph, o_=o_: e.tensor_scalar(out=o_, in0=raw[:, ph, 5:5 + n], scalar1=cw[:, ph, 0:1],
                                                                 scalar2=None, op0=ALU.mult),
                     reads=[("raw", ph), "cw"], writes=[("xc", ph)])
                for j_ in range(1, 4):
                    P.op("dve", lambda e, ph=ph, o_=o_, j_=j_: e.scalar_tensor_tensor(
                        out=o_, in0=raw[:, ph, 5 + j_:5 + j_ + n], scalar=cw[:, ph, j_:j_ + 1], in1=o_,
                        op0=ALU.mult, op1=ALU.add), reads=[("raw", ph), "cw", ("xc", ph)], writes=[("xc", ph)])
                P.op("act", lambda e, o_=o_: e.activation(out=o_, in_=o_, func=AF.Silu),
                     reads=[("xc", ph)], writes=[("xc", ph)])
            for ph in range(8):
                o_ = xc[:, ph, 0:n]
                P.op("pool", lambda e, o_=o_: e.tensor_tensor(out=sq[:, 0:n], in0=o_, in1=o_, op=ALU.mult),
                     reads=[("xc", ph)], writes=["sq"])
                ps, pk = k.psum()
                P.op("pe", lambda e, ps=ps: e.matmul(ps[:, 0:n], lhsT=c["ones"][:], rhs=sq[:, 0:n], start=True, stop=True),
                     reads=["sq", "ones"], writes=[pk])
                P.op("act", lambda e, ps=ps: e.activation(out=rin[:, 0:n], in_=ps[:, 0:n], func=AF.Ln, bias=c["eps"][:]),
                     reads=[pk, "epst"], writes=["rin"])
                if ph < 4:
                    P.op("act", lambda e: e.activation(out=rin[:, 0:n], in_=rin[:, 0:n], func=AF.Exp, scale=-0.5,
                                                       bias=lnq[:]), reads=["rin", "lnq"], writes=["rin"])
                else:
                    P.op("act", lambda e: e.activation(out=rin[:, 0:n], in_=rin[:, 0:n], func=AF.Exp, scale=-0.5),
                         reads=["rin"], writes=["rin"])
                P.op("dve", lambda e, o_=o_: e.tensor_tensor(out=o_, in0=o_, in1=rin[:, 0:n], op=ALU.mult),
                     reads=[("xc", ph), "rin"], writes=[("xc", ph)])
                P.op("pool", lambda e, o_=o_, ph=ph: e.tensor_copy(out=xcb[:, ph, 0:n], in_=o_),
                     reads=[("xc", ph)], writes=[("xcb", ph)])
            def do_chunk(ci):
                c0 = ci * 64
                qs = lambda h, c0=c0: xc[:, h, c0:c0 + 64]
                ks = lambda h, c0=c0: xc[:, 4 + h, c0:c0 + 64]
                vs = lambda h, c0=c0: xc[:, 8 + h, c0:c0 + 64]
                rk = [("xc", ph) for ph in range(12)]
                qsb = lambda h, c0=c0: xcb[:, h, c0:c0 + 64]
                ksb = lambda h, c0=c0: xcb[:, 4 + h, c0:c0 + 64]
                rkb = [("xcb", ph) for ph in range(8)]
                out = {}
                gen = ch.run_gen(out, qs, ks, vs, rk, bgraw[:, ci, 0:4], bgraw[:, ci, 4:8], "bg",
                                 qsb=qsb, ksb=ksb, rkb=rkb)
                return gen, out, c0

            def do_serial(ci, T, K_, c0):
                psWS, pkWS = k.psum()
                for h in range(4):
                    P.op("pe", lambda e, h=h, T=T: e.matmul(psWS[0:64, h * 128:(h + 1) * 128],
                                                            lhsT=T["WT"][:, h * 64:(h + 1) * 64],
                                                            rhs=Sb[:, h * 128:(h + 1) * 128], start=True, stop=True),
                         reads=[K_("WT"), "Sb"], writes=[pkWS])
                P.op("dve", lambda e, T=T: e.tensor_tensor(out=vnew[:], in0=T["U"][:], in1=psWS[0:64, :], op=ALU.subtract),
                     reads=[K_("U"), pkWS], writes=["vnew"])
                P.op("pool", lambda e: e.tensor_copy(out=vnb[:], in_=vnew[:]), reads=["vnew"], writes=["vnb"])
                psO, pkO = k.psum()
                for h in range(4):
                    P.op("pe", lambda e, h=h, T=T: e.matmul(psO[:, h * 64:(h + 1) * 64], lhsT=Sb[:, h * 128:(h + 1) * 128],
                                                            rhs=T["qd"][:, h * 64:(h + 1) * 64], start=True, stop=False),
                         reads=[K_("qd"), "Sb"], writes=[pkO])
                    P.op("pe", lambda e, h=h, T=T: e.matmul(psO[:, h * 64:(h + 1) * 64], lhsT=vnb[:, h * 128:(h + 1) * 128],
                                                            rhs=T["QK"][:, h * 64:(h + 1) * 64], start=False, stop=True),
                         reads=[K_("QK"), "vnb"], writes=[pkO])
                P.op("act", lambda e, c0=c0: e.activation(out=oTg[:, :, c0:c0 + 64],
                                                          in_=psO[:, 0:256].rearrange("p (h c) -> p h c", h=4), func=AF.Copy),
                     reads=[pkO], writes=[("oTg", ci)])
                psS, pkS = k.psum()
                for h in range(4):
                    P.op("pe", lambda e, h=h, T=T: e.matmul(psS[:, h * 128:(h + 1) * 128], lhsT=T["kd"][:, h * 128:(h + 1) * 128],
                                                            rhs=vnb[:, h * 128:(h + 1) * 128], start=True, stop=True),
                         reads=[K_("kd"), "vnb"], writes=[pkS])
                P.op("dve", lambda e, T=T: e.tensor_tensor(out=tmpS[:].rearrange("p (h e) -> p h e", h=4),
                                                          in0=S4[:].rearrange("p (h e) -> p h e", h=4),
                                                          in1=bc(T["bx"][:, 8:12], 2, 128), op=ALU.mult),
                     reads=["S4", K_("bx")], writes=["tmpS"])
                P.op("dve", lambda e: e.tensor_tensor(out=S4[:], in0=tmpS[:], in1=psS[:, :], op=ALU.add),
                     reads=["tmpS", pkS], writes=["S4"])
                P.op("pool", lambda e: e.tensor_copy(out=Sb[:], in_=S4[:]), reads=["S4"], writes=["Sb"])
            for ci0 in range(0, nch, 4):
                cis = list(range(ci0, min(ci0 + 4, nch)))
                items = [do_chunk(ci) for ci in cis]
                active = [it[0] for it in items]
                while active:
                    for g_ in list(active):
                        try:
                            next(g_)
                        except StopIteration:
                            active.remove(g_)
                for ci, (g_, o_, c0_) in zip(cis, items):
                    do_serial(ci, o_["T"], o_["K_"], c0_)
            okeys = [("oTg", ci) for ci in range(nch)]
            for h in range(4):
                P.op("pool", lambda e, h=h: e.tensor_tensor(out=sq[:, 0:n], in0=oTg[:, h, 0:n], in1=oTg[:, h, 0:n], op=ALU.mult),
                     reads=okeys, writes=["sq"])
                ps, pk = k.psum()
                P.op("pe", lambda e, ps=ps: e.matmul(ps[:, 0:n], lhsT=c["ones"][:], rhs=sq[:, 0:n], start=True, stop=True),
                     reads=["sq", "ones"], writes=[pk])
                P.op("act", lambda e, ps=ps: e.activation(out=rin[:, 0:n], in_=ps[:, 0:n], func=AF.Ln, scale=1.0 / 128,
                                                         bias=c["eps"][:]), reads=[pk, "epst"], writes=["rin"])
                P.op("act", lambda e: e.activation(out=rin[:, 0:n], in_=rin[:, 0:n], func=AF.Exp, scale=-0.5),
                     reads=["rin"], writes=["rin"])
                P.op("act", lambda e, h=h: e.activation(out=zt[:, h, 0:n], in_=zt[:, h, 0:n], func=AF.Silu),
                     reads=[("zt", h)], writes=[("zt", h)])
                P.op("dve", lambda e, h=h: e.scalar_tensor_tensor(out=rin[:, 0:n], in0=rin[:, 0:n], scalar=dnw[:, 0:1],
                                                                 in1=zt[:, h, 0:n], op0=ALU.mult, op1=ALU.mult),
                     reads=["rin", "dnw", ("zt", h)], writes=["rin"])
                P.op("dve", lambda e, h=h: e.tensor_tensor(out=obt[:, h, 0:n], in0=oTg[:, h, 0:n], in1=rin[:, 0:n], op=ALU.mult),
                     reads=okeys + ["rin"], writes=[("obt", h)])
                t0 = u0 - 48
                lo = 48 if gi == 0 else 0
                P.dma("sp", CBv[h, :, t0 + lo:t0 + n], obt[:, h, lo:n], reads=[("obt", h)], writes=[("CB", h, gi)])
            if gi == ngroups - 1:
                cvo = k.sb(st, "cvo", [3, 1536], F32)
                for g3 in range(3):
                    ps, pk = k.psum()
                    for q in range(4):
                        ph = g3 * 4 + q
                        P.op("pe", lambda e, ps=ps, q=q, ph=ph: e.transpose(out=ps[0:3, q * 128:(q + 1) * 128],
                                                                           in_=raw[:, ph, 69:72], identity=c["ident"][:]),
                             reads=[("raw", ph), "ident"], writes=[pk])
                    P.op("act", lambda e, ps=ps, g3=g3: e.activation(out=cvo[:, g3 * 512:(g3 + 1) * 512], in_=ps[0:3, :],
                                                                    func=AF.Copy), reads=[pk], writes=[("cvo", g3)])
                P.dma("sp", io["conv_p"], cvo[:], reads=[("cvo", g3) for g3 in range(3)], writes=["conv_p"])
        for gi in range(ngroups):
            do_group(gi)
        P.dma("sp", io["delta_p"].rearrange("h d e -> d h e"), S4[:].rearrange("p (h e) -> p h e", h=4),
              reads=["S4"], writes=["delta_p"])
        P.barrier()


def stage_deltanet_sample(k, c, io, sc):
    P = k.P
    DNS = sc["DNS_T"]
    with ExitStack() as st:
        dc = dn_consts(k, st, io, 1)
        seqm = dc["tril4"][:, 0:16]
        cm = k.sb(st, "cm", [128, 16, 64], F32)
        P.dma("sp", cm[:], io["cmask"], writes=["cm"])
        cw = k.sb(st, "cws", [128, 48, 4], F32)
        P.dma("sp", cw[:], io["cwt_s"], writes=["cws"])
        negA = k.sb(st, "negA", [64, 16], F32)
        dtb = k.sb(st, "dtb", [64, 16], F32)
        P.dma("sp", negA[:], io["a_log_all"].partition_broadcast(64), writes=["negA"])
        P.dma("sp", dtb[:], io["dt_bias_all"].partition_broadcast(64), writes=["dtb"])
        P.op("act", lambda e: e.activation(out=negA[:], in_=negA[:], func=AF.Exp), reads=["negA"], writes=["negA"])
        P.op("dve", lambda e: e.tensor_scalar(out=negA[:], in0=negA[:], scalar1=-1.0, scalar2=None, op0=ALU.mult),
             reads=["negA"], writes=["negA"])
        dnw = k.sb(st, "dnw", [128, 1], F32)
        P.dma("sp", dnw[:], io["delta_norm"], writes=["dnw"])
        lnq = k.sb(st, "lnq", [128, 1], F32)
        P.op("pool", lambda e: e.memset(lnq[:], math.log(128.0 ** -0.5)), writes=["lnq"])
        P.dma("sp", io["conv_s"], sc["CVS"].rearrange("(s t) n -> s t n", t=4)[:, 1:4, :], writes=["conv_s"])
        hin = k.sb(st, "hin", [48, 6144], F32)
        P.dma("sp", hin[:], io["state_conv"].rearrange("s r n -> (s r) n"), writes=["hin"])
        xx = k.sb(st, "xx", [128, 48, 16, 8], F32)
        for g6 in range(6):
            ps, pk = k.psum()
            for q in range(8):
                ph = g6 * 8 + q
                P.op("pe", lambda e, ps=ps, q=q, ph=ph: e.transpose(out=ps[:, q * 48:(q + 1) * 48],
                                                                   in_=hin[:, ph * 128:(ph + 1) * 128],
                                                                   identity=c["ident"][0:48, 0:48]),
                     reads=["hin", "ident"], writes=[pk])
            P.op("act", lambda e, ps=ps, g6=g6: e.activation(
                out=xx[:, g6 * 8:(g6 + 1) * 8, :, 0:3],
                in_=ps[:, 0:384].rearrange("p (a s r) -> p a s r", a=8, s=16), func=AF.Copy),
                reads=[pk], writes=[("xxh", g6)])
        xin = k.sb(st, "xin", [128, 48, 64], F32)
        for q in range(6):
            P.dma("sp", xin[:, q * 8:(q + 1) * 8, :], DNS[q * 8:(q + 1) * 8, :, 0:64].rearrange("a p t -> p a t"),
                  writes=[("xin", q)])
        for g6 in range(6):
            P.op("pool", lambda e, g6=g6: e.tensor_copy(
                out=xx[:, g6 * 8:(g6 + 1) * 8, :, 3:7],
                in_=xin[:, g6 * 8:(g6 + 1) * 8, 0:64].rearrange("p a (s t) -> p a s t", t=4)),
                reads=[("xin", g6)], writes=[("xxn", g6)])
        xcs = k.sb(st, "xcs", [128, 48, 64], F32)
        for ph in range(48):
            o_ = xcs[:, ph, :].rearrange("p (s t) -> p s t", t=4)
            rd = [("xxh", ph // 8), ("xxn", ph // 8), "cws"]
            P.op("dve", lambda e, ph=ph, o_=o_: e.tensor_scalar(out=o_, in0=xx[:, ph, :, 0:4], scalar1=cw[:, ph, 0:1],
                                                             scalar2=None, op0=ALU.mult), reads=rd, writes=[("xcs", ph)])
            for j_ in range(1, 4):
                P.op("dve", lambda e, ph=ph, o_=o_, j_=j_: e.scalar_tensor_tensor(
                    out=o_, in0=xx[:, ph, :, j_:j_ + 4], scalar=cw[:, ph, j_:j_ + 1], in1=o_,
                    op0=ALU.mult, op1=ALU.add), reads=rd + [("xcs", ph)], writes=[("xcs", ph)])
        P.op("act", lambda e: e.activation(out=xcs[:], in_=xcs[:], func=AF.Silu),
             reads=[("xcs", ph) for ph in range(48)], writes=["xcsA"])
        sq = k.sb(st, "sq", [128, 512], F32)
        rin = k.sb(st, "rin", [128, 512], F32)
        for g8 in range(4):
            v_ = xcs[:, g8 * 8:(g8 + 1) * 8, :].rearrange("p a t -> p (a t)")

            def l2(g8=g8, v_=v_):
                P.op("pool", lambda e: e.tensor_tensor(out=sq[:], in0=v_, in1=v_, op=ALU.mult),
                     reads=["xcsA", ("xn", g8)], writes=["sq"])
                ps, pk = k.psum()
                P.op("pe", lambda e: e.matmul(ps[:, :], lhsT=c["ones"][:], rhs=sq[:], start=True, stop=True),
                     reads=["sq", "ones"], writes=[pk])
                P.op("act", lambda e: e.activation(out=rin[:], in_=ps[:, :], func=AF.Ln, bias=c["eps"][:]),
                     reads=[pk, "epst"], writes=["rin"])
                if g8 < 2:
                    P.op("act", lambda e: e.activation(out=rin[:], in_=rin[:], func=AF.Exp, scale=-0.5, bias=lnq[:]),
                         reads=["rin", "lnq"], writes=["rin"])
                else:
                    P.op("act", lambda e: e.activation(out=rin[:], in_=rin[:], func=AF.Exp, scale=-0.5),
                         reads=["rin"], writes=["rin"])
                P.op("dve", lambda e: e.tensor_tensor(out=v_, in0=v_, in1=rin[:], op=ALU.mult),
                     reads=["xcsA", "rin"], writes=[("xn", g8)])
            l2()
        xk = ["xcsA"] + [("xn", g8) for g8 in range(4)]
        bt = k.sb(st, "bt", [64, 1, 32], F32)
        P.dma("sp", bt[:, 0, :], sc["BAS"][0:64, :], writes=["bgs"])
        braw = bt[:, :, 0:16]
        araw = bt[:, :, 16:32]
        P.op("act", lambda e: e.activation(out=braw, in_=braw, func=AF.Sigmoid), reads=["bgs"], writes=["bgs"])
        P.op("dve", lambda e: e.tensor_tensor(out=araw, in0=araw, in1=bc(dtb[:], 1, 1), op=ALU.add),
             reads=["bgs", "dtb"], writes=["bgs"])
        P.op("act", lambda e: e.activation(out=araw, in_=araw, func=AF.Exp), reads=["bgs"], writes=["bgs"])
        P.op("act", lambda e: e.activation(out=araw, in_=araw, func=AF.Ln, bias=c["ones"][0:64, 0:1]),
             reads=["bgs", "ones"], writes=["bgs"])
        P.op("dve", lambda e: e.tensor_tensor(out=araw, in0=araw, in1=bc(negA[:], 1, 1), op=ALU.mult),
             reads=["bgs", "negA"], writes=["bgs"])
        ch = DnChunk(k, st, c, dc, nset=1)
        Sall = [k.sb(st, "Sall", [128, 4, 16, 128], F32) for _ in range(1)]
        WTm = k.sb(st, "WTm", [128, 16, 64], F32)
        qdm = k.sb(st, "qdm", [128, 16, 64], F32)
        kdm = k.sb(st, "kdm", [64, 16, 128], F32)
        vnew = k.sb(st, "vnew", [64, 512], F32)
        gm = k.sb(st, "gm", [64, 4, 16], F32)
        egl = k.sb(st, "egl", [128, 64], F32)
        oTs = k.sb(st, "oTs", [128, 4, 64], F32)
        zs = k.sb(st, "zs", [128, 4, 64], F32)
        obs = k.sb(st, "obs", [128, 4, 64], BF16)

        def do_hg(hg):
            Si = 0
            S = Sall[Si]
            for h in range(4):
                P.dma("sp", S[:, h], io["state_delta"][:, 4 * hg + h].rearrange("s d e -> d s e"),
                      writes=[("S", Si, h)])
            qs = lambda h: xcs[:, 4 * hg + h, :]
            ks = lambda h: xcs[:, 16 + 4 * hg + h, :]
            vs = lambda h: xcs[:, 32 + 4 * hg + h, :]
            beta = bt[:, 0, 4 * hg:4 * hg + 4]
            g = bt[:, 0, 16 + 4 * hg:16 + 4 * hg + 4]
            T, K_ = ch.run(qs, ks, vs, xk, beta, g, "bgs", levels=1)
            P.op("pool", lambda e: e.tensor_tensor(out=gm[:], in0=bc(g, 2, 16), in1=bc(seqm, 1, 4), op=ALU.mult),
                 reads=["bgs", "dconst"], writes=["gm"])
            psE, pkE = k.psum()
            P.op("pe", lambda e: e.matmul(psE[:, 0:64], lhsT=c["ones"][0:64, :], rhs=gm[:].rearrange("p h s -> p (h s)"),
                                          start=True, stop=True), reads=["gm", "ones"], writes=[pkE])
            P.op("act", lambda e: e.activation(out=egl[:], in_=psE[:, 0:64], func=AF.Exp), reads=[pkE], writes=["egl"])
            psWS, pkWS = k.psum()
            for h in range(4):
                P.op("pool", lambda e, h=h: e.tensor_tensor(out=WTm[:], in0=bc(T["WT"][:, h * 64:(h + 1) * 64], 1, 16),
                                                            in1=cm[:], op=ALU.mult),
                     reads=[K_("WT"), "cm"], writes=["WTm"])
                for s in range(16):
                    P.op("pe", lambda e, h=h, s=s: e.matmul(psWS[0:64, h * 128:(h + 1) * 128], lhsT=WTm[:, s, :],
                                                            rhs=S[:, h, s, :], start=(s == 0), stop=(s == 15)),
                         reads=["WTm", ("S", Si, h)], writes=[pkWS])
            P.op("dve", lambda e: e.tensor_tensor(out=vnew[:], in0=T["U"][:], in1=psWS[0:64, :], op=ALU.subtract),
                 reads=[K_("U"), pkWS], writes=["vnew"])
            psO, pkO = k.psum()
            for h in range(4):
                P.op("pool", lambda e, h=h: e.tensor_tensor(out=qdm[:], in0=bc(T["qd"][:, h * 64:(h + 1) * 64], 1, 16),
                                                            in1=cm[:], op=ALU.mult),
                     reads=[K_("qd"), "cm"], writes=["qdm"])
                for s in range(16):
                    P.op("pe", lambda e, h=h, s=s: e.matmul(psO[:, h * 64:(h + 1) * 64], lhsT=S[:, h, s, :],
                                                            rhs=qdm[:, s, :], start=(s == 0), stop=False),
                         reads=["qdm", ("S", Si, h)], writes=[pkO])
                P.op("pe", lambda e, h=h: e.matmul(psO[:, h * 64:(h + 1) * 64], lhsT=vnew[:, h * 128:(h + 1) * 128],
                                                   rhs=T["QK"][:, h * 64:(h + 1) * 64], start=False, stop=True),
                     reads=[K_("QK"), "vnew"], writes=[pkO])
            P.op("act", lambda e: e.activation(out=oTs[:], in_=psO[:, 0:256].rearrange("p (h c) -> p h c", h=4),
                                               func=AF.Copy), reads=[pkO], writes=["oTs"])
            for h in range(4):
                P.op("pool", lambda e, h=h: e.tensor_tensor(out=kdm[:], in0=bc(T["kd"][:, h * 128:(h + 1) * 128], 1, 16),
                                                            in1=bc(seqm, 2, 128), op=ALU.mult),
                     reads=[K_("kd"), "dconst"], writes=["kdm"])
                for s4 in range(4):
                    psS, pkS = k.psum()
                    for q in range(4):
                        s = s4 * 4 + q
                        P.op("pe", lambda e, h=h, s=s, q=q, psS=psS: e.matmul(
                            psS[:, q * 128:(q + 1) * 128], lhsT=kdm[:, s, :], rhs=vnew[:, h * 128:(h + 1) * 128],
                            start=True, stop=True), reads=["kdm", "vnew"], writes=[pkS])
                    for q in range(4):
                        s = s4 * 4 + q
                        P.op("dve", lambda e, h=h, s=s, q=q, psS=psS: e.scalar_tensor_tensor(
                            out=S[:, h, s, :], in0=S[:, h, s, :], scalar=egl[:, h * 16 + s:h * 16 + s + 1],
                            in1=psS[:, q * 128:(q + 1) * 128], op0=ALU.mult, op1=ALU.add),
                            reads=[("S", Si, h), "egl", pkS], writes=[("S", Si, h)])
                P.dma("sp", io["delta_s"][:, 4 * hg + h].rearrange("s d e -> d s e"), S[:, h],
                      reads=[("S", Si, h)], writes=[("dso", hg, h)])
            P.dma("sp", zs[:], DNS[48 + 4 * hg:52 + 4 * hg, :, 0:64].rearrange("a p t -> p a t"), writes=["zs"])
            P.op("act", lambda e: e.activation(out=zs[:], in_=zs[:], func=AF.Silu), reads=["zs"], writes=["zs"])
            of = oTs[:].rearrange("p h c -> p (h c)")
            P.op("pool", lambda e: e.tensor_tensor(out=sq[:, 0:256], in0=of, in1=of, op=ALU.mult),
                 reads=["oTs"], writes=["sq"])
            ps, pk = k.psum()
            P.op("pe", lambda e: e.matmul(ps[:, 0:256], lhsT=c["ones"][:], rhs=sq[:, 0:256], start=True, stop=True),
                 reads=["sq", "ones"], writes=[pk])
            P.op("act", lambda e: e.activation(out=rin[:, 0:256], in_=ps[:, 0:256], func=AF.Ln, scale=1.0 / 128,
                                               bias=c["eps"][:]), reads=[pk, "epst"], writes=["rin"])
            P.op("act", lambda e: e.activation(out=rin[:, 0:256], in_=rin[:, 0:256], func=AF.Exp, scale=-0.5),
                 reads=["rin"], writes=["rin"])
            P.op("dve", lambda e: e.scalar_tensor_tensor(out=rin[:, 0:256], in0=rin[:, 0:256], scalar=dnw[:, 0:1],
                                                         in1=zs[:].rearrange("p h c -> p (h c)"), op0=ALU.mult,
                                                         op1=ALU.mult), reads=["rin", "dnw", "zs"], writes=["rin"])
            P.op("dve", lambda e: e.tensor_tensor(out=obs[:].rearrange("p h c -> p (h c)"), in0=of, in1=rin[:, 0:256],
                                                  op=ALU.mult), reads=["oTs", "rin"], writes=["obs"])
            P.dma("sp", sc["OBS_T"][4 * hg:4 * hg + 4].rearrange("a p t -> p a t"), obs[:], reads=["obs"],
                  writes=[("OBS", hg)])
        for hg in range(4):
            do_hg(hg)
        P.barrier()


def stage_exchange(k, c, io, sc):
    P = k.P
    cb2 = sc["CB"].rearrange("h e q t -> (h e q) t")
    for i in range(0 if os.environ.get("KNOCC") else 5):
        P.op("pool", lambda e, i=i: e.collective_compute(
            "AllGather", ALU.bypass, replica_groups=[[0, 1, 2, 3], [4, 5, 6, 7]],
            ins=[cb2[i * 512:(i + 1) * 512, :]], outs=[sc["GB"][i * 2048:(i + 1) * 2048, :]]), kind="cc")


def stage_branches(k, c, io, sc):
    P = k.P
    with ExitStack() as st:
        Xa = k.sb(st, "Xa", [64, 32, NQ], BF16)
        Xb = k.sb(st, "Xb", [128, 16, NQ], BF16)
        for blk in range(9):
            P.dma("sp", Xa[:, :, blk * 128:(blk + 1) * 128], sc["OA_T"][blk].rearrange("p (h t) -> p h t", h=32),
                  writes=[("Xa", blk)])
        idx = k.sb(st, "idx", [128, 32], I32)
        P.dma("sp", idx[:], io["gidx"], writes=["idx"])
        P.op("pool", lambda e: e.memset(Xb[:, :, 1088:NQ], 0.0), writes=["Xbpad"])
        P.dma("sp", Xb[:, :, 1024:1088], sc["OBS_T"].rearrange("h p t -> p h t"), writes=["Xbs"])
        gst = [k.sb(st, "gst", [128, 1024], BF16) for _ in range(2)]
        for H in range(16):
            P.op("pool", lambda e, H=H: e.indirect_dma_start(
                out=Xb[:, H, 0:1024], out_offset=None, in_=sc["GB"],
                in_offset=bass.IndirectOffsetOnAxis(ap=idx[:, H:H + 1], axis=0)),
                reads=["idx"], writes=[("Xb", H)], kind="d")
            gi = H % 2
            P.op("pool", lambda e, H=H, gi=gi: e.indirect_dma_start(
                out=gst[gi][:], out_offset=None, in_=sc["GB"],
                in_offset=bass.IndirectOffsetOnAxis(ap=idx[:, 16 + H:17 + H], axis=0)),
                reads=["idx"], writes=[("gst", gi)], kind="d")
            P.op("dve", lambda e, H=H, gi=gi: e.tensor_copy(out=Xb[:, H, 1088:1104], in_=gst[gi][:, 0:16]),
                 reads=[("gst", gi), "Xbpad"], writes=[("Xbt", H)])
        xbk = ["Xbpad", "Xbs"] + [("Xb", H) for H in range(16)] + [("Xbt", H) for H in range(16)]
        xak = [("Xa", blk) for blk in range(9)]
        Wa = [k.sb(st, "Wa", [64, 32, 256], BF16) for _ in range(2)]
        Wb = [k.sb(st, "Wb", [128, 16, 256], BF16) for _ in range(2)]
        sg = [k.sb(st, "sg", [128, 2, 512], BF16) for _ in range(2)]
        tt = [k.sb(st, "tt", [128, 512], F32) for _ in range(2)]
        uu = [k.sb(st, "uu", [128, 512], F32) for _ in range(2)]
        ms = [k.sb(st, "ms", [128, 512], BF16) for _ in range(2)]
        rot = Rot(2)
        wav = io["w_branch_a"].rearrange("(h d) n -> d h n", d=64)
        wbv = io["w_branch_b"].rearrange("(h d) n -> d h n", d=128)
        subs = [(0, 512), (512, 512), (1024, 128)]
        for nt in range(16):
            wi = nt % 2
            for q in range(4):
                P.dma("pool", Wa[wi][:, q * 8:(q + 1) * 8, :], wav[:, q * 8:(q + 1) * 8, nt * 256:(nt + 1) * 256],
                      writes=[("Wa", wi, q)])
            for q in range(2):
                P.dma("pool", Wb[wi][:, q * 8:(q + 1) * 8, :], wbv[:, q * 8:(q + 1) * 8, nt * 256:(nt + 1) * 256],
                      writes=[("Wb", wi, q)])
            for ci in range(2):
                chn = nt * 2 + ci
                for (t0, n) in subs:
                    def unit(wi=wi, ci=ci, chn=chn, t0=t0, n=n):
                        psA, pkA = k.psum()
                        psB, pkB = k.psum()
                        for h in range(32):
                            P.op("pe", lambda e, h=h: e.matmul(psA[:, 0:n], lhsT=Wa[wi][:, h, ci * 128:(ci + 1) * 128],
                                                               rhs=Xa[:, h, t0:t0 + n], start=(h == 0), stop=(h == 31)),
                                 reads=xak + [("Wa", wi, h // 8)], writes=[pkA])
                        for h in range(16):
                            P.op("pe", lambda e, h=h: e.matmul(psB[:, 0:n], lhsT=Wb[wi][:, h, ci * 128:(ci + 1) * 128],
                                                               rhs=Xb[:, h, t0:t0 + n], start=(h == 0), stop=(h == 15)),
                                 reads=xbk + [("Wb", wi, h // 8)], writes=[pkB])
                        ri = rot.nxt()
                        P.dma("sp", sg[ri][:, 0, 0:n], sc["SG_T"][chn, :, t0:t0 + n], writes=[("sg", ri, 0)])
                        P.dma("sp", sg[ri][:, 1, 0:n], sc["SG_T"][32 + chn, :, t0:t0 + n], writes=[("sg", ri, 1)])
                        P.op("dve", lambda e: e.tensor_tensor(out=tt[ri][:, 0:n], in0=psA[:, 0:n], in1=sg[ri][:, 0, 0:n],
                                                              op=ALU.mult), reads=[pkA, ("sg", ri, 0)], writes=[("tt", ri)])
                        P.op("dve", lambda e: e.tensor_tensor(out=uu[ri][:, 0:n], in0=psB[:, 0:n], in1=sg[ri][:, 1, 0:n],
                                                              op=ALU.mult), reads=[pkB, ("sg", ri, 1)], writes=[("uu", ri)])
                        P.op("pool", lambda e: e.tensor_tensor(out=ms[ri][:, 0:n], in0=tt[ri][:, 0:n], in1=uu[ri][:, 0:n],
                                                               op=ALU.add), reads=[("tt", ri), ("uu", ri)], writes=[("ms", ri)])
                        P.dma("sp", sc["MG_T"][chn, :, t0:t0 + n], ms[ri][:, 0:n], reads=[("ms", ri)],
                              writes=[("MG", chn, t0)])
                    unit()
        P.barrier()


def stage_gemm_tm(k, c, xsrc, KC, wsrc, dst, tag):
    P = k.P
    ng = KC // 16
    with ExitStack() as st:
        X = [k.sb(st, "X", [128, 16, NQ], BF16) for _ in range(2)]
        W = [k.sb(st, "W", [128, 16, 512], BF16) for _ in range(2)]
        acc = k.sb(st, "acc", [128, 9, 512], F32)
        xrot, wrot = Rot(2), Rot(2)
        for nt in range(8):
            for g in range(ng):
                def unit(nt=nt, g=g):
                    xi, wi = xrot.nxt(), wrot.nxt()
                    if ng > 1 or nt == 0:
                        for q in range(2):
                            P.dma("sp", X[xi][:, q * 8:(q + 1) * 8, :],
                                  xsrc[g * 16 + q * 8:g * 16 + (q + 1) * 8].rearrange("a p t -> p a t"),
                                  writes=[(tag, "X", xi, q)])
                    wv = wsrc[g * 2048:(g + 1) * 2048, nt * 512:(nt + 1) * 512].rearrange("(a p) n -> p a n", p=128)
                    for q in range(2):
                        P.dma("pool", W[wi][:, q * 8:(q + 1) * 8, :], wv[:, q * 8:(q + 1) * 8, :],
                              writes=[(tag, "W", wi, q)])
                    for tb in range(9):
                        ps, pk = k.psum()
                        for kc in range(16):
                            P.op("pe", lambda e, ps=ps, kc=kc, tb=tb: e.matmul(
                                ps[:, :], lhsT=X[xi][:, kc, tb * 128:(tb + 1) * 128], rhs=W[wi][:, kc, :],
                                start=(kc == 0), stop=(kc == 15)),
                                reads=[(tag, "X", xi, kc // 8), (tag, "W", wi, kc // 8)], writes=[pk])
                        if g == 0:
                            P.op("act", lambda e, ps=ps, tb=tb: e.activation(out=acc[:, tb, :], in_=ps[:, :], func=AF.Copy),
                                 reads=[pk], writes=[(tag, "acc", tb)])
                        else:
                            P.op("dve", lambda e, ps=ps, tb=tb: e.tensor_tensor(out=acc[:, tb, :], in0=ps[:, :],
                                                                             in1=acc[:, tb, :], op=ALU.add),
                                 reads=[pk, (tag, "acc", tb)], writes=[(tag, "acc", tb)])
                        if g == ng - 1:
                            P.dma("sp", dst[tb * 128:(tb + 1) * 128, nt * 512:(nt + 1) * 512], acc[:, tb, :],
                                  reads=[(tag, "acc", tb)], writes=[(tag, "dst", tb, nt)])
                unit()
        P.barrier()


def stage_gemm_tm_k32(k, c, xsrc, wsrc, dst, tag):
    P = k.P
    with ExitStack() as st:
        X = k.sb(st, "X", [128, 32, NQ], BF16)
        W = [k.sb(st, "W", [128, 32, 512], BF16) for _ in range(2)]
        stg = [k.sb(st, "stg", [128, 512], F32) for _ in range(3)]
        srot = Rot(3)
        for q in range(4):
            P.dma("sp", X[:, q * 8:(q + 1) * 8, :], xsrc[q * 8:(q + 1) * 8].rearrange("a p t -> p a t"),
                  writes=[(tag, "X", q)])
        for nt in range(8):
            wi = nt % 2
            wv = wsrc[:, nt * 512:(nt + 1) * 512].rearrange("(a p) n -> p a n", p=128)
            for q in range(4):
                P.dma("pool", W[wi][:, q * 8:(q + 1) * 8, :], wv[:, q * 8:(q + 1) * 8, :], writes=[(tag, "W", wi, q)])
            for tb in range(9):
                ps, pk = k.psum()
                for kc in range(32):
                    P.op("pe", lambda e, ps=ps, kc=kc, tb=tb, wi=wi: e.matmul(
                        ps[:, :], lhsT=X[:, kc, tb * 128:(tb + 1) * 128], rhs=W[wi][:, kc, :],
                        start=(kc == 0), stop=(kc == 31)),
                        reads=[(tag, "X", kc // 8), (tag, "W", wi, kc // 8)], writes=[pk])
                si = srot.nxt()
                P.op("act", lambda e, ps=ps, si=si: e.activation(out=stg[si][:], in_=ps[:, :], func=AF.Copy),
                     reads=[pk], writes=[(tag, "stg", si)])
                P.dma("sp", dst[tb * 128:(tb + 1) * 128, nt * 512:(nt + 1) * 512], stg[si][:],
                      reads=[(tag, "stg", si)], writes=[(tag, "dst", tb, nt)])
        P.barrier()


def stage_postnorm(k, c, ypre, wpost, resid, hout, wnext, xt_next, tag):
    P = k.P
    with ExitStack() as st:
        wp = k.sb(st, "wp", [128, D], F32)
        P.dma("sp", wp[:], wpost.partition_broadcast(128), writes=[tag + "wp"])
        if wnext is not None:
            wn = k.sb(st, "wn", [128, D], F32)
            P.dma("sp", wn[:], wnext.partition_broadcast(128), writes=[tag + "wn"])
            hTs = [k.sb(st, "hT", [128, 32, 128], BF16) for _ in range(2)]
            hn = k.sb(st, "hn", [128, D], F32)
        yts = [k.sb(st, "yt", [128, D], F32) for _ in range(2)]
        rts = [k.sb(st, "rt", [128, D], F32) for _ in range(2)]
        junk = k.sb(st, "junk", [128, D], BF16)
        ss = [k.sb(st, "ss", [128, 2], F32) for _ in range(2)]
        for blk in range(9):
            def unit(blk=blk):
                i = blk % 2
                yt, rt = yts[i], rts[i]
                P.dma("sp", yt[:], ypre[blk * 128:(blk + 1) * 128, :], writes=[(tag, "yt", i)])
                P.dma("sp", rt[:], resid[blk * 128:(blk + 1) * 128, :], writes=[(tag, "rt", i)])
                P.op("act", lambda e: e.activation(out=junk[:], in_=yt[:], func=AF.Square, accum_out=ss[i][:, 0:1]),
                     reads=[(tag, "yt", i)], writes=[(tag, "junk"), (tag, "ss", i)])
                P.op("act", lambda e: e.activation(out=ss[i][:, 0:1], in_=ss[i][:, 0:1], func=AF.Sqrt, scale=1.0 / D,
                                                   bias=c["eps"][:]), reads=[(tag, "ss", i), "epst"], writes=[(tag, "ss", i)])
                P.op("dve", lambda e: e.reciprocal(out=ss[i][:, 0:1], in_=ss[i][:, 0:1]),
                     reads=[(tag, "ss", i)], writes=[(tag, "ss", i)])
                P.op("dve", lambda e: e.scalar_tensor_tensor(out=yt[:], in0=yt[:], scalar=ss[i][:, 0:1], in1=wp[:],
                                                             op0=ALU.mult, op1=ALU.mult),
                     reads=[(tag, "yt", i), (tag, "ss", i), tag + "wp"], writes=[(tag, "yt", i)])
                P.op("pool", lambda e: e.tensor_tensor(out=rt[:], in0=rt[:], in1=yt[:], op=ALU.add),
                     reads=[(tag, "yt", i), (tag, "rt", i)], writes=[(tag, "rt", i)])
                P.dma("sp", hout[blk * 128:(blk + 1) * 128, :], rt[:], reads=[(tag, "rt", i)], writes=[(tag, "ho", blk)])
                if wnext is None:
                    return
                hT = hTs[i]
                P.op("act", lambda e: e.activation(out=junk[:], in_=rt[:], func=AF.Square, accum_out=ss[i][:, 1:2]),
                     reads=[(tag, "rt", i)], writes=[(tag, "junk"), (tag, "ss2", i)])
                P.op("act", lambda e: e.activation(out=ss[i][:, 1:2], in_=ss[i][:, 1:2], func=AF.Sqrt, scale=1.0 / D,
                                                   bias=c["eps"][:]), reads=[(tag, "ss2", i), "epst"], writes=[(tag, "ss2", i)])
                P.op("dve", lambda e: e.reciprocal(out=ss[i][:, 1:2], in_=ss[i][:, 1:2]),
                     reads=[(tag, "ss2", i)], writes=[(tag, "ss2", i)])
                P.op("dve", lambda e: e.scalar_tensor_tensor(out=hn[:], in0=rt[:], scalar=ss[i][:, 1:2], in1=wn[:],
                                                             op0=ALU.mult, op1=ALU.mult),
                     reads=[(tag, "rt", i), (tag, "ss2", i), tag + "wn"], writes=[(tag, "hn")])
                for g in range(8):
                    ps, pk = k.psum()
                    for q in range(4):
                        kc = g * 4 + q
                        P.op("pe", lambda e, ps=ps, q=q, kc=kc: e.transpose(
                            out=ps[:, q * 128:(q + 1) * 128], in_=hn[:, kc * 128:(kc + 1) * 128],
                            identity=c["ident"][:]), reads=[(tag, "hn"), "ident"], writes=[pk])
                    if g % 2 == 0:
                        P.op("act", lambda e, ps=ps, g=g: e.activation(
                            out=hT[:, g * 4:(g + 1) * 4, :], in_=ps[:].rearrange("p (a t) -> p a t", a=4), func=AF.Copy),
                            reads=[pk], writes=[(tag, "hT", i, g)])
                    else:
                        P.op("dve", lambda e, ps=ps, g=g: e.tensor_copy(
                            out=hT[:, g * 4:(g + 1) * 4, :], in_=ps[:].rearrange("p (a t) -> p a t", a=4)),
                            reads=[pk], writes=[(tag, "hT", i, g)])
                P.dma("sp", xt_next[blk], hT[:].rearrange("p a t -> p (a t)"),
                      reads=[(tag, "hT", i, g) for g in range(8)], writes=[(tag, "xt", blk)])
            unit()
        P.barrier()


def stage_mlp_up(k, c, io, sc):
    P = k.P
    with ExitStack() as st:
        g = Gemm(k, st, NQ)
        g.tag = "up"
        g.load_x(sc["XT2"], list(range(9)))
        stg = [k.sb(st, "stg", [128, 512], F32) for _ in range(3)]
        stb = [k.sb(st, "stb", [128, 512], BF16) for _ in range(3)]
        srot = Rot(3)
        subs = [(0, 512), (512, 512), (1024, 128)]
        for nt in range(32):
            def epi(ci, si_, t0, n, ps, pk, nt=nt):
                si = srot.nxt()
                P.op("act", lambda e: e.activation(out=stg[si][:, 0:n], in_=ps[:, 0:n], func=AF.Relu),
                     reads=[pk], writes=[("ustg", si)])
                P.op("pool", lambda e: e.tensor_tensor(out=stb[si][:, 0:n], in0=stg[si][:, 0:n], in1=stg[si][:, 0:n],
                                                       op=ALU.mult), reads=[("ustg", si)], writes=[("ustb", si)])
                ch = nt * 4 + ci
                P.dma("sp", sc["U_T"][ch, :, t0:t0 + n], stb[si][:, 0:n], reads=[("ustb", si)], writes=[("UT", ch, t0)])
            g.fm(io["w_up"][:, nt * 512:(nt + 1) * 512], 512, subs, epi)
        P.barrier()

def build(stop_after=None):
    k = K()
    io = {}
    io["x_tok"] = k.inp("x_tok", [NTOK, D])
    io["x_seq"] = k.inp("x_seq", [NSEQ, D])
    io["w_in"] = k.inp("w_in", [D, IN_DIM])
    io["w_dn"] = k.inp("w_dn", [D, 2048])
    io["w_ba"] = k.inp("w_ba", [D, 8])
    io["norm_mix_pre"] = k.inp("norm_mix_pre", [1, D])
    io["cs_tab"] = k.inp("cs_tab", [NTOK, 256])
    io["masks"] = k.inp("masks", [6, 128, 512])
    io["sinks"] = k.inp("sinks", [1, 32])
    io["cache_k"] = k.inp("cache_k", [16, 128, 512])
    io["cache_v"] = k.inp("cache_v", [16, 128, 512])
    io["dconst"] = k.inp("dconst", [2, 7, 128, 256])
    io["cwt"] = k.inp("cwt", [128, 12, 4])
    io["a_log"] = k.inp("a_log", [1, 4])
    io["dt_bias"] = k.inp("dt_bias", [1, 4])
    io["delta_norm"] = k.inp("delta_norm", [128, 1])
    io["gidx"] = k.inp("gidx", [128, 32], I32)
    io["w_branch_a"] = k.inp("w_branch_a", [2048, D])
    io["w_branch_b"] = k.inp("w_branch_b", [2048, D])
    io["w_out"] = k.inp("w_out", [D, D])
    io["w_up"] = k.inp("w_up", [D, 4 * D])
    io["w_down"] = k.inp("w_down", [4 * D, D])
    io["norm_mix_post"] = k.inp("norm_mix_post", [1, D])
    io["norm_mlp_pre"] = k.inp("norm_mlp_pre", [1, D])
    io["norm_mlp_post"] = k.inp("norm_mlp_post", [1, D])
    io["y_tok"] = k.out("y_tok", [NQ, D])
    io["cmask"] = k.inp("cmask", [128, 16, 64])
    io["cwt_s"] = k.inp("cwt_s", [128, 48, 4])
    io["a_log_all"] = k.inp("a_log_all", [1, 16])
    io["dt_bias_all"] = k.inp("dt_bias_all", [1, 16])
    io["state_conv"] = k.inp("state_conv", [16, 3, 6144])
    io["state_delta"] = k.inp("state_delta", [16, 16, 128, 128])
    io["conv_s"] = k.out("conv_s", [16, 3, 6144])
    io["delta_s"] = k.out("delta_s", [16, 16, 128, 128])
    io["conv_p"] = k.out("conv_p", [3, 1536])
    io["delta_p"] = k.out("delta_p", [4, 128, 128])
    io["wink_s"] = k.out("wink_s", [16, 128, 512])
    io["winv_s"] = k.out("winv_s", [16, 128, 512])
    io["wink_p"] = k.out("wink_p", [128, 512])
    io["winv_p"] = k.out("winv_p", [128, 512])
    sc = {}
    sc["XT_tok"] = k.scratch("XT_tok", [10, 128, D], BF16)
    sc["XT_seq"] = k.scratch("XT_seq", [33, 128, D], BF16)
    sc["QKV"] = k.scratch("QKV", [NTOK, 3072], F32)
    sc["SG_T"] = k.scratch("SG_T", [64, 128, NQ], BF16)
    sc["DNS_T"] = k.scratch("DNS_T", [64, 128, 128], F32)
    sc["CVS"] = k.scratch("CVS", [64, 6144], F32)
    sc["BAS"] = k.scratch("BAS", [128, 32], F32)
    sc["DN_T"] = k.scratch("DN_T", [16, 128, NSEQ], F32)
    sc["BA"] = k.scratch("BA", [NSEQ, 8], F32)
    sc["OA_T"] = k.scratch("OA_T", [9, 64, 4096], BF16)
    sc["CB"] = k.scratch("CB", [4, 128, 5, 1024], BF16)
    sc["OBS_T"] = k.scratch("OBS_T", [16, 128, 64], BF16)
    sc["GB"] = k.scratch("GB", [4 * 2560, 1024], BF16)
    sc["MG_T"] = k.scratch("MG_T", [32, 128, NQ], BF16)
    sc["YP1"] = k.scratch("YP1", [NQ, D], F32)
    sc["H1"] = k.scratch("H1", [NQ, D], F32)
    sc["XT2"] = k.scratch("XT2", [9, 128, D], BF16)
    sc["U_T"] = k.scratch("U_T", [128, 128, NQ], BF16)
    sc["YP2"] = k.scratch("YP2", [NQ, D], F32)
    with ExitStack() as cst:
        c = build_consts(k, cst)
        stages = [
            lambda: stage_norm_T(k, c, io["x_tok"], 10, io["norm_mix_pre"], sc["XT_tok"], "n1"),
            lambda: stage_norm_T(k, c, io["x_seq"], 33, io["norm_mix_pre"], sc["XT_seq"], "n2"),
            lambda: stage_inproj_tok(k, c, io, sc),
            lambda: stage_inproj_seq(k, c, io, sc),
            lambda: stage_deltanet_prompt(k, c, io, sc),
            lambda: stage_deltanet_sample(k, c, io, sc),
            lambda: stage_exchange(k, c, io, sc),
            lambda: stage_attention(k, c, io, sc),
            lambda: stage_branches(k, c, io, sc),
            lambda: stage_gemm_tm_k32(k, c, sc["MG_T"], io["w_out"], sc["YP1"], "wo"),
            lambda: stage_postnorm(k, c, sc["YP1"], io["norm_mix_post"], io["x_tok"][128:NTOK, :], sc["H1"],
                                   io["norm_mlp_pre"], sc["XT2"], "pn1"),
            lambda: stage_mlp_up(k, c, io, sc),
            lambda: stage_gemm_tm(k, c, sc["U_T"], 128, io["w_down"], sc["YP2"], "wd"),
            lambda: stage_postnorm(k, c, sc["YP2"], io["norm_mlp_post"], sc["H1"], io["y_tok"], None, None, "pn2"),
        ]
        sel = os.environ.get("KSTAGES")
        sel = [int(x) for x in sel.split(",")] if sel else None
        for si, s in enumerate(stages):
            if stop_after is not None and si >= stop_after:
                break
            if sel is not None and si not in sel:
                continue
            s()
        k.P.emit()
    return k


def rope_tab(pos):
    half = 8
    inv_freq = (np.float32(500000.0) ** (-np.arange(half, dtype=np.float32) * np.float32(2.0 / 16))).astype(np.float32)
    ang = pos.astype(np.float32)[:, None] * inv_freq[None, :]
    cos = np.cos(ang).astype(np.float32)
    sin = np.sin(ang).astype(np.float32)
    n = pos.shape[0]
    tab = np.zeros((n, 256), np.float32)
    c16 = np.concatenate([cos, cos], axis=1)
    tab[:, 0:128] = np.tile(c16, (1, 8))
    tab[:, 128:192] = np.tile(-sin, (1, 8))
    tab[:, 192:256] = np.tile(sin, (1, 8))
    return tab


def make_dconst():
    out = np.zeros((2, 7, 128, 256), np.float32)
    a = np.arange(64)
    for v in range(2):
        same = np.ones((64, 64), bool) if v == 0 else (a[:, None] // 4 == a[None, :] // 4)
        U = ((a[:, None] <= a[None, :]) & same).astype(np.float32)
        Ys = ((a[:, None] > a[None, :]) & same).astype(np.float32)
        out[v, 0, :64, :64] = U
        out[v, 1, :64, :64] = Ys
        out[v, 2, :64, :] = np.tile(Ys, (1, 4))
        out[v, 3, :64, :] = np.tile(U, (1, 4))
        out[v, 4, :64, :] = np.tile(np.eye(64, dtype=np.float32), (1, 4))
        out[v, 5, :64, :128] = 1.0
        out[v, 6, :64, :] = np.tile(((a[:, None] >= a[None, :]) & same).astype(np.float32), (1, 4))
    out[1, 6] = 0.0
    out[1, 6, :64, :16] = (a[:, None] // 4 == np.arange(16)[None, :]).astype(np.float32)
    return out


def make_cwt(conv_w, j):
    t = np.zeros((128, 12, 4), np.float32)
    for part in range(3):
        for hl in range(4):
            c0 = part * 2048 + (4 * j + hl) * 128
            t[:, part * 4 + hl, :] = conv_w[:, c0:c0 + 128].T
    return t


def make_gidx(j):
    e = np.arange(128)[:, None]
    H = np.arange(16)[None, :]
    rank, hl = H // 4, H % 4

    def row(q):
        r = (hl * 128 + e) * 5 + q
        return (r // 512) * 2048 + rank * 512 + (r % 512)
    return np.concatenate([row(j), row(4)], axis=1).astype(np.int32)


def make_masks(j):
    kk = np.arange(128)[:, None]
    qq = np.arange(128)[None, :]
    m = np.zeros((6, 128, 512), np.float32)
    mc = (kk <= qq).astype(np.float32)
    mp = (kk >= qq).astype(np.float32)
    mcm = (((kk < 64) & (qq < 64) & (kk // 4 == qq // 4) & (kk <= qq)) |
           ((kk >= 64) & (kk <= qq) & (qq < 80))).astype(np.float32)
    mpt = ((qq >= 64) & (qq < 80) & (kk >= qq - 64)).astype(np.float32)
    m[0] = np.tile(mc, (1, 4))
    m[1] = np.tile(mp, (1, 4))
    m[2] = np.tile(mp, (1, 4)) if j > 0 else 0.0
    m[3] = np.tile(mcm, (1, 4))
    m[4] = np.tile(mpt, (1, 4))
    t = np.arange(128)[None, :] % 4
    m[5, :, 0:128] = (kk >= t).astype(np.float32)
    return m


def prep_core(inp, c):
    b, j = c // 4, c % 4
    f32 = np.float32
    hp = np.concatenate([inp["meta_tokens"].astype(f32), inp["x_prompt"][b].astype(f32)], axis=0)
    x_tok = np.zeros((NTOK, D), f32)
    pos = np.zeros((NTOK,), np.int64)
    if j > 0:
        x_tok[0:128] = hp[1024 * j - 128:1024 * j]
        pos[0:128] = np.arange(1024 * j - 128, 1024 * j)
    x_tok[128:1152] = hp[1024 * j:1024 * j + 1024]
    pos[128:1152] = np.arange(1024 * j, 1024 * j + 1024)
    x_tok[1152:1216] = inp["x_sample"][16 * c:16 * c + 16].reshape(64, D)
    pos[1152:1216] = 8192 + (np.arange(64) % 4)
    x_tok[1216:1232] = hp[4096:4112]
    pos[1216:1232] = np.arange(4096, 4112)
    x_seq = np.zeros((NSEQ, D), f32)
    x_seq[48:48 + 4112] = hp
    w_in = inp["w_in"][0]
    sl = lambda base: w_in[:, base + 512 * j: base + 512 * j + 512]
    w_dn = np.ascontiguousarray(np.concatenate([sl(3072), sl(5120), sl(7168), sl(9216)], axis=1))
    w_ba = np.ascontiguousarray(np.concatenate([w_in[:, 11264 + 4 * j:11268 + 4 * j],
                                                w_in[:, 11280 + 4 * j:11284 + 4 * j]], axis=1))
    m = {
        "x_tok": x_tok, "x_seq": x_seq, "w_in": np.ascontiguousarray(w_in), "w_dn": w_dn, "w_ba": w_ba,
        "norm_mix_pre": np.ascontiguousarray(inp["norm_mix_pre"][0:1]),
        "cs_tab": rope_tab(pos),
        "masks": make_masks(j),
        "gidx": make_gidx(j),
        "w_branch_a": np.ascontiguousarray(inp["w_branch_a"][0]), "w_branch_b": np.ascontiguousarray(inp["w_branch_b"][0]),
        "w_out": np.ascontiguousarray(inp["w_out"][0]), "w_up": np.ascontiguousarray(inp["w_up"][0]),
        "w_down": np.ascontiguousarray(inp["w_down"][0]),
        "norm_mix_post": np.ascontiguousarray(inp["norm_mix_post"][0:1]),
        "norm_mlp_pre": np.ascontiguousarray(inp["norm_mlp_pre"][0:1]),
        "norm_mlp_post": np.ascontiguousarray(inp["norm_mlp_post"][0:1]),
        "dconst": make_dconst(),
        "cmask": np.ascontiguousarray(np.broadcast_to(
            (np.arange(64)[None, :] // 4 == np.arange(16)[:, None]).astype(f32)[None], (128, 16, 64))),
        "cwt_s": np.ascontiguousarray(inp["conv_w"][0].reshape(4, 48, 128).transpose(2, 1, 0)).astype(f32),
        "a_log_all": np.ascontiguousarray(inp["a_log"][0:1]).astype(f32),
        "dt_bias_all": np.ascontiguousarray(inp["dt_bias"][0:1]).astype(f32),
        "state_conv": np.ascontiguousarray(inp["state_conv"][0, 16 * c:16 * c + 16]),
        "state_delta": np.ascontiguousarray(inp["state_delta"][0, 16 * c:16 * c + 16]),
        "cwt": make_cwt(inp["conv_w"][0], j),
        "a_log": np.ascontiguousarray(inp["a_log"][0:1, 4 * j:4 * j + 4]).astype(f32),
        "dt_bias": np.ascontiguousarray(inp["dt_bias"][0:1, 4 * j:4 * j + 4]).astype(f32),
        "delta_norm": np.ascontiguousarray(inp["delta_norm"][0].reshape(128, 1)).astype(f32),
        "sinks": np.ascontiguousarray(inp["sinks"][0:1]).astype(f32),
        "cache_k": np.ascontiguousarray(inp["cache_win_k"][0, 16 * c:16 * c + 16]).reshape(16, 128, 512),
        "cache_v": np.ascontiguousarray(inp["cache_win_v"][0, 16 * c:16 * c + 16]).reshape(16, 128, 512),
    }
    return m

_CACHE = {}


def kernel(**inputs):
    inp = {kk: np.asarray(v) for kk, v in inputs.items()}
    if "k" not in _CACHE:
        _CACHE["k"] = build()
    k = _CACHE["k"]
    in_maps = [prep_core(inp, c) for c in range(8)]
    res = run_bass_kernel_spmd(k.nc, in_maps, core_ids=list(range(8)))
    R = res.results
    f32 = np.float32
    y_prompt = np.zeros((2, 4096, D), f32)
    y_sample = np.zeros((128, 4, D), f32)
    wk_p = np.zeros((1, 2, 128, 8, 64), f32)
    wv_p = np.zeros((1, 2, 128, 8, 64), f32)
    cv_p = np.zeros((1, 2, 3, 6144), f32)
    dl_p = np.zeros((1, 2, 16, 128, 128), f32)
    wk_s = np.zeros((1, 128, 128, 8, 64), f32)
    wv_s = np.zeros((1, 128, 128, 8, 64), f32)
    cv_s = np.zeros((1, 128, 3, 6144), f32)
    dl_s = np.zeros((1, 128, 16, 128, 128), f32)
    for c in range(8):
        b, j = c // 4, c % 4
        r = R[c]
        y = np.asarray(r["y_tok"])
        lo = 1024 * j
        if j == 0:
            y_prompt[b, 0:1008] = y[16:1024]
        else:
            y_prompt[b, lo - 16:lo + 1008] = y[0:1024]
        if j == 3:
            y_prompt[b, 4080:4096] = y[1088:1104]
            wk_p[0, b] = np.asarray(r["wink_p"]).reshape(128, 8, 64)
            wv_p[0, b] = np.asarray(r["winv_p"]).reshape(128, 8, 64)
        y_sample[16 * c:16 * c + 16] = y[1024:1088].reshape(16, 4, D)
        cp = np.asarray(r["conv_p"]).reshape(3, 3, 4, 128)
        cv_p[0, b].reshape(3, 3, 16, 128)[:, :, 4 * j:4 * j + 4, :] = cp
        dl_p[0, b, 4 * j:4 * j + 4] = np.asarray(r["delta_p"])
        wk_s[0, 16 * c:16 * c + 16] = np.asarray(r["wink_s"]).reshape(16, 128, 8, 64)
        wv_s[0, 16 * c:16 * c + 16] = np.asarray(r["winv_s"]).reshape(16, 128, 8, 64)
        cv_s[0, 16 * c:16 * c + 16] = np.asarray(r["conv_s"])
        dl_s[0, 16 * c:16 * c + 16] = np.asarray(r["delta_s"])
    return (y_prompt, y_sample, wk_p, wv_p, cv_p, dl_p, wk_s, wv_s, cv_s, dl_s)
```
